# Optimizing a Trainium2 kernel written in Bass

```python
import functools
import jax, jax.numpy as jnp
from jax import lax
import numpy as np

D_MODEL = 1024
BATCH = 2
SEQ = 8192
DEPTH = 1
DEC_BATCH = 32
DEC_SEQ = 8
PAST_LEN = 16384
PAGE_SIZE = 128

RET_HEADS = 4
RET_WIDTH = D_MODEL // 2
RET_DK = RET_WIDTH // RET_HEADS
RET_DV = RET_DK
RET_THETA = 10000.0
RET_CHUNK = 128
ATT_HD = 64
ATT_WIDTH = D_MODEL - RET_WIDTH
ATT_HEADS = ATT_WIDTH // ATT_HD
DIL_PATTERNS = ((128, 1), (512, 4), (2048, 16))
MAX_WINDOW = 2048
DIL_BLOCK = 128
ROPE_THETA = 10000.0
MIX_WIDTH = RET_WIDTH + ATT_WIDTH
D_FF = 4 * D_MODEL
NORM_EPS = 1e-6
NEG_INF = -1e30
IN_SPLITS = (RET_WIDTH, RET_WIDTH, RET_HEADS * RET_DV, RET_HEADS * RET_DV, ATT_WIDTH, ATT_WIDTH, ATT_WIDTH)
IN_WIDTH = sum(IN_SPLITS)
F32 = jnp.float32

kernel_name = 'hymba_retention_dilated_swa_step'


def rmsnorm(x, gain):
    xf = x.astype(F32)
    y = xf * lax.rsqrt(jnp.mean(xf * xf, axis=-1, keepdims=True) + NORM_EPS)
    return (y * gain.astype(F32)).astype(x.dtype)


def rope_half(x, pos):
    d = x.shape[-1]
    inv = 1.0 / (ROPE_THETA ** (jnp.arange(0, d, 2, dtype=F32) / d))
    ang = pos.astype(F32)[:, None] * inv[None, :]
    cos = jnp.cos(ang)[None, :, None, :]
    sin = jnp.sin(ang)[None, :, None, :]
    xf = x.astype(F32)
    x1, x2 = xf[..., : d // 2], xf[..., d // 2:]
    return jnp.concatenate([x1 * cos - x2 * sin, x1 * sin + x2 * cos], axis=-1).astype(x.dtype)


def retention_rotate(x, pos):
    d = x.shape[-1]
    inv = 1.0 / (RET_THETA ** jnp.linspace(0.0, 1.0, d // 2, dtype=F32))
    ang = pos.astype(F32)[:, None] * inv[None, :]
    cos = jnp.cos(ang)[None, :, None, :]
    sin = jnp.sin(ang)[None, :, None, :]
    xf = x.astype(F32).reshape(x.shape[:-1] + (d // 2, 2))
    xe, xo = xf[..., 0], xf[..., 1]
    out = jnp.stack([xe * cos - xo * sin, xo * cos + xe * sin], axis=-1)
    return out.reshape(x.shape).astype(x.dtype)


def retention_chunkwise(q, k, v, s0, chunk):
    B, L, H, dk = q.shape
    dv = v.shape[-1]
    n = L // chunk
    log_g = jnp.log1p(-jnp.exp2(-5.0 - jnp.arange(H, dtype=F32)))
    idx = jnp.arange(chunk, dtype=F32)
    diff = idx[:, None] - idx[None, :]
    intra = jnp.where(diff >= 0, jnp.exp(log_g[:, None, None] * jnp.maximum(diff, 0.0)), 0.0)
    q_dec = jnp.exp(log_g[:, None] * (idx + 1.0)).T
    k_dec = jnp.exp(log_g[:, None] * (chunk - 1.0 - idx)).T
    c_dec = jnp.exp(log_g * chunk)
    qc = q.astype(F32).reshape(B, n, chunk, H, dk)
    kc = k.astype(F32).reshape(B, n, chunk, H, dk)
    vc = v.astype(F32).reshape(B, n, chunk, H, dv)
    a = jnp.einsum('bnihd,bnjhd->bnhij', qc, kc) * intra
    o = jnp.einsum('bnhij,bnjhe->bnihe', a, vc)
    kv = jnp.einsum('bnjhd,bnjhe->nbhde', kc * k_dec[:, :, None], vc)

    def step(s, kv_n):
        return s * c_dec[None, :, None, None] + kv_n, s

    s_fin, s_prev = lax.scan(step, s0, kv)
    o = o + jnp.einsum('bnihd,nbhde->bnihe', qc * q_dec[:, :, None], s_prev)
    return o.reshape(B, L, H, dv), s_fin


def retention_readout(o, g, ret_gain):
    B, L = o.shape[:2]
    on = o * lax.rsqrt(jnp.mean(o * o, axis=-1, keepdims=True) + NORM_EPS)
    on = on.reshape(B, L, -1) * ret_gain.astype(F32)
    return (on * jax.nn.silu(g.astype(F32))).astype(g.dtype)


def dilated_prompt(q, k, v, dil, span):
    B, S, H, d = q.shape
    unit = dil * DIL_BLOCK
    s_pad = -(-S // unit) * unit
    m_len = s_pad // dil
    nb = m_len // DIL_BLOCK
    padw = ((0, 0), (0, s_pad - S), (0, 0), (0, 0))

    def regroup(t):
        t = jnp.pad(t, padw).reshape(B, m_len, dil, H, d).transpose(0, 2, 1, 3, 4)
        return t.reshape(B, dil, nb, DIL_BLOCK, H, d)

    def with_prev(t):
        prev = jnp.pad(t[:, :, :-1], ((0, 0), (0, 0), (1, 0), (0, 0), (0, 0), (0, 0)))
        return jnp.concatenate([prev, t], axis=3)

    qb = regroup(q)
    kw = with_prev(regroup(k))
    vw = with_prev(regroup(v)).astype(F32)
    i = jnp.arange(DIL_BLOCK)[:, None]
    jj = jnp.arange(2 * DIL_BLOCK)[None, :]
    dist = i + DIL_BLOCK - jj
    band = (dist >= 0) & (dist <= span)
    valid = band[None] & ((jnp.arange(nb)[:, None, None] > 0) | (jj >= DIL_BLOCK)[None])
    s = jnp.einsum('brnihd,brnjhd->brnhij', qb, kw, preferred_element_type=F32) * (d ** -0.5)
    s = jnp.where(valid[None, None, :, None], s, NEG_INF)
    m = jnp.max(s, axis=-1, keepdims=True)
    p = jnp.exp(s - m)
    den = jnp.sum(p, axis=-1)
    o = jnp.einsum('brnhij,brnjhd->brnihd', p, vw) / jnp.swapaxes(den, 3, 4)[..., None]
    lse = jnp.swapaxes(m[..., 0] + jnp.log(den), 3, 4)
    o = o.reshape(B, dil, m_len, H, d).transpose(0, 2, 1, 3, 4).reshape(B, s_pad, H, d)[:, :S]
    lse = lse.reshape(B, dil, m_len, H).transpose(0, 2, 1, 3).reshape(B, s_pad, H)[:, :S]
    return o, lse


def dilated_sample(q, k_all, v_all, dil, span, w_buf):
    B, L, H, d = q.shape
    idx = w_buf + jnp.arange(L)[:, None] - dil * jnp.arange(span + 1)[None, :]
    valid = idx >= 0
    flat = jnp.maximum(idx, 0).reshape(-1)
    kg = jnp.take(k_all, flat, axis=1).reshape(B, L, span + 1, H, d)
    vg = jnp.take(v_all, flat, axis=1).reshape(B, L, span + 1, H, d).astype(F32)
    s = jnp.einsum('blhd,bljhd->bhlj', q, kg, preferred_element_type=F32) * (d ** -0.5)
    s = jnp.where(valid[None, None], s, NEG_INF)
    m = jnp.max(s, axis=-1, keepdims=True)
    p = jnp.exp(s - m)
    den = jnp.sum(p, axis=-1)
    o = jnp.einsum('bhlj,bljhd->blhd', p, vg) / jnp.swapaxes(den, 1, 2)[..., None]
    lse = jnp.swapaxes(m[..., 0] + jnp.log(den), 1, 2)
    return o, lse


def combine_dilations(outs, lses):
    wts = jax.nn.softmax(jnp.stack(lses, axis=0), axis=0)
    return jnp.einsum('pblh,pblhd->blhd', wts, jnp.stack(outs, axis=0))


def prompt_mixer(rq, rk, rv, aq, ak, av):
    B, S = rq.shape[:2]
    s0 = jnp.zeros((B, RET_HEADS, RET_DK, RET_DV), F32)
    ret_o, s_fin = retention_chunkwise(rq, rk, rv, s0, min(RET_CHUNK, S))
    res = [dilated_prompt(aq, ak, av, dil, win // dil) for win, dil in DIL_PATTERNS]
    att_o = combine_dilations([r[0] for r in res], [r[1] for r in res])
    keep = min(MAX_WINDOW, S)
    return ret_o, att_o, (s_fin.astype(rv.dtype), ak[:, S - keep:], av[:, S - keep:])


def sample_mixer(state_ret, cache_k, cache_v, rq, rk, rv, aq, ak, av):
    L = rq.shape[1]
    ret_o, s_new = retention_chunkwise(rq, rk, rv, state_ret.astype(F32), L)
    w_buf = cache_k.shape[1]
    k_all = jnp.concatenate([cache_k.astype(ak.dtype), ak], axis=1)
    v_all = jnp.concatenate([cache_v.astype(av.dtype), av], axis=1)
    res = [dilated_sample(aq, k_all, v_all, dil, win // dil, w_buf) for win, dil in DIL_PATTERNS]
    att_o = combine_dilations([r[0] for r in res], [r[1] for r in res])
    return ret_o, att_o, (s_new.astype(state_ret.dtype), k_all[:, L:], v_all[:, L:])


def project(h, w_in, pos):
    B, L, _ = h.shape
    z = jnp.einsum('bld,de->ble', h, w_in)
    bounds = [int(b) for b in np.cumsum(IN_SPLITS)[:-1]]
    rq, rk, rv, rg, aq, ak, av = jnp.split(z, bounds, axis=-1)
    rq = retention_rotate(rq.reshape(B, L, RET_HEADS, RET_DK), pos)
    rk = retention_rotate(rk.reshape(B, L, RET_HEADS, RET_DK), pos) * (RET_DK ** -0.5)
    rv = rv.reshape(B, L, RET_HEADS, RET_DV)
    aq = rope_half(aq.reshape(B, L, ATT_HEADS, ATT_HD), pos)
    ak = rope_half(ak.reshape(B, L, ATT_HEADS, ATT_HD), pos)
    av = av.reshape(B, L, ATT_HEADS, ATT_HD)
    return rq, rk, rv, rg, aq, ak, av


def decoder_layer(x, c, pos, mix_fn, w_ada, b_ada, g_pre_mix, g_post_mix, g_pre_ffn, g_post_ffn,
                  w_in, ret_gain, w_o, w_up, w_down):
    B, L, _ = x.shape
    mod = jnp.einsum('bd,de->be', jax.nn.silu(c), w_ada) + b_ada
    sh1, sc1, gt1, sh2, sc2, gt2 = [m[:, None, :] for m in jnp.split(mod, 6, axis=-1)]
    h = rmsnorm(x, g_pre_mix) * (1.0 + sc1) + sh1
    rq, rk, rv, rg, aq, ak, av = project(h, w_in, pos)
    ret_o, att_o, new_state = mix_fn(rq, rk, rv, aq, ak, av)
    heads = jnp.concatenate([retention_readout(ret_o, rg, ret_gain),
                             att_o.reshape(B, L, ATT_WIDTH).astype(h.dtype)], axis=-1)
    mixed = jnp.einsum('ble,ed->bld', heads, w_o)
    x = x + gt1 * rmsnorm(mixed, g_post_mix)
    h = rmsnorm(x, g_pre_ffn) * (1.0 + sc2) + sh2
    f = jnp.einsum('blf,fd->bld', jnp.square(jax.nn.relu(jnp.einsum('bld,df->blf', h, w_up))), w_down)
    x = x + gt2 * rmsnorm(f, g_post_ffn)
    return x, new_state


def setup_inputs(seed: int = 0) -> dict:
    key = jax.random.key(seed)
    ks = jax.random.split(key, 20)
    w_buf = min(MAX_WINDOW, PAST_LEN)

    def nrm(k, shape, scale):
        return jax.random.normal(k, shape, F32) * scale

    return {
        'x_prompt': nrm(ks[0], (BATCH, SEQ, D_MODEL), 1.0),
        'x_sample': nrm(ks[1], (DEC_BATCH, DEC_SEQ, D_MODEL), 1.0),
        'c_prompt': nrm(ks[2], (BATCH, D_MODEL), 1.0),
        'c_sample': nrm(ks[3], (DEC_BATCH, D_MODEL), 1.0),
        'state_ret': nrm(ks[4], (DEPTH, DEC_BATCH, RET_HEADS, RET_DK, RET_DV), 0.5),
        'cache_win_k': nrm(ks[5], (DEPTH, DEC_BATCH, w_buf, ATT_HEADS, ATT_HD), 1.0),
        'cache_win_v': nrm(ks[6], (DEPTH, DEC_BATCH, w_buf, ATT_HEADS, ATT_HD), 1.0),
        'w_ada': nrm(ks[7], (DEPTH, D_MODEL, 6 * D_MODEL), 0.5 * D_MODEL ** -0.5),
        'b_ada': nrm(ks[8], (DEPTH, 6 * D_MODEL), 0.02),
        'g_pre_mix': 1.0 + nrm(ks[9], (DEPTH, D_MODEL), 0.05),
        'g_post_mix': 1.0 + nrm(ks[10], (DEPTH, D_MODEL), 0.05),
        'g_pre_ffn': 1.0 + nrm(ks[11], (DEPTH, D_MODEL), 0.05),
        'g_post_ffn': 1.0 + nrm(ks[12], (DEPTH, D_MODEL), 0.05),
        'w_in': nrm(ks[13], (DEPTH, D_MODEL, IN_WIDTH), D_MODEL ** -0.5),
        'ret_gain': 1.0 + nrm(ks[14], (DEPTH, RET_HEADS * RET_DV), 0.05),
        'w_o': nrm(ks[15], (DEPTH, MIX_WIDTH, D_MODEL), MIX_WIDTH ** -0.5),
        'w_up': nrm(ks[16], (DEPTH, D_MODEL, D_FF), D_MODEL ** -0.5),
        'w_down': nrm(ks[17], (DEPTH, D_FF, D_MODEL), D_FF ** -0.5),
    }


def reference(x_prompt, x_sample, c_prompt, c_sample, state_ret, cache_win_k, cache_win_v,
              w_ada, b_ada, g_pre_mix, g_post_mix, g_pre_ffn, g_post_ffn, w_in, ret_gain, w_o,
              w_up, w_down):
    pos_p = jnp.arange(x_prompt.shape[1], dtype=jnp.int32)
    pos_s = PAST_LEN + jnp.arange(x_sample.shape[1], dtype=jnp.int32)
    y_prompt, y_sample = x_prompt, x_sample
    new_p, new_s = [], []
    for l in range(DEPTH):
        layer_w = (w_ada[l], b_ada[l], g_pre_mix[l], g_post_mix[l], g_pre_ffn[l], g_post_ffn[l],
                   w_in[l], ret_gain[l], w_o[l], w_up[l], w_down[l])
        y_prompt, st_p = decoder_layer(y_prompt, c_prompt, pos_p, prompt_mixer, *layer_w)
        mix_s = functools.partial(sample_mixer, state_ret[l], cache_win_k[l], cache_win_v[l])
        y_sample, st_s = decoder_layer(y_sample, c_sample, pos_s, mix_s, *layer_w)
        new_p.append(st_p)
        new_s.append(st_s)
    state_ret_prompt = jnp.stack([s[0] for s in new_p])
    cache_win_k_prompt = jnp.stack([s[1] for s in new_p])
    cache_win_v_prompt = jnp.stack([s[2] for s in new_p])
    state_ret_sample = jnp.stack([s[0] for s in new_s])
    cache_win_k_sample = jnp.stack([s[1] for s in new_s])
    cache_win_v_sample = jnp.stack([s[2] for s in new_s])
    return (y_prompt, y_sample, state_ret_prompt, cache_win_k_prompt, cache_win_v_prompt,
            state_ret_sample, cache_win_k_sample, cache_win_v_sample)
```

```python
import os
import numpy as np
import ml_dtypes
from contextlib import ExitStack
import concourse.bass as bass
import concourse.mybir as mybir
from concourse.bass_utils import run_bass_kernel_spmd

F32 = mybir.dt.float32
BF16 = mybir.dt.bfloat16
AF = mybir.ActivationFunctionType
ALU = mybir.AluOpType

N_DMA_SEMS = 24
SAME_ENGINE_SYNC = True
NEG = -30000.0
EPS = 1e-6
GAM = [1.0 - 2.0 ** (-5 - h) for h in range(4)]


class _Rec:
    def __init__(self):
        self.call = None

    def __getattr__(self, name):
        def f(*a, **k):
            self.call = (name, a, k)
            return self
        return f


class Sched:
    def __init__(self, nc):
        self.nc = nc
        self.engs = ['pe', 'act', 'dve', 'pool', 'sp']
        self.ops = []
        self.res_w = {}
        self.res_r = {}
        self.last = {}
        self.dmas_since = []
        self.pending = {}

    def barrier(self):
        L = set(self.last.values()) | set(self.dmas_since)
        self.dmas_since = []
        for e in self.engs:
            self.pending[e] = set(L) | self.pending.get(e, set())

    def op(self, eng, fn, reads=(), writes=(), dma=False):
        oid = len(self.ops)
        deps = set()
        raw = set()
        if self.pending.get(eng):
            deps |= self.pending.pop(eng)
        for r in reads:
            w = self.res_w.get(r)
            if w is not None:
                deps.add(w)
                raw.add(w)
        for w_ in writes:
            w = self.res_w.get(w_)
            if w is not None:
                deps.add(w)
            for rd in self.res_r.get(w_, ()):
                deps.add(rd)
        rec = _Rec()
        fn(rec)
        assert rec.call is not None
        self.ops.append(dict(eng=eng, fn=rec.call, deps=deps, raw=raw, dma=dma, id=oid))
        if dma:
            self.dmas_since.append(oid)
        else:
            self.last[eng] = oid
        for r in reads:
            self.res_r.setdefault(r, []).append(oid)
        for w_ in writes:
            self.res_w[w_] = oid
            self.res_r[w_] = []
        return oid

    def pe(self, fn, reads=(), writes=()):
        return self.op('pe', fn, reads, writes)

    def act(self, fn, reads=(), writes=()):
        return self.op('act', fn, reads, writes)

    def dve(self, fn, reads=(), writes=()):
        return self.op('dve', fn, reads, writes)

    def pool(self, fn, reads=(), writes=()):
        return self.op('pool', fn, reads, writes)

    def dma(self, fn, reads=(), writes=(), q='sp'):
        return self.op(q, fn, reads, writes, dma=True)

    def emit(self, stack):
        nc = self.nc
        ops = self.ops
        needed = [False] * len(ops)
        for o in ops:
            for d in o['deps']:
                de = ops[d]
                if de['dma']:
                    continue
                if de['eng'] == o['eng'] and not o['dma']:
                    if de['eng'] == 'pe' or not SAME_ENGINE_SYNC or d not in o['raw']:
                        continue
                needed[d] = True
        sems = {e: stack.enter_context(nc.semaphore("s_" + e)) for e in ['pe', 'act', 'dve', 'pool']}
        dsems = [stack.enter_context(nc.semaphore("d%d" % i)) for i in range(N_DMA_SEMS)]
        cnt = {e: 0 for e in sems}
        ndma = 0
        duse = [0] * N_DMA_SEMS
        half = N_DMA_SEMS // 2
        nd = {'sp': 0, 'pool': 0}
        for o in ops:
            if o['dma']:
                qn = 'pool' if o['eng'] == 'pool' else 'sp'
                k = (nd[qn] % half) + (half if qn == 'pool' else 0)
                nd[qn] += 1
                ndma += 1
                o['prev_tok'] = ('d', k, duse[k] * 16) if duse[k] else None
                duse[k] += 1
                o['tok'] = ('d', k, duse[k] * 16)
            elif needed[o['id']]:
                cnt[o['eng']] += 1
                o['tok'] = ('c', o['eng'], cnt[o['eng']])
            else:
                o['tok'] = None
        all_dma_toks = [o['tok'] for o in ops if o['dma']]
        streams = {e: [] for e in self.engs}
        for o in ops:
            streams[o['eng']].append(o)
        waited = {e: {} for e in self.engs}

        def semof(key):
            return dsems[key[1]] if key[0] == 'd' else sems[key[1]]

        def run_stream(ename, handle):
            wd = waited[ename]
            for o in streams[ename]:
                toks = []
                for d in o['deps']:
                    de = ops[d]
                    t = de['tok']
                    if t is None:
                        continue
                    if (not de['dma']) and de['eng'] == ename and not o['dma']:
                        if ename == 'pe' or not SAME_ENGINE_SYNC or d not in o['raw']:
                            continue
                    toks.append(t)
                if o['dma'] and o['prev_tok'] is not None:
                    toks.append(o['prev_tok'])
                best = {}
                for t in toks:
                    key = (t[0], t[1])
                    if t[2] > best.get(key, 0):
                        best[key] = t[2]
                for key, v in best.items():
                    if wd.get(key, 0) >= v:
                        continue
                    wd[key] = v
                    handle.wait_ge(semof(key), v)
                name_, a_, k_ = o['fn']
                ins = getattr(handle, name_)(*a_, **k_)
                t = o['tok']
                if t is not None:
                    ins.then_inc(semof(t), 16 if t[0] == 'd' else 1)
            return wd

        with nc.Block() as block:
            @block.tensor
            def _(e):
                run_stream('pe', e)

            @block.scalar
            def _(e):
                run_stream('act', e)

            @block.vector
            def _(e):
                run_stream('dve', e)

            @block.gpsimd
            def _(e):
                run_stream('pool', e)

            @block.sync
            def _(e):
                wd = run_stream('sp', e)
                final = {}
                for t in all_dma_toks:
                    final[t[1]] = max(final.get(t[1], 0), t[2])
                for k, v in final.items():
                    if wd.get(('d', k), 0) < v:
                        e.wait_ge(dsems[k], v)
                for en, c in cnt.items():
                    if c:
                        e.wait_ge(sems[en], c)


def build_program(debug=False, stop_after=99):
    nc = bass.Bass("TRN2", target_bir_lowering=False)

    def din(name, shape, dt=F32):
        return nc.dram_tensor(name, list(shape), dt, kind="ExternalInput").ap()

    def dout(name, shape, dt=F32):
        return nc.dram_tensor(name, list(shape), dt, kind="ExternalOutput").ap()

    xT = din("xT", [128, 8, 8192])
    cT = din("cT", [128, 8, 5])
    wada = din("wada", [128, 8, 6144])
    bada = din("bada", [128, 48])
    gvec = din("gvec", [128, 4, 8])
    retg = din("retg", [128, 4])
    wret = din("wret", [128, 8, 2048])
    watt = din("watt", [128, 8, 1536])
    woR = din("woR", [128, 4, 1024])
    woA = din("woA", [64, 8, 1024])
    wup = din("wup", [128, 8, 4096])
    wdn = din("wdn", [8, 128, 32, 128])
    tabRK = din("tabRK", [128, 2, 8192])
    tabRQ = din("tabRQ", [128, 2, 2048])
    tabAK = din("tabAK", [128, 2, 4096])
    tabAQ = din("tabAQ", [128, 2, 2048])
    kdectab_d = din("kdectab", [128, 512])
    dmask_d = din("dmask", [128, 512])
    qdec_d = din("qdec", [128, 1024])
    cdec_d = din("cdec", [128, 4])
    flags_d = din("flags", [128, 4])
    mmain_d = din("mmain", [128, 256])
    mprev_d = din("mprev", [128, 128])
    ident_d = din("ident", [128, 128])
    sel_d = din("sel", [128, 64])

    xsT = din("xsT", [128, 8, 32])
    tabS_d = din("tabS", [128, 8, 32])
    sinit = din("sinit", [4, 128, 1024])
    kcT = din("kcT", [4, 128, 4, 2048])
    kcn = din("kcn", [4, 2048, 512])
    vcn = din("vcn", [4, 2048, 512])
    mmult_d = din("mmult", [128, 136])
    dec8_d = din("dec8", [128, 640])
    ysT = dout("ysT", [128, 8, 32])
    sstate = dout("sstate", [4, 128, 1024])
    kcs = dout("kcs", [4, 2040, 512])
    vcs = dout("vcs", [4, 2040, 512])
    ksnew = dout("ksnew", [128, 4, 32])
    vsnew = dout("vsnew", [8, 4, 512])
    yT = dout("yT", [128, 8, 2048])
    kTo = dout("kTo", [128, 4, 2048])
    vnat = dout("vnat", [4096, 8, 65])
    sfin = dout("sfin", [128, 2, 512])
    if debug:
        dbg_hatt = dout("dbg_hatt", [64, 8, 2048])
        dbg_hret = dout("dbg_hret", [128, 4, 2048])
        dbg_x1 = dout("dbg_x1", [128, 8, 2048])

    st = ExitStack()
    S = Sched(nc)

    KB = 1024
    p0 = [0]

    def at(name, shape, dt, off):
        assert off % 32 == 0, (name, off)
        nbytes = int(np.prod(shape[1:])) * (2 if dt == BF16 else 4)
        assert off + nbytes <= 212700, (name, off, nbytes)
        return nc.alloc_sbuf_tensor_at(name, list(shape), dt, offset=off + 16640)

    def sb(name, shape, dt):
        nbytes = int(np.prod(shape[1:])) * (2 if dt == BF16 else 4)
        off = p0[0]
        p0[0] = (off + nbytes + 31) // 32 * 32
        assert p0[0] <= 18 * KB, name
        return at(name, shape, dt, off)

    def seq_at(base):
        cur = [base]

        def f(name, shape, dt):
            nbytes = int(np.prod(shape[1:])) * (2 if dt == BF16 else 4)
            off = cur[0]
            cur[0] = (off + nbytes + 31) // 32 * 32
            return at(name, shape, dt, off)
        f.cur = cur
        return f

    ps = [st.enter_context(nc.psum_tensor("ps%d" % i, [128, 512], F32)) for i in range(7)]
    psb = st.enter_context(nc.psum_tensor("psb", [128, 1024], BF16))

    ident = sb("identb", [128, 128], BF16)
    ones = sb("onesb", [128, 128], BF16)
    self32 = sb("self32", [128, 64], F32)
    mmain = sb("mmain_s", [128, 256], BF16)
    mprev = sb("mprev_s", [128, 128], BF16)
    kdectab = sb("kdectab_s", [128, 512], F32)
    dmask = sb("dmask_s", [128, 512], F32)
    cdec = sb("cdec_s", [128, 4], F32)
    flags = sb("flags_s", [128, 4], F32)
    gv = sb("gv_s", [128, 4, 8], F32)
    rgain = sb("rgain_s", [128, 4], F32)
    bad = sb("bad_s", [128, 48], F32)
    cTs = sb("cT_s", [128, 8, 5], F32)
    scT = sb("scT", [128, 8, 5], BF16)
    mod = sb("mod", [128, 48, 5], F32)
    gm1 = sb("gm1", [128, 8, 5], F32)
    gg1 = sb("gg1", [128, 8, 5], F32)
    gm2 = sb("gm2", [128, 8, 5], F32)
    gg2 = sb("gg2", [128, 8, 5], F32)
    epsb = sb("epsb", [128, 1], F32)
    xs = sb("xs", [128, 8, 32], F32)
    x1s = sb("x1s", [128, 8, 32], F32)
    hTs = sb("hTs", [128, 8, 32], BF16)
    hret_s = sb("hret_s", [128, 4, 32], BF16)
    hatt_s = sb("hatt_s", [64, 8, 32], BF16)
    aqs = sb("aqs", [128, 4, 32], BF16)
    aks = sb("aks", [128, 4, 32], BF16)
    tabS = sb("tabS_s", [128, 8, 32], F32)
    S.dma(lambda e: e.dma_start(out=xs[:], in_=xsT), writes=['xs'])
    S.dma(lambda e: e.dma_start(out=tabS[:], in_=tabS_d), writes=['tabS'])

    for (dst, src, q, nm) in [(ident, ident_d, 'pool', 'ident'), (mmain, mmain_d, 'pool', 'mmain'),
                              (mprev, mprev_d, 'pool', 'mprev'), (self32, sel_d, 'sp', 'sel'),
                              (kdectab, kdectab_d, 'sp', 'kdectab'), (dmask, dmask_d, 'sp', 'dmask'),
                              (cdec, cdec_d, 'sp', 'cdec'),
                              (flags, flags_d, 'sp', 'flags'), (gv, gvec, 'sp', 'gv'),
                              (rgain, retg, 'sp', 'rgain'), (bad, bada, 'sp', 'bad'), (cTs, cT, 'sp', 'cTs')]:
        S.dma(lambda e, d=dst, s=src: e.dma_start(out=d[:], in_=s), writes=[nm], q=q)
    S.pool(lambda e: e.memset(ones[:], 1.0), writes=['ones'])
    S.pool(lambda e: e.memset(epsb[:], EPS), writes=['epsb'])

    S.act(lambda e: e.activation(out=scT[:], in_=cTs[:], func=AF.Silu), reads=['cTs'], writes=['scT'])
    wa = [at("wa%d" % i, [128, 8, 1024], BF16, (98 + 16 * i) * KB) for i in range(2)]
    for fc in range(6):
        wb_ = wa[fc % 2]
        S.dma(lambda e, d=wb_, fc=fc: e.dma_start(out=d[:], in_=wada[:, :, fc * 1024:(fc + 1) * 1024]),
              writes=['wa%d' % (fc % 2)], q='pool')
        pb = ps[fc % 2]
        for ft in range(8):
            for kt in range(8):
                S.pe(lambda e, pb=pb, wb_=wb_, ft=ft, kt=kt: e.matmul(
                    pb[:, ft * 8:ft * 8 + 5], lhsT=wb_[:, kt, ft * 128:(ft + 1) * 128], rhs=scT[:, kt, :],
                    start=(kt == 0), stop=(kt == 7)), reads=['wa%d' % (fc % 2), 'scT'], writes=['ps%d' % (fc % 2)])
        for ft in range(8):
            f = fc * 8 + ft
            S.act(lambda e, pb=pb, ft=ft, f=f: e.activation(
                out=mod[:, f, :], in_=pb[:, ft * 8:ft * 8 + 5], func=AF.Identity, bias=bad[:, f:f + 1], scale=1.0),
                reads=['ps%d' % (fc % 2), 'bad'], writes=['mod'])
    for (dst, nm, gi, mo, addone) in [(gm1, 'gm1', 0, 8, True), (gg1, 'gg1', 1, 16, False),
                                      (gm2, 'gm2', 2, 32, True), (gg2, 'gg2', 3, 40, False)]:
        for kt in range(8):
            if addone:
                S.dve(lambda e, dst=dst, gi=gi, mo=mo, kt=kt: e.tensor_scalar(
                    out=dst[:, kt, :], in0=mod[:, mo + kt, :], scalar1=1.0, scalar2=gv[:, gi, kt:kt + 1],
                    op0=ALU.add, op1=ALU.mult), reads=['mod', 'gv'], writes=[nm])
            else:
                S.dve(lambda e, dst=dst, gi=gi, mo=mo, kt=kt: e.tensor_scalar(
                    out=dst[:, kt, :], in0=mod[:, mo + kt, :], scalar1=gv[:, gi, kt:kt + 1], scalar2=None,
                    op0=ALU.mult), reads=['mod', 'gv'], writes=[nm])
    SH1, SH2 = 0, 24

    rstd = [sb("rstd%d" % i, [128, 512], F32) for i in range(2)]
    rot_t = [at("rot_t%d" % i, [128, 512], F32, (22 + 2 * i) * KB) for i in range(4)]
    gctr = [0]

    def gbank():
        gctr[0] += 1
        return gctr[0] % 3

    def rstd_from_sq(sq_aps, sq_res, out_i, scale, N=512):
        n = len(sq_aps)
        for i, a in enumerate(sq_aps):
            S.pe(lambda e, a=a, i=i: e.matmul(ps[3][:, 0:N], lhsT=ones[:], rhs=a, start=(i == 0), stop=(i == n - 1)),
                 reads=['ones'] + sq_res, writes=['ps3'])
        r = rstd[out_i]
        S.act(lambda e, r=r: e.activation(out=r[:, 0:N], in_=ps[3][:, 0:N], func=AF.Ln, bias=epsb[:, 0:1], scale=scale),
              reads=['ps3', 'epsb'], writes=['rstd%d' % out_i])
        S.act(lambda e, r=r: e.activation(out=r[:, 0:N], in_=r[:, 0:N], func=AF.Exp, scale=-0.5), reads=['rstd%d' % out_i], writes=['rstd%d' % out_i])
        return r

    xt = [at("xt%d" % i, [128, 8, 512], F32, (30 + 16 * i) * KB) for i in range(2)]
    sq = at("sq", [128, 8, 512], BF16, 62 * KB)
    hT0 = at("hT", [128, 8, 512], BF16, 70 * KB)
    hTb = [hT0, hT0]
    curh = [0]
    tmpf = [at("tmpf%d" % i, [128, 512], F32, (18 + 2 * i) * KB) for i in range(2)]
    sqbox = [sq]

    def load_x(col0, bi):
        S.dma(lambda e: e.dma_start(out=xt[bi][:], in_=xT[:, :, col0:col0 + 512]), writes=['xt%d' % bi])

    def prenorm(bi, gm, shrow, seq=0):
        x = xt[bi]
        hT = hTb[bi]
        for kt in range(8):
            S.act(lambda e, kt=kt: e.activation(out=sq[:, kt, :], in_=x[:, kt, :], func=AF.Square),
                  reads=['xt%d' % bi], writes=['sq'])
        r = rstd_from_sq([sq[:, kt, :] for kt in range(8)], ['sq'], 0, 1.0 / 1024)
        for kt in range(8):
            t = tmpf[kt % 2]
            S.dve(lambda e, t=t, kt=kt: e.tensor_tensor(out=t[:], in0=x[:, kt, :], in1=r[:], op=ALU.mult),
                  reads=['xt%d' % bi, 'rstd0'], writes=['tmpf%d' % (kt % 2)])
            S.act(lambda e, t=t, kt=kt: e.activation(out=hT[:, kt, :], in_=t[:], func=AF.Identity,
                                                     bias=mod[:, shrow + kt, seq:seq + 1], scale=gm[:, kt, seq:seq + 1]),
                  reads=['tmpf%d' % (kt % 2), 'mod', 'gm1', 'gm2'], writes=['hT%d' % bi])

    def proj_fm(w, wres, col0, bank):
        hT = hTb[curh[0]]
        for kt in range(8):
            S.pe(lambda e, kt=kt: e.matmul(ps[bank][:], lhsT=w[:, kt, col0:col0 + 128], rhs=hT[:, kt, :],
                                           start=(kt == 0), stop=(kt == 7)),
                 reads=[wres, 'hT%d' % curh[0]], writes=['ps%d' % bank])

    def rotate(bA, bB, tab, tabres, outA, outB, ores, outA32=None, outB32=None, o32res=None, N=512, cs=None):
        t0, t1, t2, t3 = [t[:, 0:N] for t in rot_t]
        cos, sin = cs if cs is not None else (tab[:, 0, :], tab[:, 1, :])
        pA, pB = ps[bA][:, 0:N], ps[bB][:, 0:N]
        S.dve(lambda e: e.tensor_tensor(out=t0, in0=pA, in1=cos, op=ALU.mult), reads=['ps%d' % bA, tabres], writes=['rot0'])
        S.dve(lambda e: e.tensor_tensor(out=t1, in0=pB, in1=sin, op=ALU.mult), reads=['ps%d' % bB, tabres], writes=['rot1'])
        S.dve(lambda e: e.tensor_tensor(out=t2, in0=pB, in1=cos, op=ALU.mult), reads=['ps%d' % bB, tabres], writes=['rot2'])
        S.dve(lambda e: e.tensor_tensor(out=t3, in0=pA, in1=sin, op=ALU.mult), reads=['ps%d' % bA, tabres], writes=['rot3'])
        S.dve(lambda e: e.tensor_tensor(out=outA, in0=t0, in1=t1, op=ALU.subtract), reads=['rot0', 'rot1'], writes=[ores])
        S.dve(lambda e: e.tensor_tensor(out=outB, in0=t2, in1=t3, op=ALU.add), reads=['rot2', 'rot3'], writes=[ores])
        if outA32 is not None:
            S.pool(lambda e: e.tensor_tensor(out=outA32, in0=t0, in1=t1, op=ALU.subtract), reads=['rot0', 'rot1'], writes=[o32res])
            S.pool(lambda e: e.tensor_tensor(out=outB32, in0=t2, in1=t3, op=ALU.add), reads=['rot2', 'rot3'], writes=[o32res])

    def proj_fm_s(w, wres, col0, bank):
        for kt in range(8):
            S.pe(lambda e, kt=kt: e.matmul(ps[bank][:, 0:32], lhsT=w[:, kt, col0:col0 + 128], rhs=hTs[:, kt, :],
                                           start=(kt == 0), stop=(kt == 7)),
                 reads=[wres, 'hTs'], writes=['ps%d' % bank])

    def tabs(kind):
        return (tabS[:, 2 * kind, :], tabS[:, 2 * kind + 1, :])

    if stop_after < 1:
        S.emit(st)
        return nc
    hret = at("hret", [128, 4, 2048], BF16, 82 * KB)
    S.barrier()
    if True:
        sb1 = seq_at(98 * KB)
        wr = sb1("wr", [128, 8, 2048], BF16)
        qdec = sb1("qdec_s", [128, 1024], F32)
        dec8 = sb1("dec8_s", [128, 640], F32)
        S.dma(lambda e: e.dma_start(out=dec8[:], in_=dec8_d), writes=['dec8'])
        S.dma(lambda e: e.dma_start(out=qdec[:], in_=qdec_d), writes=['qdec'])
        S.dma(lambda e: e.dma_start(out=wr[:, :, 0:1024], in_=wret[:, :, 0:1024]), writes=['wr'], q='pool')
        S.dma(lambda e: e.dma_start(out=wr[:, :, 1024:2048], in_=wret[:, :, 1024:2048]), writes=['wr'], q='pool')
        tk_off = sb1.cur[0]
        tk = [sb1("tk%d" % i, [128, 2, 512], F32) for i in range(2)]
        tq = [sb1("tq%d" % i, [128, 2, 512], F32) for i in range(2)]
        rk = sb1("rk", [128, 4, 512], BF16)
        rq = sb1("rq", [128, 4, 512], BF16)
        rqd = sb1("rqd", [128, 4, 512], BF16)
        rv = sb1("rv", [128, 4, 512], BF16)
        sg = sb1("sg", [128, 4, 512], BF16)
        sgf = sb1("sgf", [128, 512], F32)
        kdec4 = sb1("kdec4", [128, 4, 512], BF16)
        am4 = sb1("am4", [128, 4, 512], BF16)
        Sst = [sb1("Sst%d" % p, [128, 512], F32) for p in range(2)]
        Sring = [[sb1("Sr%d_%d" % (p, v), [128, 512], BF16) for v in range(5)] for p in range(2)]
        osq = sb1("osq", [128, 512], BF16)
        on = sb1("on", [128, 512], F32)
        for p in range(2):
            S.pool(lambda e, p=p: e.memset(Sst[p][:], 0.0), writes=['Sst%d' % p])
            S.pool(lambda e, p=p: e.memset(Sring[p][0][:], 0.0), writes=['Sr%d_0' % p])
        sver = 0
        rctr = [0]
        hT1a_off = sb1.cur[0]
        hT1a = sb1("hT1a", [128, 8, 512], BF16)
        hTb[1] = hT1a
        load_x(0, 0)
        load_x(512, 1)
        prenorm(0, gm1, SH1)
        for ti in range(16):
            slot, own = ti // 4, ti >= 12
            bi = ti % 2
            col0 = ti * 512
            S.dma(lambda e, bi=bi, col0=col0: e.dma_start(out=tk[bi][:], in_=tabRK[:, :, col0:col0 + 512]), writes=['tk%d' % bi])
            if own:
                oc0 = col0 - 6144
                S.dma(lambda e, bi=bi, oc0=oc0: e.dma_start(out=tq[bi][:], in_=tabRQ[:, :, oc0:oc0 + 512]), writes=['tq%d' % bi])
            if ti + 1 < 16:
                prenorm((ti + 1) % 2, gm1, SH1)
            if ti + 2 < 16:
                load_x((ti + 2) * 512, ti % 2)
            curh[0] = bi
            hT = hTb[bi]
            ringb = [0, 1, 2] if own else [0, 1, 2, 6]

            def rb():
                rctr[0] += 1
                return ringb[rctr[0] % len(ringb)]
            for p in range(2):
                bA, bB = rb(), rb()
                proj_fm(wr, 'wr', (4 + 2 * p) * 128, bA)
                proj_fm(wr, 'wr', (5 + 2 * p) * 128, bB)
                rotate(bA, bB, tk[bi], 'tk%d' % bi, rk[:, 2 * p, :], rk[:, 2 * p + 1, :], 'rk')
            for c in range(4):
                b = rb()
                for kt in range(8):
                    S.pe(lambda e, kt=kt, c=c: e.matmul(ps[b][:], lhsT=hT[:, kt, c * 128:(c + 1) * 128], rhs=wr[:, kt, 1536:2048],
                                                        start=(kt == 0), stop=(kt == 7)), reads=['wr', 'hT%d' % bi], writes=['ps%d' % b])
                if own:
                    S.act(lambda e, c=c: e.copy(out=rv[:, c, :], in_=ps[b][:]), reads=['ps%d' % b], writes=['rv%d' % c])
                else:
                    for h in range(4):
                        sc_ = float(GAM[h] ** (128 * (3 - c)))
                        S.act(lambda e, c=c, h=h, sc_=sc_: e.mul(out=rv[:, c, h * 128:(h + 1) * 128], in_=ps[b][:, h * 128:(h + 1) * 128], mul=sc_),
                              reads=['ps%d' % b], writes=['rv%d' % c])
            if not own:
                for rd in range(2):
                    for cl in range(2):
                        c = 2 * rd + cl
                        cc = slice(c * 128, (c + 1) * 128)
                        for idx in range(4):
                            S.pe(lambda e, idx=idx, cc=cc, cl=cl: e.transpose(out=psb[:, (cl * 4 + idx) * 128:(cl * 4 + idx + 1) * 128], in_=rk[:, idx, cc], identity=ident[:]),
                                 reads=['rk', 'ident'], writes=['psb'])
                    for cl in range(2):
                        c = 2 * rd + cl
                        S.dve(lambda e, c=c, cl=cl: e.tensor_tensor(out=kdec4[:, c, :], in0=psb[:, cl * 512:(cl + 1) * 512], in1=kdectab[:], op=ALU.mult),
                              reads=['psb', 'kdectab'], writes=['kdec4_%d' % c])
                for p in range(2):
                    for ab in range(2):
                        for c in range(4):
                            S.pe(lambda e, p=p, ab=ab, c=c: e.matmul(ps[4 + p][:, ab * 256:(ab + 1) * 256], lhsT=kdec4[:, c, (2 * p + ab) * 128:(2 * p + ab + 1) * 128],
                                                                     rhs=rv[:, c, p * 256:(p + 1) * 256], start=(c == 0), stop=(c == 3)),
                                 reads=['kdec4_%d' % c, 'rv%d' % c], writes=['ps%d' % (4 + p)])
                for p in range(2):
                    S.dve(lambda e, p=p: e.scalar_tensor_tensor(out=Sst[p][:], in0=Sst[p][:], scalar=cdec[:, 2 + p:3 + p], in1=ps[4 + p][:],
                                                                op0=ALU.mult, op1=ALU.add), reads=['Sst%d' % p, 'cdec', 'ps%d' % (4 + p)], writes=['Sst%d' % p])
                    if ti % 4 == 3 and slot < 3:
                        S.dve(lambda e, p=p, slot=slot: e.tensor_scalar(out=Sst[p][:], in0=Sst[p][:], scalar1=flags[:, slot:slot + 1], scalar2=None, op0=ALU.mult),
                              reads=['Sst%d' % p, 'flags'], writes=['Sst%d' % p])
                    if ti == 11:
                        S.act(lambda e, p=p: e.copy(out=Sring[p][0][:], in_=Sst[p][:]), reads=['Sst%d' % p], writes=['Sr%d_0' % p])
                continue
            if own:
                for p in range(2):
                    proj_fm(wr, 'wr', (0 + 2 * p) * 128, 0)
                    proj_fm(wr, 'wr', (1 + 2 * p) * 128, 1)
                    rotate(0, 1, tq[bi], 'tq%d' % bi, rq[:, 2 * p, :], rq[:, 2 * p + 1, :], 'rq')
                    for ab in range(2):
                        S.pool(lambda e, p=p, ab=ab: e.tensor_tensor(
                            out=rqd[:, 2 * p + ab, :], in0=rq[:, 2 * p + ab, :], in1=qdec[:, p * 512:(p + 1) * 512], op=ALU.mult),
                            reads=['rq', 'qdec'], writes=['rqd'])
                for h in range(4):
                    proj_fm(wr, 'wr', (8 + h) * 128, 2)
                    S.act(lambda e: e.activation(out=sgf[:], in_=ps[2][:], func=AF.Silu), reads=['ps2'], writes=['sgf'])
                    S.dve(lambda e, h=h: e.tensor_scalar(out=sg[:, h, :], in0=sgf[:], scalar1=rgain[:, h:h + 1], scalar2=None, op0=ALU.mult),
                          reads=['sgf', 'rgain'], writes=['sg'])
            for c in range(4):
                g = (ti - 12) * 4 + c
                cc = slice(c * 128, (c + 1) * 128)
                for idx in range(4):
                    S.pe(lambda e, idx=idx, cc=cc: e.transpose(out=psb[:, idx * 128:(idx + 1) * 128], in_=rk[:, idx, cc], identity=ident[:]),
                         reads=['rk', 'ident'], writes=['psb'])
                S.dve(lambda e, c=c: e.tensor_tensor(out=kdec4[:, c, :], in0=psb[:, 0:512], in1=kdectab[:], op=ALU.mult),
                      reads=['psb', 'kdectab'], writes=['kdec4_%d' % c])
                for p in range(2):
                    for ab in range(2):
                        S.pe(lambda e, p=p, ab=ab, c=c: e.matmul(ps[4 + p][:, ab * 256:(ab + 1) * 256], lhsT=kdec4[:, c, (2 * p + ab) * 128:(2 * p + ab + 1) * 128],
                                                                 rhs=rv[:, c, p * 256:(p + 1) * 256], start=True, stop=True),
                             reads=['kdec4_%d' % c, 'rv%d' % c], writes=['ps%d' % (4 + p)])
                for p in range(2):
                    S.dve(lambda e, p=p: e.scalar_tensor_tensor(out=Sst[p][:], in0=Sst[p][:], scalar=cdec[:, p:p + 1], in1=ps[4 + p][:],
                                                                op0=ALU.mult, op1=ALU.add), reads=['Sst%d' % p, 'cdec', 'ps%d' % (4 + p)], writes=['Sst%d' % p])
                    nv = (g + 1) % 5
                    S.act(lambda e, p=p, nv=nv: e.copy(out=Sring[p][nv][:], in_=Sst[p][:]), reads=['Sst%d' % p], writes=['Sr%d_%d' % (p, nv)])
            OST = int(os.environ.get("OWN_STAGE", "3"))
            for c in range(4 if OST >= 2 else 0):
                cc = slice(c * 128, (c + 1) * 128)
                gbs = [gbank(), gbank()]
                for h in range(4):
                    p, hl = h // 2, h % 2
                    bsl = slice(64 * hl, 64 * hl + 64)
                    gb = gbs[hl]
                    S.pe(lambda e, p=p, bsl=bsl, cc=cc, gb=gb, h=h: e.matmul(ps[gb][:, h * 128:(h + 1) * 128], lhsT=rk[bsl, 2 * p, cc], rhs=rq[bsl, 2 * p, cc], start=True, stop=False),
                         reads=['rk', 'rq'], writes=['ps%d' % gb])
                    S.pe(lambda e, p=p, bsl=bsl, cc=cc, gb=gb, h=h: e.matmul(ps[gb][:, h * 128:(h + 1) * 128], lhsT=rk[bsl, 2 * p + 1, cc], rhs=rq[bsl, 2 * p + 1, cc], start=False, stop=True),
                         reads=['rk', 'rq'], writes=['ps%d' % gb])
                for hl in range(2):
                    gb = gbs[hl]
                    v_ = lambda X: X.rearrange("p (a b i) -> p a b i", a=2, b=2)[:, :, hl, :]
                    S.dve(lambda e, c=c, gb=gb, v_=v_: e.tensor_tensor(out=v_(am4[:, c, :]), in0=v_(ps[gb][:]), in1=v_(dmask[:]), op=ALU.mult),
                          reads=['ps%d' % gb, 'dmask'], writes=['am%d_%d' % (c, hl)])
            for c in range(4 if OST >= 3 else 0):
                g = (ti - 12) * 4 + c
                cc = slice(c * 128, (c + 1) * 128)
                sv = g % 5
                otb = 6 if c % 2 == 0 else 4
                ot = ps[otb]
                for h in range(4):
                    p, hl = h // 2, h % 2
                    bsl = slice(64 * hl, 64 * hl + 64)
                    osl = ot[:, h * 128:(h + 1) * 128]
                    S.pe(lambda e, h=h, c=c, osl=osl: e.matmul(osl, lhsT=rv[:, c, h * 128:(h + 1) * 128], rhs=am4[:, c, h * 128:(h + 1) * 128], start=True, stop=False),
                         reads=['rv%d' % c, 'am%d_%d' % (c, hl)], writes=['ps%d' % otb])
                    for ab in range(2):
                        S.pe(lambda e, p=p, ab=ab, hl=hl, bsl=bsl, cc=cc, osl=osl, sv=sv: e.matmul(
                            osl, lhsT=Sring[p][sv][bsl, ab * 256 + hl * 128: ab * 256 + hl * 128 + 128], rhs=rqd[bsl, 2 * p + ab, cc],
                            start=False, stop=(ab == 1)), reads=['Sr%d_%d' % (p, sv), 'rqd'], writes=['ps%d' % otb])
                S.act(lambda e: e.activation(out=osq[:], in_=ot[:], func=AF.Square), reads=['ps%d' % otb], writes=['osq'])
                r = rstd_from_sq([osq[:]], ['osq'], 1, 1.0 / 128)
                S.dve(lambda e, r=r: e.tensor_tensor(out=on[:], in0=ot[:], in1=r[:], op=ALU.mult), reads=['ps%d' % otb, 'rstd1'], writes=['on'])
                oc = (ti - 12) * 512 + c * 128
                S.pool(lambda e, oc=oc, cc=cc: e.tensor_tensor(out=hret[:, :, oc:oc + 128], in0=on[:].rearrange("p (h i) -> p h i", h=4),
                                                               in1=sg[:, :, cc], op=ALU.mult), reads=['on', 'sg'], writes=[])
        for p in range(2):
            S.dma(lambda e, p=p: e.dma_start(out=sfin[:, p, :], in_=Sst[p][:]), reads=['Sst%d' % p])
        S.barrier()
        sb1 = seq_at(tk_off)
        rks = sb1("rks", [128, 4, 32], BF16)
        rqs = sb1("rqs", [128, 4, 32], BF16)
        rqds = sb1("rqds", [128, 4, 32], BF16)
        rvs = sb1("rvs", [128, 4, 512], BF16)
        sgs = sb1("sgs", [128, 4, 32], BF16)
        sgfs = sb1("sgfs", [128, 32], F32)
        kdecs = sb1("kdecs", [8, 512], BF16)
        ams = sb1("ams", [128, 32], BF16)
        S.pool(lambda e: e.memset(rvs[:], 0.0), writes=['rvs'])
        S.pool(lambda e: e.memset(ams[:], 0.0), writes=['ams'])
        S0f = [at("S0f0", [128, 1024], F32, hT1a_off)] * 2
        S1f = [at("S1f0", [128, 1024], F32, hT1a_off + 4 * KB)] * 2
        S0b = [sb1("S0b0", [128, 1024], BF16)] * 2
        osqs = sb1("osqs", [128, 128], BF16)
        ons = sb1("ons", [128, 128], F32)
        for kt in range(8):
            S.act(lambda e, kt=kt: e.activation(out=sq[:, kt, 0:32], in_=xs[:, kt, :], func=AF.Square), reads=['xs'], writes=['sq'])
        r = rstd_from_sq([sq[:, kt, 0:32] for kt in range(8)], ['sq'], 0, 1.0 / 1024, N=32)
        for kt in range(8):
            t = tmpf[kt % 2]
            S.dve(lambda e, t=t, kt=kt: e.tensor_tensor(out=t[:, 0:32], in0=xs[:, kt, :], in1=r[:, 0:32], op=ALU.mult),
                  reads=['xs', 'rstd0'], writes=['tmpf%d' % (kt % 2)])
            for s_ in range(4):
                S.act(lambda e, t=t, kt=kt, s_=s_: e.activation(out=hTs[:, kt, s_ * 8:s_ * 8 + 8], in_=t[:, s_ * 8:s_ * 8 + 8], func=AF.Identity,
                                                                bias=mod[:, SH1 + kt, 1 + s_:2 + s_], scale=gm1[:, kt, 1 + s_:2 + s_]),
                      reads=['tmpf%d' % (kt % 2), 'mod', 'gm1'], writes=['hTs'])
        for p in range(2):
            proj_fm_s(wr, 'wr', (4 + 2 * p) * 128, 0)
            proj_fm_s(wr, 'wr', (5 + 2 * p) * 128, 1)
            rotate(0, 1, None, 'tabS', rks[:, 2 * p, :], rks[:, 2 * p + 1, :], 'rks', N=32, cs=tabs(1))
            proj_fm_s(wr, 'wr', (0 + 2 * p) * 128, 0)
            proj_fm_s(wr, 'wr', (1 + 2 * p) * 128, 1)
            rotate(0, 1, None, 'tabS', rqs[:, 2 * p, :], rqs[:, 2 * p + 1, :], 'rqs', N=32, cs=tabs(0))
            for ab in range(2):
                S.pool(lambda e, p=p, ab=ab: e.tensor_tensor(out=rqds[:, 2 * p + ab, :], in0=rqs[:, 2 * p + ab, :],
                                                             in1=dec8[:, 544 + 32 * p:544 + 32 * p + 32], op=ALU.mult),
                       reads=['rqs', 'dec8'], writes=['rqds'])
        for h in range(4):
            proj_fm_s(wr, 'wr', (8 + h) * 128, 2)
            S.act(lambda e: e.activation(out=sgfs[:], in_=ps[2][:, 0:32], func=AF.Silu), reads=['ps2'], writes=['sgfs'])
            S.dve(lambda e, h=h: e.tensor_scalar(out=sgs[:, h, :], in0=sgfs[:], scalar1=rgain[:, h:h + 1], scalar2=None, op0=ALU.mult),
                  reads=['sgfs', 'rgain'], writes=['sgs'])
        for s_ in range(4):
            s8 = slice(s_ * 8, s_ * 8 + 8)
            for kt in range(8):
                S.pe(lambda e, kt=kt, s8=s8: e.matmul(ps[2][0:8, :], lhsT=hTs[:, kt, s8], rhs=wr[:, kt, 1536:2048], start=(kt == 0), stop=(kt == 7)),
                     reads=['wr', 'hTs'], writes=['ps2'])
            S.act(lambda e, s_=s_: e.copy(out=rvs[0:8, s_, :], in_=ps[2][0:8, :]), reads=['ps2'], writes=['rvs'])
        ot = ps[6]
        for s_ in range(4):
            s8 = slice(s_ * 8, s_ * 8 + 8)
            si = 0
            S.dma(lambda e, s_=s_, si=si: e.dma_start(out=S0f[si][:], in_=sinit[s_]), writes=['S0f%d' % si])
            S.pool(lambda e, si=si: e.tensor_copy(out=S0b[si][:], in_=S0f[si][:]), reads=['S0f%d' % si], writes=['S0b%d' % si])
            for idx in range(4):
                S.pe(lambda e, idx=idx, s8=s8: e.transpose(out=psb[0:8, idx * 128:(idx + 1) * 128], in_=rks[:, idx, s8], identity=ident[:]),
                     reads=['rks', 'ident'], writes=['psb'])
            S.dve(lambda e: e.tensor_tensor(out=kdecs[:], in0=psb[0:8, 0:512], in1=dec8[0:8, 0:512], op=ALU.mult),
                  reads=['psb', 'dec8'], writes=['kdecs'])
            for p in range(2):
                for ab in range(2):
                    S.pe(lambda e, p=p, ab=ab, s_=s_: e.matmul(ps[4 + p][:, ab * 256:(ab + 1) * 256], lhsT=kdecs[0:8, (2 * p + ab) * 128:(2 * p + ab + 1) * 128],
                                                               rhs=rvs[0:8, s_, p * 256:(p + 1) * 256], start=True, stop=True),
                         reads=['kdecs', 'rvs'], writes=['ps%d' % (4 + p)])
                S.dve(lambda e, p=p, si=si: e.scalar_tensor_tensor(out=S1f[si][:, p * 512:(p + 1) * 512], in0=S0f[si][:, p * 512:(p + 1) * 512],
                                                                  scalar=dec8[:, 608 + p:609 + p], in1=ps[4 + p][:], op0=ALU.mult, op1=ALU.add),
                      reads=['S0f%d' % si, 'dec8', 'ps%d' % (4 + p)], writes=['S1f%d' % si])
            S.dma(lambda e, s_=s_, si=si: e.dma_start(out=sstate[s_], in_=S1f[si][:]), reads=['S1f%d' % si])
            for h in range(4):
                p, hl = h // 2, h % 2
                bsl = slice(64 * hl, 64 * hl + 64)
                gb = gbank()
                S.pe(lambda e, p=p, bsl=bsl, s8=s8, gb=gb: e.matmul(ps[gb][0:8, 0:8], lhsT=rks[bsl, 2 * p, s8], rhs=rqs[bsl, 2 * p, s8], start=True, stop=False),
                     reads=['rks', 'rqs'], writes=['ps%d' % gb])
                S.pe(lambda e, p=p, bsl=bsl, s8=s8, gb=gb: e.matmul(ps[gb][0:8, 0:8], lhsT=rks[bsl, 2 * p + 1, s8], rhs=rqs[bsl, 2 * p + 1, s8], start=False, stop=True),
                     reads=['rks', 'rqs'], writes=['ps%d' % gb])
                S.dve(lambda e, h=h, gb=gb: e.tensor_tensor(out=ams[0:8, h * 8:h * 8 + 8], in0=ps[gb][0:8, 0:8], in1=dec8[0:8, 512 + h * 8:512 + h * 8 + 8], op=ALU.mult),
                      reads=['ps%d' % gb, 'dec8'], writes=['ams'])
                osl = ot[:, s_ * 32 + h * 8: s_ * 32 + h * 8 + 8]
                S.pe(lambda e, h=h, s_=s_, osl=osl: e.matmul(osl, lhsT=rvs[:, s_, h * 128:(h + 1) * 128], rhs=ams[:, h * 8:h * 8 + 8], start=True, stop=False),
                     reads=['rvs', 'ams'], writes=['ps6'])
                for ab in range(2):
                    S.pe(lambda e, p=p, ab=ab, hl=hl, bsl=bsl, s8=s8, osl=osl, si=si: e.matmul(
                        osl, lhsT=S0b[si][bsl, p * 512 + ab * 256 + hl * 128: p * 512 + ab * 256 + hl * 128 + 128], rhs=rqds[bsl, 2 * p + ab, s8],
                        start=False, stop=(ab == 1)), reads=['S0b%d' % si, 'rqds'], writes=['ps6'])
        S.act(lambda e: e.activation(out=osqs[:], in_=ot[:, 0:128], func=AF.Square), reads=['ps6'], writes=['osqs'])
        r = rstd_from_sq([osqs[:]], ['osqs'], 1, 1.0 / 128, N=128)
        S.dve(lambda e, r=r: e.tensor_tensor(out=ons[:], in0=ot[:, 0:128], in1=r[:, 0:128], op=ALU.mult), reads=['ps6', 'rstd1'], writes=['ons'])
        for s_ in range(4):
            S.pool(lambda e, s_=s_: e.tensor_tensor(out=hret_s[:, :, s_ * 8:s_ * 8 + 8], in0=ons[:, s_ * 32:(s_ + 1) * 32].rearrange("p (h l) -> p h l", h=4),
                                                    in1=sgs[:, :, s_ * 8:s_ * 8 + 8], op=ALU.mult), reads=['ons', 'sgs'], writes=['hret_s'])
    S.barrier()
    free_res = []

    if stop_after < 2:
        S.emit(st)
        return nc
    aq = at("aq", [128, 4, 2048], BF16, 98 * KB)
    ak = at("ak", [128, 4, 4096], BF16, 114 * KB)
    hatt = at("hatt", [64, 8, 2048], BF16, 175 * KB)
    if True:
        sb2 = seq_at(146 * KB)
        wt = sb2("wt", [128, 8, 1536], BF16)
        S.dma(lambda e: e.dma_start(out=wt[:, :, 0:768], in_=watt[:, :, 0:768]), writes=['wt'], q='pool')
        S.dma(lambda e: e.dma_start(out=wt[:, :, 768:1536], in_=watt[:, :, 768:1536]), writes=['wt'], q='pool')
        tak = [sb2("tak%d" % i, [128, 2, 512], F32) for i in range(2)]
        taq = [sb2("taq%d" % i, [128, 2, 512], F32) for i in range(2)]
        k32_0 = sb2("k32_0", [128, 4, 512], F32)
        hT1b_off = sb2.cur[0]
        hT1b = sb2("hT1b", [128, 8, 512], BF16)
        hTb[1] = hT1b
        k32 = [k32_0, k32_0]
        vst = [sb2("vst%d" % i, [128, 8, 65], F32) for i in range(2)]
        for i in range(2):
            S.pool(lambda e, i=i: e.memset(vst[i][:], 1.0), writes=['vst%d' % i])
        vdone = []
        r2 = [0]

        def rb2():
            r2[0] += 1
            return [0, 1, 2, 4, 5, 6][r2[0] % 6]
        load_x(4096, 0)
        load_x(4096 + 512, 1)
        prenorm(0, gm1, SH1)
        for ti in range(8):
            own = ti >= 4
            bi = ti % 2
            col0 = 4096 + ti * 512
            kc0 = ti * 512
            S.dma(lambda e, bi=bi, kc0=kc0: e.dma_start(out=tak[bi][:], in_=tabAK[:, :, kc0:kc0 + 512]), writes=['tak%d' % bi])
            if own:
                S.dma(lambda e, bi=bi, kc0=kc0: e.dma_start(out=taq[bi][:], in_=tabAQ[:, :, kc0 - 2048:kc0 - 2048 + 512]), writes=['taq%d' % bi])
            if ti + 1 < 8:
                prenorm((ti + 1) % 2, gm1, SH1)
            if ti + 2 < 8:
                load_x(4096 + (ti + 2) * 512, ti % 2)
            curh[0] = bi
            hT = hTb[bi]
            for p in range(2):
                bA, bB = rb2(), rb2()
                proj_fm(wt, 'wt', (4 + 2 * p) * 128, bA)
                proj_fm(wt, 'wt', (5 + 2 * p) * 128, bB)
                if own:
                    rotate(bA, bB, tak[bi], 'tak%d' % bi, ak[:, 2 * p, kc0:kc0 + 512], ak[:, 2 * p + 1, kc0:kc0 + 512], 'ak%d' % p,
                           k32[bi][:, 2 * p, :], k32[bi][:, 2 * p + 1, :], 'k32_0')
                else:
                    rotate(bA, bB, tak[bi], 'tak%d' % bi, ak[:, 2 * p, kc0:kc0 + 512], ak[:, 2 * p + 1, kc0:kc0 + 512], 'ak%d' % p)
            if own:
                oc0 = kc0 - 2048
                S.dma(lambda e, bi=bi, oc0=oc0: e.dma_start(out=kTo[:, :, oc0:oc0 + 512], in_=k32[bi][:]), reads=['k32_0'])
                for p in range(2):
                    bA, bB = rb2(), rb2()
                    proj_fm(wt, 'wt', (0 + 2 * p) * 128, bA)
                    proj_fm(wt, 'wt', (1 + 2 * p) * 128, bB)
                    rotate(bA, bB, taq[bi], 'taq%d' % bi, aq[:, 2 * p, oc0:oc0 + 512], aq[:, 2 * p + 1, oc0:oc0 + 512], 'aq%d' % p)
            for c in range(4):
                vi = (ti * 4 + c) % 2
                vb = rb2()
                for kt in range(8):
                    S.pe(lambda e, kt=kt, c=c: e.matmul(ps[vb][:], lhsT=hT[:, kt, c * 128:(c + 1) * 128], rhs=wt[:, kt, 1024:1536],
                                                        start=(kt == 0), stop=(kt == 7)), reads=['wt', 'hT%d' % bi], writes=['ps%d' % vb])
                S.act(lambda e, vi=vi: e.copy(out=vst[vi][:, :, 0:64], in_=ps[vb][:].rearrange("p (h d) -> p h d", h=8)),
                      reads=['ps%d' % vb], writes=['vst%d' % vi])
                r0 = kc0 + c * 128
                vdone.append(S.dma(lambda e, vi=vi, r0=r0: e.dma_start(out=vnat[r0:r0 + 128, :, :], in_=vst[vi][:]),
                                   reads=['vst%d' % vi], writes=['vnat']))
    if True:
        S.barrier()
        vsn32 = at("vsn32", [8, 4, 512], F32, hT1b_off)
        k32 = [k32_0, vsn32]
        for p in range(2):
            proj_fm_s(wt, 'wt', (4 + 2 * p) * 128, 0)
            proj_fm_s(wt, 'wt', (5 + 2 * p) * 128, 1)
            rotate(0, 1, None, 'tabS', aks[:, 2 * p, :], aks[:, 2 * p + 1, :], 'aks',
                   k32[0][:, 2 * p, 0:32], k32[0][:, 2 * p + 1, 0:32], 'k32_0', N=32, cs=tabs(3))
            proj_fm_s(wt, 'wt', (0 + 2 * p) * 128, 0)
            proj_fm_s(wt, 'wt', (1 + 2 * p) * 128, 1)
            rotate(0, 1, None, 'tabS', aqs[:, 2 * p, :], aqs[:, 2 * p + 1, :], 'aqs', N=32, cs=tabs(2))
        S.dma(lambda e: e.dma_start(out=ksnew, in_=k32[0][:, :, 0:32]), reads=['k32_0'])
        for s_ in range(4):
            s8 = slice(s_ * 8, s_ * 8 + 8)
            for kt in range(8):
                S.pe(lambda e, kt=kt, s8=s8: e.matmul(ps[2][0:8, :], lhsT=hTs[:, kt, s8], rhs=wt[:, kt, 1024:1536], start=(kt == 0), stop=(kt == 7)),
                     reads=['wt', 'hTs'], writes=['ps2'])
            S.act(lambda e, s_=s_: e.copy(out=vsn32[0:8, s_, :], in_=ps[2][0:8, :]), reads=['ps2'], writes=['k32_1'])
        S.dma(lambda e: e.dma_start(out=vsnew, in_=vsn32[0:8, :, :]), reads=['k32_1'], writes=['vsnew'])
    S.barrier()

    if stop_after < 2.05:
        S.emit(st)
        return nc
    if True:
        V1 = at("V1", [128, 17, 8 * 65], BF16, 64 * KB)
        V4 = at("V4", [128, 20, 8 * 65], BF16, 146 * KB)
        V16 = at("V16", [128, 32, 8 * 65], BF16, 30 * KB)
        vflat = vnat.rearrange("t h d -> t (h d)")
        S.dma(lambda e: e.dma_start(out=V1[:], in_=vflat[1920:4096, :].rearrange("(a p) f -> p a f", p=128)),
              reads=['vnat'], writes=['V1'], q='pool')
        S.dma(lambda e: e.dma_start(out=V4[:].rearrange("p (g r) f -> p g r f", r=4),
                                    in_=vflat[1536:4096, :].rearrange("(g p r) f -> p g r f", p=128, r=4)),
              reads=['vnat'], writes=['V4'], q='pool')
        S.dma(lambda e: e.dma_start(out=V16[:].rearrange("p (g r) f -> p g r f", r=16),
                                    in_=vflat[0:4096, :].rearrange("(g p r) f -> p g r f", p=128, r=16)),
              reads=['vnat'], writes=['V16'], q='pool')
        if stop_after == 2.1:
            S.emit(st)
            return nc
        uacc = [at("uacc0", [65, 2048], F32, 22 * KB)] * 2
        pT = [at("pT%d" % i, [128, 256], BF16, 18 * KB + 512 * i) for i in range(8)]
        pE = [at("pE%d" % i, [128, 256], BF16, 167 * KB + 512 * i) for i in range(4)]
        pctr = [0]
        SB = [0, 1, 2, 3]
        OB = [4, 5]
        sctr = [0]
        octr = [0]
        LAG = 3
        obst = {'ob': OB[0]}
        for h in range(8):
            if h in (1, 2, 3, 4):
                s_ = h - 1
                S.dma(lambda e, s_=s_: e.dma_start(out=kcs[s_], in_=kcn[s_, 8:2048, :]), reads=['uarow'])
                S.dma(lambda e, s_=s_: e.dma_start(out=vcs[s_], in_=vcn[s_, 8:2048, :]), reads=['uarow'])
            p, q4 = h // 4, h % 4
            hs = slice(32 * q4, 32 * q4 + 32)
            tp = (32 * q4, 0)
            ua = uacc[0]
            allua = ['ua1_%d' % g for g in range(4)] + ['ua4_%d' % r_ for r_ in range(4)] + ['ua16_%d' % r_ for r_ in range(16)]
            pending = []

            def emit_pv(D, Vt, Vres, r, kb, vt, nb, pt, pi, prev_p, h=h, ua=ua):
                qb = kb
                if qb % 4 == 0 or D == 16:
                    obst['ob'] = OB[octr[0] % 2]
                    octr[0] += 1
                ob = obst['ob']
                osl = ps[ob][0:65, qb % 4 * 128: qb % 4 * 128 + 128] if D != 16 else ps[ob][0:65, 0:128]
                ppt, ppi, pkb = prev_p
                pvt = vt - (1 if D == 1 else D)
                poff = 0 if pkb == -1 else 128
                S.pe(lambda e: e.matmul(osl, lhsT=Vt[:, pvt, h * 65:(h + 1) * 65], rhs=ppt[:, poff:poff + 128], start=True, stop=False),
                     reads=[Vres, 'pT%d' % ppi], writes=['ps%d' % ob])
                S.pe(lambda e: e.matmul(osl, lhsT=Vt[:, vt, h * 65:(h + 1) * 65], rhs=pt[:, 0:128], start=False, stop=True),
                     reads=[Vres, 'pT%d' % pi], writes=['ps%d' % ob])
                if (qb % 4 == 3) or D == 16:
                    if D == 1:
                        g = qb // 4
                        dst = ua[:, g * 512:g * 512 + 512]
                        src = ps[ob][0:65, 0:512]
                        S.dve(lambda e: e.tensor_copy(out=dst, in_=src), reads=['ps%d' % ob], writes=['ua1_%d' % g])
                    elif D == 4:
                        dst = ua[:, r:2048:4]
                        src = ps[ob][0:65, 0:512]
                        S.dve(lambda e: e.tensor_tensor(out=dst, in0=src, in1=dst, op=ALU.add),
                              reads=['ps%d' % ob] + ['ua1_%d' % g for g in range(4)], writes=['ua4_%d' % r])
                    else:
                        dst = ua[:, r:2048:16]
                        src = ps[ob][0:65, 0:128]
                        S.dve(lambda e: e.tensor_tensor(out=dst, in0=src, in1=dst, op=ALU.add),
                              reads=['ps%d' % ob, 'ua4_%d' % (r % 4)], writes=['ua16_%d' % r])

            for (D, Vt, Vres) in [(1, V1, 'V1'), (4, V4, 'V4'), (16, V16, 'V16')]:
                nb = 16 // D
                for r in range(D):
                    prev_p = None
                    for kb in range(-1, nb):
                        kcol0 = 2048 + r + D * 128 * kb
                        kcols = slice(kcol0, kcol0 + 127 * D + 1, D)
                        if D == 1:
                            vt = kb + 1
                        elif D == 4:
                            vt = (kb + 1) * 4 + r
                        else:
                            vt = (kb + 1) * 16 + r
                        qb0 = max(kb, 0)
                        qb1 = min(kb + 1, nb - 1)
                        nq = (qb1 - qb0 + 1) * 128
                        qcol0 = r + D * 128 * qb0
                        qcols = slice(qcol0, qcol0 + (nq - 1) * D + 1, D)
                        if kb == -1:
                            mk = mprev[:, 0:128]
                        elif kb == nb - 1:
                            mk = mmain[:, 0:128]
                        else:
                            mk = mmain[:, 0:256]
                        sbk = SB[sctr[0] % 4]
                        sctr[0] += 1
                        sp_ = ps[sbk][:, 0:nq]
                        sres = 'ps%d' % sbk
                        S.pe(lambda e: e.matmul(sp_, lhsT=ak[hs, 2 * p, kcols], rhs=aq[hs, 2 * p, qcols], start=True, stop=False, tile_position=tp),
                             reads=[], writes=[sres])
                        S.pe(lambda e: e.matmul(sp_, lhsT=ak[hs, 2 * p + 1, kcols], rhs=aq[hs, 2 * p + 1, qcols], start=False, stop=True, tile_position=tp),
                             reads=[], writes=[sres])
                        pi = pctr[0] % 8
                        pctr[0] += 1
                        pt = pT[pi]
                        pe_ = pE[pi % 4]
                        S.act(lambda e: e.activation(out=pe_[:, 0:nq], in_=sp_, func=AF.Exp), reads=[sres], writes=['pE%d' % (pi % 4)])
                        S.pool(lambda e: e.tensor_tensor(out=pt[:, 0:nq], in0=pe_[:, 0:nq], in1=mk, op=ALU.mult),
                               reads=['pE%d' % (pi % 4), 'mmain', 'mprev'], writes=['pT%d' % pi])
                        if kb >= 0:
                            pending.append((D, Vt, Vres, r, kb, vt, nb, pt, pi, prev_p))
                        prev_p = (pt, pi, kb)
                        while len(pending) > LAG:
                            emit_pv(*pending.pop(0))
            while pending:
                emit_pv(*pending.pop(0))
            S.act(lambda e: e.activation(out=ua[64:65, :], in_=ua[64:65, :], func=AF.Ln), reads=allua, writes=['uarow'])
            S.act(lambda e: e.activation(out=ua[64:65, :], in_=ua[64:65, :], func=AF.Exp, scale=-1.0), reads=['uarow'], writes=['uarow'])
            for tt in range(4):
                S.pe(lambda e: e.matmul(ps[6][0:64, :], lhsT=self32[0:65, :], rhs=ua[0:65, tt * 512:(tt + 1) * 512], start=True, stop=True),
                     reads=allua + ['uarow', 'sel'], writes=['ps6'])
                S.dve(lambda e: e.tensor_tensor(out=hatt[:, h, tt * 512:(tt + 1) * 512], in0=ua[0:64, tt * 512:(tt + 1) * 512], in1=ps[6][0:64, :], op=ALU.mult),
                      reads=allua + ['uarow', 'ps6'], writes=[])
    S.barrier()
    if debug and os.environ.get("DUMP", "1") == "1":
        dstg = [at("dstg%d" % i, [128, 2048], F32, (30 + 8 * i) * KB) for i in range(2)]
        for i in range(12):
            d_ = dstg[i % 2]
            if i < 4:
                S.act(lambda e: e.copy(out=d_[:], in_=hret[:, i, :]), reads=['hret'], writes=['dstg%d' % (i % 2)])
                S.dma(lambda e: e.dma_start(out=dbg_hret[:, i, :], in_=d_[:]), reads=['dstg%d' % (i % 2)])
            else:
                S.act(lambda e: e.copy(out=d_[0:64, :], in_=hatt[:, i - 4, :]), reads=['hatt'], writes=['dstg%d' % (i % 2)])
                S.dma(lambda e: e.dma_start(out=dbg_hatt[:, i - 4, :], in_=d_[0:64, :]), reads=['dstg%d' % (i % 2)])
        S.barrier()
    if True:
        akc = [at("akc%d" % i, [128, 4, 2176], BF16, (98 + 17 * i) * KB) for i in range(2)]
        Vs = [at("Vs%d" % i, [128, 17, 512], BF16, (132 + 17 * i) * KB) for i in range(2)]
        mmult = at("mmult_s", [128, 136], F32, 18 * KB)
        pes = at("pes", [128, 136], F32, 19 * KB)
        pTs = [at("pTs%d" % i, [128, 136], BF16, 20 * KB + 512 * i) for i in range(2)]
        rden = at("rden", [64, 8], F32, 22 * KB)
        S.dma(lambda e: e.dma_start(out=mmult[:], in_=mmult_d), writes=['mmult'])
        for i in range(2):
            S.pool(lambda e, i=i: e.memset(akc[i][:, :, 2048:2176], 0.0), writes=['akc%d' % i])
            S.pool(lambda e, i=i: e.memset(Vs[i][:, 16, :], 0.0), writes=['Vs%d' % i])
        SBs = [0, 1, 2, 3]
        OBs = [4, 5]
        sc2 = 0
        for s_ in range(4):
            bi = s_ % 2
            s8 = slice(s_ * 8, s_ * 8 + 8)
            S.dma(lambda e, s_=s_, bi=bi: e.dma_start(out=akc[bi][:, :, 0:2048], in_=kcT[s_]), writes=['akc%d' % bi], q='pool')
            S.pool(lambda e, bi=bi, s8=s8: e.tensor_copy(out=akc[bi][:, :, 2048:2056], in_=aks[:, :, s8]), reads=['aks'], writes=['akc%d' % bi])
            S.dma(lambda e, s_=s_, bi=bi: e.dma_start(out=Vs[bi][:, 0:16, :], in_=vcn[s_].rearrange("(a p) f -> p a f", p=128)), writes=['Vs%d' % bi], q='pool')
            S.dma(lambda e, s_=s_, bi=bi: e.dma_start(out=Vs[bi][0:8, 16, :], in_=vsnew[:, s_, :]), reads=['vsnew'], writes=['Vs%d' % bi], q='pool')
            for h in range(8):
                p, q4 = h // 4, h % 4
                hs = slice(32 * q4, 32 * q4 + 32)
                tp = (32 * q4, 0)
                sbk = SBs[sc2 % 4]
                ob = OBs[sc2 % 2]
                pk = sc2 % 2
                sc2 += 1
                for tl in range(17):
                    for ab in range(2):
                        S.pe(lambda e, tl=tl, ab=ab: e.matmul(ps[sbk][:, tl * 8:tl * 8 + 8], lhsT=akc[bi][hs, 2 * p + ab, tl * 128:(tl + 1) * 128],
                                                              rhs=aqs[hs, 2 * p + ab, s8], start=(ab == 0), stop=(ab == 1), tile_position=tp),
                             reads=['akc%d' % bi, 'aqs'], writes=['ps%d' % sbk])
                S.act(lambda e: e.activation(out=pes[:], in_=ps[sbk][:, 0:136], func=AF.Exp), reads=['ps%d' % sbk], writes=['pes'])
                S.dve(lambda e: e.tensor_tensor(out=pTs[pk][:], in0=pes[:], in1=mmult[:], op=ALU.mult), reads=['pes', 'mmult'], writes=['pTs%d' % pk])
                for tl in range(17):
                    S.pe(lambda e, tl=tl: e.matmul(ps[ob][0:64, 0:8], lhsT=Vs[bi][:, tl, h * 64:(h + 1) * 64], rhs=pTs[pk][:, tl * 8:tl * 8 + 8],
                                                   start=(tl == 0), stop=(tl == 16)), reads=['Vs%d' % bi, 'pTs%d' % pk], writes=['ps%d' % ob])
                for tl in range(17):
                    S.pe(lambda e, tl=tl: e.matmul(ps[ob][0:64, 8:16], lhsT=ones[:, 0:64], rhs=pTs[pk][:, tl * 8:tl * 8 + 8],
                                                   start=(tl == 0), stop=(tl == 16)), reads=['ones', 'pTs%d' % pk], writes=['ps%d' % ob])
                S.dve(lambda e: e.reciprocal(out=rden[:], in_=ps[ob][0:64, 8:16]), reads=['ps%d' % ob], writes=['rden'])
                S.dve(lambda e: e.tensor_tensor(out=hatt_s[:, h, s8], in0=ps[ob][0:64, 0:8], in1=rden[:], op=ALU.mult),
                      reads=['ps%d' % ob, 'rden'], writes=['hatt_s'])
    S.barrier()
    if stop_after < 4:
        S.emit(st)
        return nc
    x1 = at("x1", [128, 8, 2048], F32, 98 * KB)
    if True:
        woRs = at("woRs", [128, 4, 1024], BF16, 162 * KB)
        woAs = at("woAs", [64, 8, 1024], BF16, 62 * KB)
        sq = at("sq2", [128, 8, 512], BF16, 22 * KB)
        S.dma(lambda e: e.dma_start(out=woRs[:], in_=woR), writes=['woRs'], q='pool')
        S.dma(lambda e: e.dma_start(out=woAs[:], in_=woA), writes=['woAs'], q='pool')
        load_x(6144, 0)
        for ti in range(4):
            bi = ti % 2
            tc0 = ti * 512
            if ti + 1 < 4:
                load_x(6144 + (ti + 1) * 512, (ti + 1) % 2)
            for dt_ in range(8):
                gb = gbank()
                dc = slice(dt_ * 128, (dt_ + 1) * 128)
                for hh in range(4):
                    S.pe(lambda e, gb=gb, hh=hh, dc=dc: e.matmul(ps[gb][:], lhsT=woRs[:, hh, dc], rhs=hret[:, hh, tc0:tc0 + 512], start=(hh == 0), stop=False),
                         reads=['woRs', 'hret'], writes=['ps%d' % gb])
                for hh in range(8):
                    S.pe(lambda e, gb=gb, hh=hh, dc=dc: e.matmul(ps[gb][:], lhsT=woAs[0:64, hh, dc], rhs=hatt[0:64, hh, tc0:tc0 + 512], start=False, stop=(hh == 7)),
                         reads=['woAs', 'hatt'], writes=['ps%d' % gb])
                S.act(lambda e, gb=gb, dt_=dt_: e.copy(out=x1[:, dt_, tc0:tc0 + 512], in_=ps[gb][:]), reads=['ps%d' % gb], writes=['x1'])
                S.act(lambda e, gb=gb, dt_=dt_: e.activation(out=sq[:, dt_, :], in_=ps[gb][:], func=AF.Square), reads=['ps%d' % gb], writes=['sq'])
            r = rstd_from_sq([sq[:, kt, :] for kt in range(8)], ['sq'], 0, 1.0 / 1024)
            for dt_ in range(8):
                t = tmpf[dt_ % 2]
                S.dve(lambda e, t=t, dt_=dt_: e.scalar_tensor_tensor(out=t[:], in0=x1[:, dt_, tc0:tc0 + 512], scalar=gg1[:, dt_, 0:1], in1=r[:], op0=ALU.mult, op1=ALU.mult),
                      reads=['x1', 'gg1', 'rstd0'], writes=['tmpf%d' % (dt_ % 2)])
                S.pool(lambda e, t=t, dt_=dt_: e.tensor_tensor(out=x1[:, dt_, tc0:tc0 + 512], in0=t[:], in1=xt[bi][:, dt_, :], op=ALU.add),
                       reads=['tmpf%d' % (dt_ % 2), 'xt%d' % bi], writes=['x1'])
    if True:
        for dt_ in range(8):
            gb = gbank()
            dc = slice(dt_ * 128, (dt_ + 1) * 128)
            for hh in range(4):
                S.pe(lambda e, gb=gb, hh=hh, dc=dc: e.matmul(ps[gb][:, 0:32], lhsT=woRs[:, hh, dc], rhs=hret_s[:, hh, :], start=(hh == 0), stop=False),
                     reads=['woRs', 'hret_s'], writes=['ps%d' % gb])
            for hh in range(8):
                S.pe(lambda e, gb=gb, hh=hh, dc=dc: e.matmul(ps[gb][:, 0:32], lhsT=woAs[0:64, hh, dc], rhs=hatt_s[0:64, hh, :], start=False, stop=(hh == 7)),
                     reads=['woAs', 'hatt_s'], writes=['ps%d' % gb])
            S.act(lambda e, gb=gb, dt_=dt_: e.copy(out=x1s[:, dt_, :], in_=ps[gb][:, 0:32]), reads=['ps%d' % gb], writes=['x1s'])
            S.act(lambda e, gb=gb, dt_=dt_: e.activation(out=sq[:, dt_, 0:32], in_=ps[gb][:, 0:32], func=AF.Square), reads=['ps%d' % gb], writes=['sq'])
        r = rstd_from_sq([sq[:, kt, 0:32] for kt in range(8)], ['sq'], 0, 1.0 / 1024, N=32)
        for dt_ in range(8):
            t = tmpf[dt_ % 2]
            for s_ in range(4):
                s8 = slice(s_ * 8, s_ * 8 + 8)
                S.dve(lambda e, t=t, dt_=dt_, s_=s_, s8=s8: e.scalar_tensor_tensor(out=t[:, s8], in0=x1s[:, dt_, s8], scalar=gg1[:, dt_, 1 + s_:2 + s_], in1=r[:, s8],
                                                                                 op0=ALU.mult, op1=ALU.mult), reads=['x1s', 'gg1', 'rstd0'], writes=['tmpf%d' % (dt_ % 2)])
            S.pool(lambda e, t=t, dt_=dt_: e.tensor_tensor(out=x1s[:, dt_, :], in0=t[:, 0:32], in1=xs[:, dt_, :], op=ALU.add),
                   reads=['tmpf%d' % (dt_ % 2), 'xs'], writes=['x1s'])
    if debug:
        S.dma(lambda e: e.dma_start(out=dbg_x1, in_=x1[:]), reads=['x1'])
    S.barrier()
    if stop_after < 5:
        S.emit(st)
        return nc
    if True:
        h2 = at("h2", [128, 8, 1056], BF16, 162 * KB)
        hff = at("hff", [128, 16, 1056], BF16, 30 * KB)
        fbuf = at("fbuf", [128, 8, 1056], F32, 63 * KB)
        wu = [at("wu%d" % i, [128, 8, 512], BF16, (179 + 8 * i) * KB) for i in range(2)]
        wd = [at("wd%d" % i, [128, 16, 128], BF16, (195 + 4 * i) * KB) for i in range(2)]
        rl = [at("rl%d" % i, [128, 512], BF16, (96 + i) * KB) for i in range(2)]
        sq = at("sq3", [128, 8, 512], BF16, 22 * KB)
        uctr = [0]
        dctr = [0]
        def c2_prenorm(T):
            t0 = T * 1024
            for hh in range(2):
                cs = slice(t0 + hh * 512, t0 + hh * 512 + 512)
                hs_ = slice(hh * 512, hh * 512 + 512)
                for kt in range(8):
                    S.act(lambda e, kt=kt, cs=cs: e.activation(out=sq[:, kt, :], in_=x1[:, kt, cs], func=AF.Square), reads=['x1'], writes=['sq'])
                r = rstd_from_sq([sq[:, kt, :] for kt in range(8)], ['sq'], 0, 1.0 / 1024)
                for kt in range(8):
                    t = tmpf[kt % 2]
                    S.dve(lambda e, t=t, kt=kt, cs=cs: e.tensor_tensor(out=t[:], in0=x1[:, kt, cs], in1=r[:], op=ALU.mult),
                          reads=['x1', 'rstd0'], writes=['tmpf%d' % (kt % 2)])
                    S.act(lambda e, t=t, kt=kt, hs_=hs_: e.activation(out=h2[:, kt, hs_], in_=t[:], func=AF.Identity,
                                                                      bias=mod[:, SH2 + kt, 0:1], scale=gm2[:, kt, 0:1]),
                          reads=['tmpf%d' % (kt % 2), 'mod', 'gm2'], writes=['h2'])
            if T == 0:
                for kt in range(8):
                    S.act(lambda e, kt=kt: e.activation(out=sq[:, kt, 0:32], in_=x1s[:, kt, :], func=AF.Square), reads=['x1s'], writes=['sq'])
                r = rstd_from_sq([sq[:, kt, 0:32] for kt in range(8)], ['sq'], 1, 1.0 / 1024, N=32)
                for kt in range(8):
                    t = tmpf[kt % 2]
                    S.dve(lambda e, t=t, kt=kt: e.tensor_tensor(out=t[:, 0:32], in0=x1s[:, kt, :], in1=r[:, 0:32], op=ALU.mult),
                          reads=['x1s', 'rstd1'], writes=['tmpf%d' % (kt % 2)])
                    for s_ in range(4):
                        S.act(lambda e, t=t, kt=kt, s_=s_: e.activation(out=h2[:, kt, 1024 + s_ * 8:1024 + s_ * 8 + 8], in_=t[:, s_ * 8:s_ * 8 + 8], func=AF.Identity,
                                                                        bias=mod[:, SH2 + kt, 1 + s_:2 + s_], scale=gm2[:, kt, 1 + s_:2 + s_]),
                              reads=['tmpf%d' % (kt % 2), 'mod', 'gm2'], writes=['h2'])
        def c2_up(T, fh):
            for fc in range(4):
                wi = uctr[0] % 2
                uctr[0] += 1
                f0 = fh * 2048 + fc * 512
                S.dma(lambda e, wi=wi, f0=f0: e.dma_start(out=wu[wi][:], in_=wup[:, :, f0:f0 + 512]), writes=['wu%d' % wi], q='pool')
                for ft in range(4):
                    fi = fc * 4 + ft
                    for hh in range(2):
                        gb = gbank()
                        for kt in range(8):
                            S.pe(lambda e, gb=gb, wi=wi, ft=ft, kt=kt, hh=hh: e.matmul(ps[gb][:], lhsT=wu[wi][:, kt, ft * 128:(ft + 1) * 128], rhs=h2[:, kt, hh * 512:(hh + 1) * 512],
                                                                             start=(kt == 0), stop=(kt == 7)), reads=['wu%d' % wi, 'h2'], writes=['ps%d' % gb])
                        ri = (fi * 2 + hh) % 2
                        S.act(lambda e, gb=gb, ri=ri: e.activation(out=rl[ri][:], in_=ps[gb][:], func=AF.Relu), reads=['ps%d' % gb], writes=['rl%d' % ri])
                        S.dve(lambda e, ri=ri, fi=fi, hh=hh: e.tensor_tensor(out=hff[:, fi, hh * 512:(hh + 1) * 512], in0=rl[ri][:], in1=rl[ri][:], op=ALU.mult),
                               reads=['rl%d' % ri], writes=['hff'])
                    if T == 0:
                        gb = gbank()
                        for kt in range(8):
                            S.pe(lambda e, gb=gb, wi=wi, ft=ft, kt=kt: e.matmul(ps[gb][:, 0:32], lhsT=wu[wi][:, kt, ft * 128:(ft + 1) * 128], rhs=h2[:, kt, 1024:1056],
                                                                             start=(kt == 0), stop=(kt == 7)), reads=['wu%d' % wi, 'h2'], writes=['ps%d' % gb])
                        ri = fi % 2
                        S.act(lambda e, gb=gb, ri=ri: e.activation(out=rl[ri][:, 0:32], in_=ps[gb][:, 0:32], func=AF.Relu), reads=['ps%d' % gb], writes=['rl%d' % ri])
                        S.dve(lambda e, ri=ri, fi=fi: e.tensor_tensor(out=hff[:, fi, 1024:1056], in0=rl[ri][:, 0:32], in1=rl[ri][:, 0:32], op=ALU.mult),
                               reads=['rl%d' % ri], writes=['hff'])
        def c2_down(T, fh):
            for dt_ in range(8):
                wi = dctr[0] % 2
                dctr[0] += 1
                S.dma(lambda e, wi=wi, dt_=dt_, fh=fh: e.dma_start(out=wd[wi][:], in_=wdn[dt_, :, fh * 16:(fh + 1) * 16, :]),
                      writes=['wd%d' % wi], q='pool')
                for hh in range(2):
                    gb = gbank()
                    for ft in range(16):
                        S.pe(lambda e, gb=gb, wi=wi, ft=ft, hh=hh: e.matmul(ps[gb][:], lhsT=wd[wi][:, ft, :], rhs=hff[:, ft, hh * 512:(hh + 1) * 512],
                                                                         start=(ft == 0), stop=(ft == 15)), reads=['wd%d' % wi, 'hff'], writes=['ps%d' % gb])
                    fs = fbuf[:, dt_, hh * 512:(hh + 1) * 512]
                    if fh == 0:
                        S.act(lambda e, gb=gb, fs=fs: e.copy(out=fs, in_=ps[gb][:]), reads=['ps%d' % gb], writes=['fbuf'])
                    else:
                        S.dve(lambda e, gb=gb, fs=fs: e.tensor_tensor(out=fs, in0=ps[gb][:], in1=fs, op=ALU.add), reads=['ps%d' % gb, 'fbuf'], writes=['fbuf'])
                if T == 0:
                    gb = gbank()
                    for ft in range(16):
                        S.pe(lambda e, gb=gb, wi=wi, ft=ft: e.matmul(ps[gb][:, 0:32], lhsT=wd[wi][:, ft, :], rhs=hff[:, ft, 1024:1056],
                                                                  start=(ft == 0), stop=(ft == 15)), reads=['wd%d' % wi, 'hff'], writes=['ps%d' % gb])
                    fs = fbuf[:, dt_, 1024:1056]
                    if fh == 0:
                        S.act(lambda e, gb=gb, fs=fs: e.copy(out=fs, in_=ps[gb][:, 0:32]), reads=['ps%d' % gb], writes=['fbuf'])
                    else:
                        S.dve(lambda e, gb=gb, fs=fs: e.tensor_tensor(out=fs, in0=ps[gb][:, 0:32], in1=fs, op=ALU.add), reads=['ps%d' % gb, 'fbuf'], writes=['fbuf'])
        def c2_post(T):
            t0 = T * 1024
            for hh in range(2):
                cs = slice(t0 + hh * 512, t0 + hh * 512 + 512)
                hs_ = slice(hh * 512, hh * 512 + 512)
                for kt in range(8):
                    S.act(lambda e, kt=kt, hs_=hs_: e.activation(out=sq[:, kt, :], in_=fbuf[:, kt, hs_], func=AF.Square), reads=['fbuf'], writes=['sq'])
                r = rstd_from_sq([sq[:, kt, :] for kt in range(8)], ['sq'], 0, 1.0 / 1024)
                for kt in range(8):
                    t = tmpf[kt % 2]
                    S.dve(lambda e, t=t, kt=kt, hs_=hs_: e.scalar_tensor_tensor(out=t[:], in0=fbuf[:, kt, hs_], scalar=gg2[:, kt, 0:1], in1=r[:], op0=ALU.mult, op1=ALU.mult),
                          reads=['fbuf', 'gg2', 'rstd0'], writes=['tmpf%d' % (kt % 2)])
                    S.pool(lambda e, t=t, kt=kt, cs=cs: e.tensor_tensor(out=x1[:, kt, cs], in0=t[:], in1=x1[:, kt, cs], op=ALU.add),
                           reads=['tmpf%d' % (kt % 2), 'x1'], writes=['x1'])
                S.dma(lambda e, cs=cs: e.dma_start(out=yT[:, :, cs], in_=x1[:, :, cs]), reads=['x1'])
            if T == 0:
                for kt in range(8):
                    S.act(lambda e, kt=kt: e.activation(out=sq[:, kt, 0:32], in_=fbuf[:, kt, 1024:1056], func=AF.Square), reads=['fbuf'], writes=['sq'])
                r = rstd_from_sq([sq[:, kt, 0:32] for kt in range(8)], ['sq'], 1, 1.0 / 1024, N=32)
                for kt in range(8):
                    t = tmpf[kt % 2]
                    for s_ in range(4):
                        s8 = slice(s_ * 8, s_ * 8 + 8)
                        S.dve(lambda e, t=t, kt=kt, s_=s_, s8=s8: e.scalar_tensor_tensor(out=t[:, s8], in0=fbuf[:, kt, 1024 + s_ * 8:1024 + s_ * 8 + 8], scalar=gg2[:, kt, 1 + s_:2 + s_],
                                                                                       in1=r[:, s8], op0=ALU.mult, op1=ALU.mult), reads=['fbuf', 'gg2', 'rstd1'], writes=['tmpf%d' % (kt % 2)])
                    S.pool(lambda e, t=t, kt=kt: e.tensor_tensor(out=x1s[:, kt, :], in0=t[:, 0:32], in1=x1s[:, kt, :], op=ALU.add),
                           reads=['tmpf%d' % (kt % 2), 'x1s'], writes=['x1s'])
                S.dma(lambda e: e.dma_start(out=ysT, in_=x1s[:]), reads=['x1s'])

        c2_prenorm(0)
        c2_up(0, 0)
        c2_down(0, 0)
        c2_up(0, 1)
        c2_prenorm(1)
        c2_down(0, 1)
        c2_up(1, 0)
        c2_post(0)
        c2_down(1, 0)
        c2_up(1, 1)
        c2_down(1, 1)
        c2_post(1)
    S.emit(st)
    return nc


def _pm(w, ncols):
    return np.ascontiguousarray(w.reshape(8, 128, ncols).transpose(1, 0, 2))


def _const_tables(j):
    f32 = np.float32
    inv_r = (1.0 / (f32(10000.0) ** np.linspace(0.0, 1.0, 64, dtype=f32))).astype(f32)
    inv_a = (1.0 / (f32(10000.0) ** (np.arange(0, 64, 2, dtype=f32) / f32(64)))).astype(f32)

    def tab(pos, inv_p, scale):
        ang = (pos.astype(f32)[None, :] * inv_p.astype(f32)[:, None]).astype(np.float64)
        t = np.stack([np.cos(ang), np.sin(ang)], axis=1) * scale
        return np.ascontiguousarray(t.astype(f32))

    t = np.arange(2048)
    pos_all = np.concatenate([(j - 3 + s) * 2048 + t for s in range(4)])
    pos_all = np.maximum(pos_all, 0)
    inv_r_p = inv_r[np.arange(128) % 64]
    inv_a_p = inv_a[np.arange(128) % 32]
    tabRK = tab(pos_all, inv_r_p, 128.0 ** -0.5)
    tabRQ = tab(pos_all[6144:], inv_r_p, 1.0)
    tabAK = tab(pos_all[4096:], inv_a_p, 1.0)
    tabAQ = tab(pos_all[6144:], inv_a_p, 0.125)
    log_g = np.log1p(-np.exp2(-5.0 - np.arange(4, dtype=np.float64)))
    jj = np.arange(128)
    kdectab = np.zeros((128, 4, 128))
    qdec = np.zeros((128, 2, 512))
    cdec = np.zeros((128, 4))
    for p in range(2):
        for f in range(128):
            hh = 2 * p + f // 64
            for ab in range(2):
                kdectab[:, 2 * p + ab, f] = np.exp(log_g[hh] * (127.0 - jj))
            qdec[f, p, :] = np.tile(np.exp(log_g[hh] * (jj + 1.0)), 4)
            cdec[f, p] = np.exp(log_g[hh] * 128.0)
            cdec[f, 2 + p] = np.exp(log_g[hh] * 512.0)
    dmask = np.zeros((128, 4, 128))
    for hh in range(4):
        d = jj[None, :] - jj[:, None]
        dmask[:, hh, :] = np.where(d >= 0, np.exp(log_g[hh] * np.maximum(d, 0)), 0.0)
    flags = np.zeros((128, 4))
    for s in range(3):
        flags[:, s] = 1.0 if (j - 3 + s) >= 0 else 0.0
    jp = jj[:, None]
    ii = jj[None, :]
    mmain = np.concatenate([np.where(jp <= ii, 1.0, 0.0), np.where(jp >= ii, 1.0, 0.0)], axis=1)
    mprev = np.where(jp >= ii, 1.0, 0.0) if j > 0 else np.zeros((128, 128))
    sel = np.zeros((128, 64))
    sel[64, :] = 1.0
    pos_s = np.tile(16384 + np.arange(8), 4)
    tabS = np.concatenate([tab(pos_s, inv_r_p, 1.0), tab(pos_s, inv_r_p, 128.0 ** -0.5),
                           tab(pos_s, inv_a_p, 0.125), tab(pos_s, inv_a_p, 1.0)], axis=1)
    dec8 = np.zeros((128, 640))
    l8 = np.arange(8)
    for p in range(2):
        for f in range(128):
            hh = 2 * p + f // 64
            for ab in range(2):
                dec8[0:8, (2 * p + ab) * 128 + f] = np.exp(log_g[hh] * (7.0 - l8))
            dec8[f, 544 + 32 * p:544 + 32 * p + 32] = np.tile(np.exp(log_g[hh] * (l8 + 1.0)), 4)
            dec8[f, 608 + p] = np.exp(log_g[hh] * 8.0)
    for hh in range(4):
        d8 = l8[None, :] - l8[:, None]
        dec8[0:8, 512 + hh * 8:512 + hh * 8 + 8] = np.where(d8 >= 0, np.exp(log_g[hh] * np.maximum(d8, 0)), 0.0)
    mm = np.zeros((2176, 8))
    for l in range(8):
        for Dd in (1, 4, 16):
            idx = 2048 + l - Dd * np.arange(129)
            np.add.at(mm[:, l], idx, 1.0)
    mmult = mm.reshape(17, 128, 8).transpose(1, 0, 2).reshape(128, 136)
    c = lambda a: np.ascontiguousarray(a, dtype=f32)
    return dict(tabS=c(tabS), dec8=c(dec8), mmult=c(mmult), tabRK=tabRK, tabRQ=tabRQ, tabAK=tabAK, tabAQ=tabAQ, kdectab=c(kdectab.reshape(128, 512)),
                dmask=c(dmask.reshape(128, 512)), qdec=c(qdec.reshape(128, 1024)), cdec=c(cdec), flags=c(flags),
                mmain=c(mmain), mprev=c(mprev), ident=c(np.eye(128)), sel=c(sel))


def _weight_layouts(w_ada, b_ada, g_pre_mix, g_post_mix, g_pre_ffn, g_post_ffn, w_in, ret_gain, w_o, w_up, w_down):
    w_in = w_in[0]
    cols_ret = []
    for base in (0, 512):
        for p in range(2):
            for ab in range(2):
                for hl in range(2):
                    hh = 2 * p + hl
                    cols_ret += [base + 128 * hh + 2 * i + ab for i in range(64)]
    cols_ret += list(range(1536, 2048))
    cols_ret += list(range(1024, 1536))
    cols_att = []
    for base in (2048, 2560):
        for p in range(2):
            for ab in range(2):
                for q4 in range(4):
                    hh = 4 * p + q4
                    cols_att += [base + 64 * hh + 32 * ab + i for i in range(32)]
    cols_att += list(range(3072, 3584))
    d = {}
    d['wret'] = _pm(w_in[:, cols_ret], 2048)
    d['watt'] = _pm(w_in[:, cols_att], 1536)
    d['wada'] = _pm(w_ada[0], 6144)
    d['bada'] = np.ascontiguousarray(b_ada[0].reshape(48, 128).T)
    d['gvec'] = np.ascontiguousarray(np.stack([g.reshape(8, 128).T for g in (g_pre_mix[0], g_post_mix[0], g_pre_ffn[0], g_post_ffn[0])], axis=1))
    d['retg'] = np.ascontiguousarray(ret_gain[0].reshape(4, 128).T)
    wo = w_o[0]
    d['woR'] = np.ascontiguousarray(wo[0:512].reshape(4, 128, 1024).transpose(1, 0, 2))
    d['woA'] = np.ascontiguousarray(wo[512:1024].reshape(8, 64, 1024).transpose(1, 0, 2))
    d['wup'] = _pm(w_up[0], 4096)
    d['wdn'] = np.ascontiguousarray(w_down[0].reshape(32, 128, 8, 128).transpose(2, 1, 0, 3))
    return d


_NC_CACHE = {}


def kernel(x_prompt, x_sample, c_prompt, c_sample, state_ret, cache_win_k, cache_win_v,
           w_ada, b_ada, g_pre_mix, g_post_mix, g_pre_ffn, g_post_ffn, w_in, ret_gain, w_o, w_up, w_down):
    f32 = np.float32
    args = [np.asarray(a, dtype=f32) for a in (w_ada, b_ada, g_pre_mix, g_post_mix, g_pre_ffn, g_post_ffn, w_in, ret_gain, w_o, w_up, w_down)]
    wl = _weight_layouts(*args)
    x_prompt = np.asarray(x_prompt, dtype=f32)
    x_sample = np.asarray(x_sample, dtype=f32)
    state_ret = np.asarray(state_ret, dtype=f32)
    cache_win_k = np.asarray(cache_win_k, dtype=f32)
    cache_win_v = np.asarray(cache_win_v, dtype=f32)
    c_prompt = np.asarray(c_prompt, dtype=f32)
    c_sample = np.asarray(c_sample, dtype=f32)
    if 'nc' not in _NC_CACHE:
        _NC_CACHE['nc'] = build_program()
    nc = _NC_CACHE['nc']
    in_maps = []
    for core in range(8):
        b, j = core // 4, core % 4
        xs = np.zeros((4, 2048, 1024), f32)
        for s in range(4):
            qq = j - 3 + s
            if qq >= 0:
                xs[s] = x_prompt[b, qq * 2048:(qq + 1) * 2048]
        xTc = np.ascontiguousarray(xs.reshape(8192, 8, 128).transpose(2, 1, 0))
        cc = np.concatenate([c_prompt[b:b + 1], c_sample[core * 4:(core + 1) * 4]], axis=0)
        cTc = np.ascontiguousarray(cc.reshape(5, 8, 128).transpose(2, 1, 0))
        m = dict(wl)
        m.update(_const_tables(j))
        m['xT'] = xTc
        m['cT'] = cTc
        sq_ = slice(core * 4, core * 4 + 4)
        m['xsT'] = np.ascontiguousarray(x_sample[sq_].reshape(32, 8, 128).transpose(2, 1, 0))
        si = np.zeros((4, 128, 2, 2, 2, 128), f32)
        st_ = state_ret[0, sq_]
        for p in range(2):
            for hl in range(2):
                for ab in range(2):
                    si[:, 64 * hl:64 * hl + 64, p, ab, hl, :] = st_[:, 2 * p + hl, ab::2, :]
        m['sinit'] = np.ascontiguousarray(si.reshape(4, 128, 1024))
        kc = cache_win_k[0, sq_]
        kct = kc.reshape(4, 2048, 2, 4, 2, 32).transpose(0, 3, 5, 2, 4, 1)
        m['kcT'] = np.ascontiguousarray(kct.reshape(4, 128, 4, 2048))
        m['kcn'] = np.ascontiguousarray(kc.reshape(4, 2048, 512))
        m['vcn'] = np.ascontiguousarray(cache_win_v[0, sq_].reshape(4, 2048, 512))
        in_maps.append(m)
    res = run_bass_kernel_spmd(nc, in_maps, core_ids=list(range(8)))
    R = res.results
    y_prompt = np.zeros((2, 8192, 1024), f32)
    state_p = np.zeros((1, 2, 4, 128, 128), f32)
    kp = np.zeros((1, 2, 2048, 8, 64), f32)
    vp = np.zeros((1, 2, 2048, 8, 64), f32)
    for core in range(8):
        b, j = core // 4, core % 4
        yTc = R[core]['yT']
        y_prompt[b, j * 2048:(j + 1) * 2048] = yTc.transpose(2, 1, 0).reshape(2048, 1024)
        if j == 3:
            kt_ = R[core]['kTo']
            for p in range(2):
                for ab in range(2):
                    for q4 in range(4):
                        kp[0, b, :, 4 * p + q4, 32 * ab:32 * ab + 32] = kt_[32 * q4:32 * q4 + 32, 2 * p + ab, :].T
            vp[0, b] = R[core]['vnat'][2048:4096, :, 0:64]
            sf = R[core]['sfin']
            for p in range(2):
                for ab in range(2):
                    for hl in range(2):
                        blk = sf[64 * hl:64 * hl + 64, p, ab * 256 + hl * 128: ab * 256 + hl * 128 + 128]
                        state_p[0, b, 2 * p + hl, ab::2, :] = blk
    y_sample = np.zeros((32, 8, 1024), f32)
    state_s = np.zeros((1, 32, 4, 128, 128), f32)
    ks = np.zeros((1, 32, 2048, 8, 64), f32)
    vs = np.zeros((1, 32, 2048, 8, 64), f32)
    for core in range(8):
        r_ = R[core]
        y_sample[core * 4:(core + 1) * 4] = r_['ysT'].transpose(2, 1, 0).reshape(4, 8, 1024)
        for s_ in range(4):
            q = core * 4 + s_
            sf = r_['sstate'][s_].reshape(128, 2, 512)
            for p in range(2):
                for ab in range(2):
                    for hl in range(2):
                        state_s[0, q, 2 * p + hl, ab::2, :] = sf[64 * hl:64 * hl + 64, p, ab * 256 + hl * 128: ab * 256 + hl * 128 + 128]
            ks[0, q, 0:2040] = r_['kcs'][s_].reshape(2040, 8, 64)
            vs[0, q, 0:2040] = r_['vcs'][s_].reshape(2040, 8, 64)
            kn = r_['ksnew'][:, :, s_ * 8:s_ * 8 + 8]
            for p in range(2):
                for ab in range(2):
                    for q4 in range(4):
                        ks[0, q, 2040:2048, 4 * p + q4, 32 * ab:32 * ab + 32] = kn[32 * q4:32 * q4 + 32, 2 * p + ab, :].T
            vs[0, q, 2040:2048] = r_['vsnew'][:, s_, :].reshape(8, 8, 64)
    return (y_prompt, y_sample, state_p, kp, vp, state_s, ks, vs)
```

```python
import os
import numpy as np
import ml_dtypes
from contextlib import ExitStack
import concourse.bass as bass
import concourse.mybir as mybir
from concourse.bass_utils import run_bass_kernel_spmd

F32 = mybir.dt.float32
BF16 = mybir.dt.bfloat16
AF = mybir.ActivationFunctionType
ALU = mybir.AluOpType

N_DMA_SEMS = 24
SAME_ENGINE_SYNC = True
NEG = -30000.0
EPS = 1e-6
GAM = [1.0 - 2.0 ** (-5 - h) for h in range(4)]


class _Rec:
    def __init__(self):
        self.call = None

    def __getattr__(self, name):
        def f(*a, **k):
            self.call = (name, a, k)
            return self
        return f


class Sched:
    def __init__(self, nc):
        self.nc = nc
        self.engs = ['pe', 'act', 'dve', 'pool', 'sp']
        self.ops = []
        self.res_w = {}
        self.res_r = {}
        self.last = {}
        self.dmas_since = []
        self.pending = {}

    def barrier(self):
        L = set(self.last.values()) | set(self.dmas_since)
        self.dmas_since = []
        for e in self.engs:
            self.pending[e] = set(L) | self.pending.get(e, set())

    def op(self, eng, fn, reads=(), writes=(), dma=False):
        oid = len(self.ops)
        deps = set()
        raw = set()
        if self.pending.get(eng):
            deps |= self.pending.pop(eng)
        for r in reads:
            w = self.res_w.get(r)
            if w is not None:
                deps.add(w)
                raw.add(w)
        for w_ in writes:
            w = self.res_w.get(w_)
            if w is not None:
                deps.add(w)
            for rd in self.res_r.get(w_, ()):
                deps.add(rd)
        rec = _Rec()
        fn(rec)
        assert rec.call is not None
        self.ops.append(dict(eng=eng, fn=rec.call, deps=deps, raw=raw, dma=dma, id=oid))
        if dma:
            self.dmas_since.append(oid)
        else:
            self.last[eng] = oid
        for r in reads:
            self.res_r.setdefault(r, []).append(oid)
        for w_ in writes:
            self.res_w[w_] = oid
            self.res_r[w_] = []
        return oid

    def pe(self, fn, reads=(), writes=()):
        return self.op('pe', fn, reads, writes)

    def act(self, fn, reads=(), writes=()):
        return self.op('act', fn, reads, writes)

    def dve(self, fn, reads=(), writes=()):
        return self.op('dve', fn, reads, writes)

    def pool(self, fn, reads=(), writes=()):
        return self.op('pool', fn, reads, writes)

    def dma(self, fn, reads=(), writes=(), q='sp'):
        return self.op(q, fn, reads, writes, dma=True)

    def emit(self, stack):
        nc = self.nc
        ops = self.ops
        needed = [False] * len(ops)
        for o in ops:
            for d in o['deps']:
                de = ops[d]
                if de['dma']:
                    continue
                if de['eng'] == o['eng'] and not o['dma']:
                    if de['eng'] == 'pe' or not SAME_ENGINE_SYNC or d not in o['raw']:
                        continue
                needed[d] = True
        sems = {e: stack.enter_context(nc.semaphore("s_" + e)) for e in ['pe', 'act', 'dve', 'pool']}
        dsems = [stack.enter_context(nc.semaphore("d%d" % i)) for i in range(N_DMA_SEMS)]
        cnt = {e: 0 for e in sems}
        ndma = 0
        duse = [0] * N_DMA_SEMS
        half = N_DMA_SEMS // 2
        nd = {'sp': 0, 'pool': 0}
        for o in ops:
            if o['dma']:
                qn = 'pool' if o['eng'] == 'pool' else 'sp'
                k = (nd[qn] % half) + (half if qn == 'pool' else 0)
                nd[qn] += 1
                ndma += 1
                o['prev_tok'] = ('d', k, duse[k] * 16) if duse[k] else None
                duse[k] += 1
                o['tok'] = ('d', k, duse[k] * 16)
            elif needed[o['id']]:
                cnt[o['eng']] += 1
                o['tok'] = ('c', o['eng'], cnt[o['eng']])
            else:
                o['tok'] = None
        all_dma_toks = [o['tok'] for o in ops if o['dma']]
        streams = {e: [] for e in self.engs}
        for o in ops:
            streams[o['eng']].append(o)
        waited = {e: {} for e in self.engs}

        def semof(key):
            return dsems[key[1]] if key[0] == 'd' else sems[key[1]]

        def run_stream(ename, handle):
            wd = waited[ename]
            for o in streams[ename]:
                toks = []
                for d in o['deps']:
                    de = ops[d]
                    t = de['tok']
                    if t is None:
                        continue
                    if (not de['dma']) and de['eng'] == ename and not o['dma']:
                        if ename == 'pe' or not SAME_ENGINE_SYNC or d not in o['raw']:
                            continue
                    toks.append(t)
                if o['dma'] and o['prev_tok'] is not None:
                    toks.append(o['prev_tok'])
                best = {}
                for t in toks:
                    key = (t[0], t[1])
                    if t[2] > best.get(key, 0):
                        best[key] = t[2]
                for key, v in best.items():
                    if wd.get(key, 0) >= v:
                        continue
                    wd[key] = v
                    handle.wait_ge(semof(key), v)
                name_, a_, k_ = o['fn']
                ins = getattr(handle, name_)(*a_, **k_)
                t = o['tok']
                if t is not None:
                    ins.then_inc(semof(t), 16 if t[0] == 'd' else 1)
            return wd

        with nc.Block() as block:
            @block.tensor
            def _(e):
                run_stream('pe', e)

            @block.scalar
            def _(e):
                run_stream('act', e)

            @block.vector
            def _(e):
                run_stream('dve', e)

            @block.gpsimd
            def _(e):
                run_stream('pool', e)

            @block.sync
            def _(e):
                wd = run_stream('sp', e)
                final = {}
                for t in all_dma_toks:
                    final[t[1]] = max(final.get(t[1], 0), t[2])
                for k, v in final.items():
                    if wd.get(('d', k), 0) < v:
                        e.wait_ge(dsems[k], v)
                for en, c in cnt.items():
                    if c:
                        e.wait_ge(sems[en], c)


def build_program(debug=False, stop_after=99):
    nc = bass.Bass("TRN2", target_bir_lowering=False)

    def din(name, shape, dt=F32):
        return nc.dram_tensor(name, list(shape), dt, kind="ExternalInput").ap()

    def dout(name, shape, dt=F32):
        return nc.dram_tensor(name, list(shape), dt, kind="ExternalOutput").ap()

    xT = din("xT", [128, 8, 8192])
    cT = din("cT", [128, 8, 5])
    wada = din("wada", [128, 8, 6144])
    bada = din("bada", [128, 48])
    gvec = din("gvec", [128, 4, 8])
    retg = din("retg", [128, 4])
    wret = din("wret", [128, 8, 2048])
    watt = din("watt", [128, 8, 1536])
    woR = din("woR", [128, 4, 1024])
    woA = din("woA", [64, 8, 1024])
    wup = din("wup", [128, 8, 4096])
    wdn = din("wdn", [8, 128, 32, 128])
    tabRK = din("tabRK", [128, 2, 8192])
    tabRQ = din("tabRQ", [128, 2, 2048])
    tabAK = din("tabAK", [128, 2, 4096])
    tabAQ = din("tabAQ", [128, 2, 2048])
    kdectab_d = din("kdectab", [128, 512])
    dmask_d = din("dmask", [128, 512])
    qdec_d = din("qdec", [128, 1024])
    cdec_d = din("cdec", [128, 4])
    flags_d = din("flags", [128, 4])
    mmain_d = din("mmain", [128, 256])
    mprev_d = din("mprev", [128, 128])
    ident_d = din("ident", [128, 128])
    sel_d = din("sel", [128, 64])

    xsT = din("xsT", [128, 8, 32])
    tabS_d = din("tabS", [128, 8, 32])
    sinit = din("sinit", [4, 128, 1024])
    kcT = din("kcT", [4, 128, 4, 2048])
    kcn = din("kcn", [4, 2048, 512])
    vcn = din("vcn", [4, 2048, 512])
    mmult_d = din("mmult", [128, 136])
    dec8_d = din("dec8", [128, 640])
    ysT = dout("ysT", [128, 8, 32])
    sstate = dout("sstate", [4, 128, 1024])
    kcs = dout("kcs", [4, 2040, 512])
    vcs = dout("vcs", [4, 2040, 512])
    ksnew = dout("ksnew", [128, 4, 32])
    vsnew = dout("vsnew", [8, 4, 512])
    yT = dout("yT", [128, 8, 2048])
    kTo = dout("kTo", [128, 4, 2048])
    vnat = dout("vnat", [4096, 8, 65])
    sfin = dout("sfin", [128, 2, 512])
    if debug:
        dbg_hatt = dout("dbg_hatt", [64, 8, 2048])
        dbg_hret = dout("dbg_hret", [128, 4, 2048])
        dbg_x1 = dout("dbg_x1", [128, 8, 2048])

    st = ExitStack()
    S = Sched(nc)

    KB = 1024
    p0 = [0]

    def at(name, shape, dt, off):
        assert off % 32 == 0, (name, off)
        nbytes = int(np.prod(shape[1:])) * (2 if dt == BF16 else 4)
        assert off + nbytes <= 212700, (name, off, nbytes)
        return nc.alloc_sbuf_tensor_at(name, list(shape), dt, offset=off + 16640)

    def sb(name, shape, dt):
        nbytes = int(np.prod(shape[1:])) * (2 if dt == BF16 else 4)
        off = p0[0]
        p0[0] = (off + nbytes + 31) // 32 * 32
        assert p0[0] <= 18 * KB, name
        return at(name, shape, dt, off)

    def seq_at(base):
        cur = [base]

        def f(name, shape, dt):
            nbytes = int(np.prod(shape[1:])) * (2 if dt == BF16 else 4)
            off = cur[0]
            cur[0] = (off + nbytes + 31) // 32 * 32
            return at(name, shape, dt, off)
        f.cur = cur
        return f

    ps = [st.enter_context(nc.psum_tensor("ps%d" % i, [128, 512], F32)) for i in range(7)]
    psb = st.enter_context(nc.psum_tensor("psb", [128, 1024], BF16))

    ident = sb("identb", [128, 128], BF16)
    ones = sb("onesb", [128, 128], BF16)
    self32 = sb("self32", [128, 64], F32)
    mmain = sb("mmain_s", [128, 256], BF16)
    mprev = sb("mprev_s", [128, 128], BF16)
    kdectab = sb("kdectab_s", [128, 512], F32)
    dmask = sb("dmask_s", [128, 512], F32)
    cdec = sb("cdec_s", [128, 4], F32)
    flags = sb("flags_s", [128, 4], F32)
    gv = sb("gv_s", [128, 4, 8], F32)
    rgain = sb("rgain_s", [128, 4], F32)
    bad = sb("bad_s", [128, 48], F32)
    cTs = sb("cT_s", [128, 8, 5], F32)
    scT = sb("scT", [128, 8, 5], BF16)
    mod = sb("mod", [128, 48, 5], F32)
    gm1 = sb("gm1", [128, 8, 5], F32)
    gg1 = sb("gg1", [128, 8, 5], F32)
    gm2 = sb("gm2", [128, 8, 5], F32)
    gg2 = sb("gg2", [128, 8, 5], F32)
    epsb = sb("epsb", [128, 1], F32)
    xs = sb("xs", [128, 8, 32], F32)
    x1s = sb("x1s", [128, 8, 32], F32)
    hTs = sb("hTs", [128, 8, 32], BF16)
    hret_s = sb("hret_s", [128, 4, 32], BF16)
    hatt_s = sb("hatt_s", [64, 8, 32], BF16)
    aqs = sb("aqs", [128, 4, 32], BF16)
    aks = sb("aks", [128, 4, 32], BF16)
    tabS = sb("tabS_s", [128, 8, 32], F32)
    S.dma(lambda e: e.dma_start(out=xs[:], in_=xsT), writes=['xs'])
    S.dma(lambda e: e.dma_start(out=tabS[:], in_=tabS_d), writes=['tabS'])

    for (dst, src, q, nm) in [(ident, ident_d, 'pool', 'ident'), (mmain, mmain_d, 'pool', 'mmain'),
                              (mprev, mprev_d, 'pool', 'mprev'), (self32, sel_d, 'sp', 'sel'),
                              (kdectab, kdectab_d, 'sp', 'kdectab'), (dmask, dmask_d, 'sp', 'dmask'),
                              (cdec, cdec_d, 'sp', 'cdec'),
                              (flags, flags_d, 'sp', 'flags'), (gv, gvec, 'sp', 'gv'),
                              (rgain, retg, 'sp', 'rgain'), (bad, bada, 'sp', 'bad'), (cTs, cT, 'sp', 'cTs')]:
        S.dma(lambda e, d=dst, s=src: e.dma_start(out=d[:], in_=s), writes=[nm], q=q)
    S.pool(lambda e: e.memset(ones[:], 1.0), writes=['ones'])
    S.pool(lambda e: e.memset(epsb[:], EPS), writes=['epsb'])

    S.act(lambda e: e.activation(out=scT[:], in_=cTs[:], func=AF.Silu), reads=['cTs'], writes=['scT'])
    wa = [at("wa%d" % i, [128, 8, 1024], BF16, (98 + 16 * i) * KB) for i in range(2)]
    for fc in range(6):
        wb_ = wa[fc % 2]
        S.dma(lambda e, d=wb_, fc=fc: e.dma_start(out=d[:], in_=wada[:, :, fc * 1024:(fc + 1) * 1024]),
              writes=['wa%d' % (fc % 2)], q='pool')
        pb = ps[fc % 2]
        for ft in range(8):
            for kt in range(8):
                S.pe(lambda e, pb=pb, wb_=wb_, ft=ft, kt=kt: e.matmul(
                    pb[:, ft * 8:ft * 8 + 5], lhsT=wb_[:, kt, ft * 128:(ft + 1) * 128], rhs=scT[:, kt, :],
                    start=(kt == 0), stop=(kt == 7)), reads=['wa%d' % (fc % 2), 'scT'], writes=['ps%d' % (fc % 2)])
        for ft in range(8):
            f = fc * 8 + ft
            S.act(lambda e, pb=pb, ft=ft, f=f: e.activation(
                out=mod[:, f, :], in_=pb[:, ft * 8:ft * 8 + 5], func=AF.Identity, bias=bad[:, f:f + 1], scale=1.0),
                reads=['ps%d' % (fc % 2), 'bad'], writes=['mod'])
    for (dst, nm, gi, mo, addone) in [(gm1, 'gm1', 0, 8, True), (gg1, 'gg1', 1, 16, False),
                                      (gm2, 'gm2', 2, 32, True), (gg2, 'gg2', 3, 40, False)]:
        for kt in range(8):
            if addone:
                S.dve(lambda e, dst=dst, gi=gi, mo=mo, kt=kt: e.tensor_scalar(
                    out=dst[:, kt, :], in0=mod[:, mo + kt, :], scalar1=1.0, scalar2=gv[:, gi, kt:kt + 1],
                    op0=ALU.add, op1=ALU.mult), reads=['mod', 'gv'], writes=[nm])
            else:
                S.dve(lambda e, dst=dst, gi=gi, mo=mo, kt=kt: e.tensor_scalar(
                    out=dst[:, kt, :], in0=mod[:, mo + kt, :], scalar1=gv[:, gi, kt:kt + 1], scalar2=None,
                    op0=ALU.mult), reads=['mod', 'gv'], writes=[nm])
    SH1, SH2 = 0, 24

    rstd = [sb("rstd%d" % i, [128, 512], F32) for i in range(2)]
    rot_t = [at("rot_t%d" % i, [128, 512], F32, (22 + 2 * i) * KB) for i in range(4)]
    gctr = [0]

    def gbank():
        gctr[0] += 1
        return gctr[0] % 3

    def rstd_from_sq(sq_aps, sq_res, out_i, scale, N=512):
        n = len(sq_aps)
        for i, a in enumerate(sq_aps):
            S.pe(lambda e, a=a, i=i: e.matmul(ps[3][:, 0:N], lhsT=ones[:], rhs=a, start=(i == 0), stop=(i == n - 1)),
                 reads=['ones'] + sq_res, writes=['ps3'])
        r = rstd[out_i]
        S.act(lambda e, r=r: e.activation(out=r[:, 0:N], in_=ps[3][:, 0:N], func=AF.Ln, bias=epsb[:, 0:1], scale=scale),
              reads=['ps3', 'epsb'], writes=['rstd%d' % out_i])
        S.act(lambda e, r=r: e.activation(out=r[:, 0:N], in_=r[:, 0:N], func=AF.Exp, scale=-0.5), reads=['rstd%d' % out_i], writes=['rstd%d' % out_i])
        return r

    xt = [at("xt%d" % i, [128, 8, 512], F32, (30 + 16 * i) * KB) for i in range(2)]
    sq = at("sq", [128, 8, 512], BF16, 62 * KB)
    hT0 = at("hT", [128, 8, 512], BF16, 70 * KB)
    hTb = [hT0, hT0]
    curh = [0]
    tmpf = [at("tmpf%d" % i, [128, 512], F32, (18 + 2 * i) * KB) for i in range(2)]
    sqbox = [sq]

    def load_x(col0, bi):
        S.dma(lambda e: e.dma_start(out=xt[bi][:], in_=xT[:, :, col0:col0 + 512]), writes=['xt%d' % bi])

    def prenorm(bi, gm, shrow, seq=0):
        x = xt[bi]
        hT = hTb[bi]
        for kt in range(8):
            S.act(lambda e, kt=kt: e.activation(out=sq[:, kt, :], in_=x[:, kt, :], func=AF.Square),
                  reads=['xt%d' % bi], writes=['sq'])
        r = rstd_from_sq([sq[:, kt, :] for kt in range(8)], ['sq'], 0, 1.0 / 1024)
        for kt in range(8):
            t = tmpf[kt % 2]
            S.dve(lambda e, t=t, kt=kt: e.tensor_tensor(out=t[:], in0=x[:, kt, :], in1=r[:], op=ALU.mult),
                  reads=['xt%d' % bi, 'rstd0'], writes=['tmpf%d' % (kt % 2)])
            S.act(lambda e, t=t, kt=kt: e.activation(out=hT[:, kt, :], in_=t[:], func=AF.Identity,
                                                     bias=mod[:, shrow + kt, seq:seq + 1], scale=gm[:, kt, seq:seq + 1]),
                  reads=['tmpf%d' % (kt % 2), 'mod', 'gm1', 'gm2'], writes=['hT%d' % bi])

    def proj_fm(w, wres, col0, bank):
        hT = hTb[curh[0]]
        for kt in range(8):
            S.pe(lambda e, kt=kt: e.matmul(ps[bank][:], lhsT=w[:, kt, col0:col0 + 128], rhs=hT[:, kt, :],
                                           start=(kt == 0), stop=(kt == 7)),
                 reads=[wres, 'hT%d' % curh[0]], writes=['ps%d' % bank])

    def rotate(bA, bB, tab, tabres, outA, outB, ores, outA32=None, outB32=None, o32res=None, N=512, cs=None):
        t0, t1, t2, t3 = [t[:, 0:N] for t in rot_t]
        cos, sin = cs if cs is not None else (tab[:, 0, :], tab[:, 1, :])
        pA, pB = ps[bA][:, 0:N], ps[bB][:, 0:N]
        S.dve(lambda e: e.tensor_tensor(out=t0, in0=pA, in1=cos, op=ALU.mult), reads=['ps%d' % bA, tabres], writes=['rot0'])
        S.dve(lambda e: e.tensor_tensor(out=t1, in0=pB, in1=sin, op=ALU.mult), reads=['ps%d' % bB, tabres], writes=['rot1'])
        S.dve(lambda e: e.tensor_tensor(out=t2, in0=pB, in1=cos, op=ALU.mult), reads=['ps%d' % bB, tabres], writes=['rot2'])
        S.dve(lambda e: e.tensor_tensor(out=t3, in0=pA, in1=sin, op=ALU.mult), reads=['ps%d' % bA, tabres], writes=['rot3'])
        S.dve(lambda e: e.tensor_tensor(out=outA, in0=t0, in1=t1, op=ALU.subtract), reads=['rot0', 'rot1'], writes=[ores])
        S.dve(lambda e: e.tensor_tensor(out=outB, in0=t2, in1=t3, op=ALU.add), reads=['rot2', 'rot3'], writes=[ores])
        if outA32 is not None:
            S.pool(lambda e: e.tensor_tensor(out=outA32, in0=t0, in1=t1, op=ALU.subtract), reads=['rot0', 'rot1'], writes=[o32res])
            S.pool(lambda e: e.tensor_tensor(out=outB32, in0=t2, in1=t3, op=ALU.add), reads=['rot2', 'rot3'], writes=[o32res])

    def proj_fm_s(w, wres, col0, bank):
        for kt in range(8):
            S.pe(lambda e, kt=kt: e.matmul(ps[bank][:, 0:32], lhsT=w[:, kt, col0:col0 + 128], rhs=hTs[:, kt, :],
                                           start=(kt == 0), stop=(kt == 7)),
                 reads=[wres, 'hTs'], writes=['ps%d' % bank])

    def tabs(kind):
        return (tabS[:, 2 * kind, :], tabS[:, 2 * kind + 1, :])

    if stop_after < 1:
        S.emit(st)
        return nc
    hret = at("hret", [128, 4, 2048], BF16, 82 * KB)
    S.barrier()
    if True:
        sb1 = seq_at(98 * KB)
        wr = sb1("wr", [128, 8, 2048], BF16)
        qdec = sb1("qdec_s", [128, 1024], F32)
        dec8 = sb1("dec8_s", [128, 640], F32)
        S.dma(lambda e: e.dma_start(out=dec8[:], in_=dec8_d), writes=['dec8'])
        S.dma(lambda e: e.dma_start(out=qdec[:], in_=qdec_d), writes=['qdec'])
        S.dma(lambda e: e.dma_start(out=wr[:, :, 0:1024], in_=wret[:, :, 0:1024]), writes=['wr'], q='pool')
        S.dma(lambda e: e.dma_start(out=wr[:, :, 1024:2048], in_=wret[:, :, 1024:2048]), writes=['wr'], q='pool')
        tk_off = sb1.cur[0]
        tk = [sb1("tk%d" % i, [128, 2, 512], F32) for i in range(2)]
        tq = [sb1("tq%d" % i, [128, 2, 512], F32) for i in range(2)]
        rk = sb1("rk", [128, 4, 512], BF16)
        rq = sb1("rq", [128, 4, 512], BF16)
        rqd = sb1("rqd", [128, 4, 512], BF16)
        rv = sb1("rv", [128, 4, 512], BF16)
        sg = sb1("sg", [128, 4, 512], BF16)
        sgf = sb1("sgf", [128, 512], F32)
        kdec4 = sb1("kdec4", [128, 4, 512], BF16)
        am4 = sb1("am4", [128, 4, 512], BF16)
        Sst = [sb1("Sst%d" % p, [128, 512], F32) for p in range(2)]
        Sring = [[sb1("Sr%d_%d" % (p, v), [128, 512], BF16) for v in range(5)] for p in range(2)]
        osq = sb1("osq", [128, 512], BF16)
        on = sb1("on", [128, 512], F32)
        for p in range(2):
            S.pool(lambda e, p=p: e.memset(Sst[p][:], 0.0), writes=['Sst%d' % p])
            S.pool(lambda e, p=p: e.memset(Sring[p][0][:], 0.0), writes=['Sr%d_0' % p])
        sver = 0
        rctr = [0]
        hT1a_off = sb1.cur[0]
        hT1a = sb1("hT1a", [128, 8, 512], BF16)
        hTb[1] = hT1a
        load_x(0, 0)
        load_x(512, 1)
        prenorm(0, gm1, SH1)
        for ti in range(16):
            slot, own = ti // 4, ti >= 12
            bi = ti % 2
            col0 = ti * 512
            S.dma(lambda e, bi=bi, col0=col0: e.dma_start(out=tk[bi][:], in_=tabRK[:, :, col0:col0 + 512]), writes=['tk%d' % bi])
            if own:
                oc0 = col0 - 6144
                S.dma(lambda e, bi=bi, oc0=oc0: e.dma_start(out=tq[bi][:], in_=tabRQ[:, :, oc0:oc0 + 512]), writes=['tq%d' % bi])
            if ti + 1 < 16:
                prenorm((ti + 1) % 2, gm1, SH1)
            if ti + 2 < 16:
                load_x((ti + 2) * 512, ti % 2)
            curh[0] = bi
            hT = hTb[bi]
            ringb = [0, 1, 2] if own else [0, 1, 2, 6]

            def rb():
                rctr[0] += 1
                return ringb[rctr[0] % len(ringb)]
            for p in range(2):
                bA, bB = rb(), rb()
                proj_fm(wr, 'wr', (4 + 2 * p) * 128, bA)
                proj_fm(wr, 'wr', (5 + 2 * p) * 128, bB)
                rotate(bA, bB, tk[bi], 'tk%d' % bi, rk[:, 2 * p, :], rk[:, 2 * p + 1, :], 'rk')
            for c in range(4):
                b = rb()
                for kt in range(8):
                    S.pe(lambda e, kt=kt, c=c: e.matmul(ps[b][:], lhsT=hT[:, kt, c * 128:(c + 1) * 128], rhs=wr[:, kt, 1536:2048],
                                                        start=(kt == 0), stop=(kt == 7)), reads=['wr', 'hT%d' % bi], writes=['ps%d' % b])
                if own:
                    S.act(lambda e, c=c: e.copy(out=rv[:, c, :], in_=ps[b][:]), reads=['ps%d' % b], writes=['rv%d' % c])
                else:
                    for h in range(4):
                        sc_ = float(GAM[h] ** (128 * (3 - c)))
                        S.act(lambda e, c=c, h=h, sc_=sc_: e.mul(out=rv[:, c, h * 128:(h + 1) * 128], in_=ps[b][:, h * 128:(h + 1) * 128], mul=sc_),
                              reads=['ps%d' % b], writes=['rv%d' % c])
            if not own:
                for rd in range(2):
                    for cl in range(2):
                        c = 2 * rd + cl
                        cc = slice(c * 128, (c + 1) * 128)
                        for idx in range(4):
                            S.pe(lambda e, idx=idx, cc=cc, cl=cl: e.transpose(out=psb[:, (cl * 4 + idx) * 128:(cl * 4 + idx + 1) * 128], in_=rk[:, idx, cc], identity=ident[:]),
                                 reads=['rk', 'ident'], writes=['psb'])
                    for cl in range(2):
                        c = 2 * rd + cl
                        S.dve(lambda e, c=c, cl=cl: e.tensor_tensor(out=kdec4[:, c, :], in0=psb[:, cl * 512:(cl + 1) * 512], in1=kdectab[:], op=ALU.mult),
                              reads=['psb', 'kdectab'], writes=['kdec4_%d' % c])
                for p in range(2):
                    for ab in range(2):
                        for c in range(4):
                            S.pe(lambda e, p=p, ab=ab, c=c: e.matmul(ps[4 + p][:, ab * 256:(ab + 1) * 256], lhsT=kdec4[:, c, (2 * p + ab) * 128:(2 * p + ab + 1) * 128],
                                                                     rhs=rv[:, c, p * 256:(p + 1) * 256], start=(c == 0), stop=(c == 3)),
                                 reads=['kdec4_%d' % c, 'rv%d' % c], writes=['ps%d' % (4 + p)])
                for p in range(2):
                    S.dve(lambda e, p=p: e.scalar_tensor_tensor(out=Sst[p][:], in0=Sst[p][:], scalar=cdec[:, 2 + p:3 + p], in1=ps[4 + p][:],
                                                                op0=ALU.mult, op1=ALU.add), reads=['Sst%d' % p, 'cdec', 'ps%d' % (4 + p)], writes=['Sst%d' % p])
                    if ti % 4 == 3 and slot < 3:
                        S.dve(lambda e, p=p, slot=slot: e.tensor_scalar(out=Sst[p][:], in0=Sst[p][:], scalar1=flags[:, slot:slot + 1], scalar2=None, op0=ALU.mult),
                              reads=['Sst%d' % p, 'flags'], writes=['Sst%d' % p])
                    if ti == 11:
                        S.act(lambda e, p=p: e.copy(out=Sring[p][0][:], in_=Sst[p][:]), reads=['Sst%d' % p], writes=['Sr%d_0' % p])
                continue
            if own:
                for p in range(2):
                    proj_fm(wr, 'wr', (0 + 2 * p) * 128, 0)
                    proj_fm(wr, 'wr', (1 + 2 * p) * 128, 1)
                    rotate(0, 1, tq[bi], 'tq%d' % bi, rq[:, 2 * p, :], rq[:, 2 * p + 1, :], 'rq')
                    for ab in range(2):
                        S.pool(lambda e, p=p, ab=ab: e.tensor_tensor(
                            out=rqd[:, 2 * p + ab, :], in0=rq[:, 2 * p + ab, :], in1=qdec[:, p * 512:(p + 1) * 512], op=ALU.mult),
                            reads=['rq', 'qdec'], writes=['rqd'])
                for h in range(4):
                    proj_fm(wr, 'wr', (8 + h) * 128, 2)
                    S.act(lambda e: e.activation(out=sgf[:], in_=ps[2][:], func=AF.Silu), reads=['ps2'], writes=['sgf'])
                    S.dve(lambda e, h=h: e.tensor_scalar(out=sg[:, h, :], in0=sgf[:], scalar1=rgain[:, h:h + 1], scalar2=None, op0=ALU.mult),
                          reads=['sgf', 'rgain'], writes=['sg'])
            for c in range(4):
                g = (ti - 12) * 4 + c
                cc = slice(c * 128, (c + 1) * 128)
                for idx in range(4):
                    S.pe(lambda e, idx=idx, cc=cc: e.transpose(out=psb[:, idx * 128:(idx + 1) * 128], in_=rk[:, idx, cc], identity=ident[:]),
                         reads=['rk', 'ident'], writes=['psb'])
                S.dve(lambda e, c=c: e.tensor_tensor(out=kdec4[:, c, :], in0=psb[:, 0:512], in1=kdectab[:], op=ALU.mult),
                      reads=['psb', 'kdectab'], writes=['kdec4_%d' % c])
                for p in range(2):
                    for ab in range(2):
                        S.pe(lambda e, p=p, ab=ab, c=c: e.matmul(ps[4 + p][:, ab * 256:(ab + 1) * 256], lhsT=kdec4[:, c, (2 * p + ab) * 128:(2 * p + ab + 1) * 128],
                                                                 rhs=rv[:, c, p * 256:(p + 1) * 256], start=True, stop=True),
                             reads=['kdec4_%d' % c, 'rv%d' % c], writes=['ps%d' % (4 + p)])
                for p in range(2):
                    S.dve(lambda e, p=p: e.scalar_tensor_tensor(out=Sst[p][:], in0=Sst[p][:], scalar=cdec[:, p:p + 1], in1=ps[4 + p][:],
                                                                op0=ALU.mult, op1=ALU.add), reads=['Sst%d' % p, 'cdec', 'ps%d' % (4 + p)], writes=['Sst%d' % p])
                    nv = (g + 1) % 5
                    S.act(lambda e, p=p, nv=nv: e.copy(out=Sring[p][nv][:], in_=Sst[p][:]), reads=['Sst%d' % p], writes=['Sr%d_%d' % (p, nv)])
            OST = int(os.environ.get("OWN_STAGE", "3"))
            for c in range(4 if OST >= 2 else 0):
                cc = slice(c * 128, (c + 1) * 128)
                gbs = [gbank(), gbank()]
                for h in range(4):
                    p, hl = h // 2, h % 2
                    bsl = slice(64 * hl, 64 * hl + 64)
                    gb = gbs[hl]
                    S.pe(lambda e, p=p, bsl=bsl, cc=cc, gb=gb, h=h: e.matmul(ps[gb][:, h * 128:(h + 1) * 128], lhsT=rk[bsl, 2 * p, cc], rhs=rq[bsl, 2 * p, cc], start=True, stop=False),
                         reads=['rk', 'rq'], writes=['ps%d' % gb])
                    S.pe(lambda e, p=p, bsl=bsl, cc=cc, gb=gb, h=h: e.matmul(ps[gb][:, h * 128:(h + 1) * 128], lhsT=rk[bsl, 2 * p + 1, cc], rhs=rq[bsl, 2 * p + 1, cc], start=False, stop=True),
                         reads=['rk', 'rq'], writes=['ps%d' % gb])
                for hl in range(2):
                    gb = gbs[hl]
                    v_ = lambda X: X.rearrange("p (a b i) -> p a b i", a=2, b=2)[:, :, hl, :]
                    S.dve(lambda e, c=c, gb=gb, v_=v_: e.tensor_tensor(out=v_(am4[:, c, :]), in0=v_(ps[gb][:]), in1=v_(dmask[:]), op=ALU.mult),
                          reads=['ps%d' % gb, 'dmask'], writes=['am%d_%d' % (c, hl)])
            for c in range(4 if OST >= 3 else 0):
                g = (ti - 12) * 4 + c
                cc = slice(c * 128, (c + 1) * 128)
                sv = g % 5
                otb = 6 if c % 2 == 0 else 4
                ot = ps[otb]
                for h in range(4):
                    p, hl = h // 2, h % 2
                    bsl = slice(64 * hl, 64 * hl + 64)
                    osl = ot[:, h * 128:(h + 1) * 128]
                    S.pe(lambda e, h=h, c=c, osl=osl: e.matmul(osl, lhsT=rv[:, c, h * 128:(h + 1) * 128], rhs=am4[:, c, h * 128:(h + 1) * 128], start=True, stop=False),
                         reads=['rv%d' % c, 'am%d_%d' % (c, hl)], writes=['ps%d' % otb])
                    for ab in range(2):
                        S.pe(lambda e, p=p, ab=ab, hl=hl, bsl=bsl, cc=cc, osl=osl, sv=sv: e.matmul(
                            osl, lhsT=Sring[p][sv][bsl, ab * 256 + hl * 128: ab * 256 + hl * 128 + 128], rhs=rqd[bsl, 2 * p + ab, cc],
                            start=False, stop=(ab == 1)), reads=['Sr%d_%d' % (p, sv), 'rqd'], writes=['ps%d' % otb])
                S.act(lambda e: e.activation(out=osq[:], in_=ot[:], func=AF.Square), reads=['ps%d' % otb], writes=['osq'])
                r = rstd_from_sq([osq[:]], ['osq'], 1, 1.0 / 128)
                S.dve(lambda e, r=r: e.tensor_tensor(out=on[:], in0=ot[:], in1=r[:], op=ALU.mult), reads=['ps%d' % otb, 'rstd1'], writes=['on'])
                oc = (ti - 12) * 512 + c * 128
                S.pool(lambda e, oc=oc, cc=cc: e.tensor_tensor(out=hret[:, :, oc:oc + 128], in0=on[:].rearrange("p (h i) -> p h i", h=4),
                                                               in1=sg[:, :, cc], op=ALU.mult), reads=['on', 'sg'], writes=[])
        for p in range(2):
            S.dma(lambda e, p=p: e.dma_start(out=sfin[:, p, :], in_=Sst[p][:]), reads=['Sst%d' % p])
        S.barrier()
        sb1 = seq_at(tk_off)
        rks = sb1("rks", [128, 4, 32], BF16)
        rqs = sb1("rqs", [128, 4, 32], BF16)
        rqds = sb1("rqds", [128, 4, 32], BF16)
        rvs = sb1("rvs", [128, 4, 512], BF16)
        sgs = sb1("sgs", [128, 4, 32], BF16)
        sgfs = sb1("sgfs", [128, 32], F32)
        kdecs = sb1("kdecs", [8, 512], BF16)
        ams = sb1("ams", [128, 32], BF16)
        S.pool(lambda e: e.memset(rvs[:], 0.0), writes=['rvs'])
        S.pool(lambda e: e.memset(ams[:], 0.0), writes=['ams'])
        S0f = [at("S0f0", [128, 1024], F32, hT1a_off)] * 2
        S1f = [at("S1f0", [128, 1024], F32, hT1a_off + 4 * KB)] * 2
        S0b = [sb1("S0b0", [128, 1024], BF16)] * 2
        osqs = sb1("osqs", [128, 128], BF16)
        ons = sb1("ons", [128, 128], F32)
        for kt in range(8):
            S.act(lambda e, kt=kt: e.activation(out=sq[:, kt, 0:32], in_=xs[:, kt, :], func=AF.Square), reads=['xs'], writes=['sq'])
        r = rstd_from_sq([sq[:, kt, 0:32] for kt in range(8)], ['sq'], 0, 1.0 / 1024, N=32)
        for kt in range(8):
            t = tmpf[kt % 2]
            S.dve(lambda e, t=t, kt=kt: e.tensor_tensor(out=t[:, 0:32], in0=xs[:, kt, :], in1=r[:, 0:32], op=ALU.mult),
                  reads=['xs', 'rstd0'], writes=['tmpf%d' % (kt % 2)])
            for s_ in range(4):
                S.act(lambda e, t=t, kt=kt, s_=s_: e.activation(out=hTs[:, kt, s_ * 8:s_ * 8 + 8], in_=t[:, s_ * 8:s_ * 8 + 8], func=AF.Identity,
                                                                bias=mod[:, SH1 + kt, 1 + s_:2 + s_], scale=gm1[:, kt, 1 + s_:2 + s_]),
                      reads=['tmpf%d' % (kt % 2), 'mod', 'gm1'], writes=['hTs'])
        for p in range(2):
            proj_fm_s(wr, 'wr', (4 + 2 * p) * 128, 0)
            proj_fm_s(wr, 'wr', (5 + 2 * p) * 128, 1)
            rotate(0, 1, None, 'tabS', rks[:, 2 * p, :], rks[:, 2 * p + 1, :], 'rks', N=32, cs=tabs(1))
            proj_fm_s(wr, 'wr', (0 + 2 * p) * 128, 0)
            proj_fm_s(wr, 'wr', (1 + 2 * p) * 128, 1)
            rotate(0, 1, None, 'tabS', rqs[:, 2 * p, :], rqs[:, 2 * p + 1, :], 'rqs', N=32, cs=tabs(0))
            for ab in range(2):
                S.pool(lambda e, p=p, ab=ab: e.tensor_tensor(out=rqds[:, 2 * p + ab, :], in0=rqs[:, 2 * p + ab, :],
                                                             in1=dec8[:, 544 + 32 * p:544 + 32 * p + 32], op=ALU.mult),
                       reads=['rqs', 'dec8'], writes=['rqds'])
        for h in range(4):
            proj_fm_s(wr, 'wr', (8 + h) * 128, 2)
            S.act(lambda e: e.activation(out=sgfs[:], in_=ps[2][:, 0:32], func=AF.Silu), reads=['ps2'], writes=['sgfs'])
            S.dve(lambda e, h=h: e.tensor_scalar(out=sgs[:, h, :], in0=sgfs[:], scalar1=rgain[:, h:h + 1], scalar2=None, op0=ALU.mult),
                  reads=['sgfs', 'rgain'], writes=['sgs'])
        for s_ in range(4):
            s8 = slice(s_ * 8, s_ * 8 + 8)
            for kt in range(8):
                S.pe(lambda e, kt=kt, s8=s8: e.matmul(ps[2][0:8, :], lhsT=hTs[:, kt, s8], rhs=wr[:, kt, 1536:2048], start=(kt == 0), stop=(kt == 7)),
                     reads=['wr', 'hTs'], writes=['ps2'])
            S.act(lambda e, s_=s_: e.copy(out=rvs[0:8, s_, :], in_=ps[2][0:8, :]), reads=['ps2'], writes=['rvs'])
        ot = ps[6]
        for s_ in range(4):
            s8 = slice(s_ * 8, s_ * 8 + 8)
            si = 0
            S.dma(lambda e, s_=s_, si=si: e.dma_start(out=S0f[si][:], in_=sinit[s_]), writes=['S0f%d' % si])
            S.pool(lambda e, si=si: e.tensor_copy(out=S0b[si][:], in_=S0f[si][:]), reads=['S0f%d' % si], writes=['S0b%d' % si])
            for idx in range(4):
                S.pe(lambda e, idx=idx, s8=s8: e.transpose(out=psb[0:8, idx * 128:(idx + 1) * 128], in_=rks[:, idx, s8], identity=ident[:]),
                     reads=['rks', 'ident'], writes=['psb'])
            S.dve(lambda e: e.tensor_tensor(out=kdecs[:], in0=psb[0:8, 0:512], in1=dec8[0:8, 0:512], op=ALU.mult),
                  reads=['psb', 'dec8'], writes=['kdecs'])
            for p in range(2):
                for ab in range(2):
                    S.pe(lambda e, p=p, ab=ab, s_=s_: e.matmul(ps[4 + p][:, ab * 256:(ab + 1) * 256], lhsT=kdecs[0:8, (2 * p + ab) * 128:(2 * p + ab + 1) * 128],
                                                               rhs=rvs[0:8, s_, p * 256:(p + 1) * 256], start=True, stop=True),
                         reads=['kdecs', 'rvs'], writes=['ps%d' % (4 + p)])
                S.dve(lambda e, p=p, si=si: e.scalar_tensor_tensor(out=S1f[si][:, p * 512:(p + 1) * 512], in0=S0f[si][:, p * 512:(p + 1) * 512],
                                                                  scalar=dec8[:, 608 + p:609 + p], in1=ps[4 + p][:], op0=ALU.mult, op1=ALU.add),
                      reads=['S0f%d' % si, 'dec8', 'ps%d' % (4 + p)], writes=['S1f%d' % si])
            S.dma(lambda e, s_=s_, si=si: e.dma_start(out=sstate[s_], in_=S1f[si][:]), reads=['S1f%d' % si])
            for h in range(4):
                p, hl = h // 2, h % 2
                bsl = slice(64 * hl, 64 * hl + 64)
                gb = gbank()
                S.pe(lambda e, p=p, bsl=bsl, s8=s8, gb=gb: e.matmul(ps[gb][0:8, 0:8], lhsT=rks[bsl, 2 * p, s8], rhs=rqs[bsl, 2 * p, s8], start=True, stop=False),
                     reads=['rks', 'rqs'], writes=['ps%d' % gb])
                S.pe(lambda e, p=p, bsl=bsl, s8=s8, gb=gb: e.matmul(ps[gb][0:8, 0:8], lhsT=rks[bsl, 2 * p + 1, s8], rhs=rqs[bsl, 2 * p + 1, s8], start=False, stop=True),
                     reads=['rks', 'rqs'], writes=['ps%d' % gb])
                S.dve(lambda e, h=h, gb=gb: e.tensor_tensor(out=ams[0:8, h * 8:h * 8 + 8], in0=ps[gb][0:8, 0:8], in1=dec8[0:8, 512 + h * 8:512 + h * 8 + 8], op=ALU.mult),
                      reads=['ps%d' % gb, 'dec8'], writes=['ams'])
                osl = ot[:, s_ * 32 + h * 8: s_ * 32 + h * 8 + 8]
                S.pe(lambda e, h=h, s_=s_, osl=osl: e.matmul(osl, lhsT=rvs[:, s_, h * 128:(h + 1) * 128], rhs=ams[:, h * 8:h * 8 + 8], start=True, stop=False),
                     reads=['rvs', 'ams'], writes=['ps6'])
                for ab in range(2):
                    S.pe(lambda e, p=p, ab=ab, hl=hl, bsl=bsl, s8=s8, osl=osl, si=si: e.matmul(
                        osl, lhsT=S0b[si][bsl, p * 512 + ab * 256 + hl * 128: p * 512 + ab * 256 + hl * 128 + 128], rhs=rqds[bsl, 2 * p + ab, s8],
                        start=False, stop=(ab == 1)), reads=['S0b%d' % si, 'rqds'], writes=['ps6'])
        S.act(lambda e: e.activation(out=osqs[:], in_=ot[:, 0:128], func=AF.Square), reads=['ps6'], writes=['osqs'])
        r = rstd_from_sq([osqs[:]], ['osqs'], 1, 1.0 / 128, N=128)
        S.dve(lambda e, r=r: e.tensor_tensor(out=ons[:], in0=ot[:, 0:128], in1=r[:, 0:128], op=ALU.mult), reads=['ps6', 'rstd1'], writes=['ons'])
        for s_ in range(4):
            S.pool(lambda e, s_=s_: e.tensor_tensor(out=hret_s[:, :, s_ * 8:s_ * 8 + 8], in0=ons[:, s_ * 32:(s_ + 1) * 32].rearrange("p (h l) -> p h l", h=4),
                                                    in1=sgs[:, :, s_ * 8:s_ * 8 + 8], op=ALU.mult), reads=['ons', 'sgs'], writes=['hret_s'])
    S.barrier()
    free_res = []

    if stop_after < 2:
        S.emit(st)
        return nc
    aq = at("aq", [128, 4, 2048], BF16, 98 * KB)
    ak = at("ak", [128, 4, 4096], BF16, 114 * KB)
    hatt = at("hatt", [64, 8, 2048], BF16, 175 * KB)
    if True:
        sb2 = seq_at(146 * KB)
        wt = sb2("wt", [128, 8, 1536], BF16)
        S.dma(lambda e: e.dma_start(out=wt[:, :, 0:768], in_=watt[:, :, 0:768]), writes=['wt'], q='pool')
        S.dma(lambda e: e.dma_start(out=wt[:, :, 768:1536], in_=watt[:, :, 768:1536]), writes=['wt'], q='pool')
        tak = [sb2("tak%d" % i, [128, 2, 512], F32) for i in range(2)]
        taq = [sb2("taq%d" % i, [128, 2, 512], F32) for i in range(2)]
        k32_0 = sb2("k32_0", [128, 4, 512], F32)
        hT1b_off = sb2.cur[0]
        hT1b = sb2("hT1b", [128, 8, 512], BF16)
        hTb[1] = hT1b
        k32 = [k32_0, k32_0]
        vst = [sb2("vst%d" % i, [128, 8, 65], F32) for i in range(2)]
        for i in range(2):
            S.pool(lambda e, i=i: e.memset(vst[i][:], 1.0), writes=['vst%d' % i])
        vdone = []
        r2 = [0]

        def rb2():
            r2[0] += 1
            return [0, 1, 2, 4, 5, 6][r2[0] % 6]
        load_x(4096, 0)
        load_x(4096 + 512, 1)
        prenorm(0, gm1, SH1)
        for ti in range(8):
            own = ti >= 4
            bi = ti % 2
            col0 = 4096 + ti * 512
            kc0 = ti * 512
            S.dma(lambda e, bi=bi, kc0=kc0: e.dma_start(out=tak[bi][:], in_=tabAK[:, :, kc0:kc0 + 512]), writes=['tak%d' % bi])
            if own:
                S.dma(lambda e, bi=bi, kc0=kc0: e.dma_start(out=taq[bi][:], in_=tabAQ[:, :, kc0 - 2048:kc0 - 2048 + 512]), writes=['taq%d' % bi])
            if ti + 1 < 8:
                prenorm((ti + 1) % 2, gm1, SH1)
            if ti + 2 < 8:
                load_x(4096 + (ti + 2) * 512, ti % 2)
            curh[0] = bi
            hT = hTb[bi]
            for p in range(2):
                bA, bB = rb2(), rb2()
                proj_fm(wt, 'wt', (4 + 2 * p) * 128, bA)
                proj_fm(wt, 'wt', (5 + 2 * p) * 128, bB)
                if own:
                    rotate(bA, bB, tak[bi], 'tak%d' % bi, ak[:, 2 * p, kc0:kc0 + 512], ak[:, 2 * p + 1, kc0:kc0 + 512], 'ak%d' % p,
                           k32[bi][:, 2 * p, :], k32[bi][:, 2 * p + 1, :], 'k32_0')
                else:
                    rotate(bA, bB, tak[bi], 'tak%d' % bi, ak[:, 2 * p, kc0:kc0 + 512], ak[:, 2 * p + 1, kc0:kc0 + 512], 'ak%d' % p)
            if own:
                oc0 = kc0 - 2048
                S.dma(lambda e, bi=bi, oc0=oc0: e.dma_start(out=kTo[:, :, oc0:oc0 + 512], in_=k32[bi][:]), reads=['k32_0'])
                for p in range(2):
                    bA, bB = rb2(), rb2()
                    proj_fm(wt, 'wt', (0 + 2 * p) * 128, bA)
                    proj_fm(wt, 'wt', (1 + 2 * p) * 128, bB)
                    rotate(bA, bB, taq[bi], 'taq%d' % bi, aq[:, 2 * p, oc0:oc0 + 512], aq[:, 2 * p + 1, oc0:oc0 + 512], 'aq%d' % p)
            for c in range(4):
                vi = (ti * 4 + c) % 2
                vb = rb2()
                for kt in range(8):
                    S.pe(lambda e, kt=kt, c=c: e.matmul(ps[vb][:], lhsT=hT[:, kt, c * 128:(c + 1) * 128], rhs=wt[:, kt, 1024:1536],
                                                        start=(kt == 0), stop=(kt == 7)), reads=['wt', 'hT%d' % bi], writes=['ps%d' % vb])
                S.act(lambda e, vi=vi: e.copy(out=vst[vi][:, :, 0:64], in_=ps[vb][:].rearrange("p (h d) -> p h d", h=8)),
                      reads=['ps%d' % vb], writes=['vst%d' % vi])
                r0 = kc0 + c * 128
                vdone.append(S.dma(lambda e, vi=vi, r0=r0: e.dma_start(out=vnat[r0:r0 + 128, :, :], in_=vst[vi][:]),
                                   reads=['vst%d' % vi], writes=['vnat']))
    if True:
        S.barrier()
        vsn32 = at("vsn32", [8, 4, 512], F32, hT1b_off)
        k32 = [k32_0, vsn32]
        for p in range(2):
            proj_fm_s(wt, 'wt', (4 + 2 * p) * 128, 0)
            proj_fm_s(wt, 'wt', (5 + 2 * p) * 128, 1)
            rotate(0, 1, None, 'tabS', aks[:, 2 * p, :], aks[:, 2 * p + 1, :], 'aks',
                   k32[0][:, 2 * p, 0:32], k32[0][:, 2 * p + 1, 0:32], 'k32_0', N=32, cs=tabs(3))
            proj_fm_s(wt, 'wt', (0 + 2 * p) * 128, 0)
            proj_fm_s(wt, 'wt', (1 + 2 * p) * 128, 1)
            rotate(0, 1, None, 'tabS', aqs[:, 2 * p, :], aqs[:, 2 * p + 1, :], 'aqs', N=32, cs=tabs(2))
        S.dma(lambda e: e.dma_start(out=ksnew, in_=k32[0][:, :, 0:32]), reads=['k32_0'])
        for s_ in range(4):
            s8 = slice(s_ * 8, s_ * 8 + 8)
            for kt in range(8):
                S.pe(lambda e, kt=kt, s8=s8: e.matmul(ps[2][0:8, :], lhsT=hTs[:, kt, s8], rhs=wt[:, kt, 1024:1536], start=(kt == 0), stop=(kt == 7)),
                     reads=['wt', 'hTs'], writes=['ps2'])
            S.act(lambda e, s_=s_: e.copy(out=vsn32[0:8, s_, :], in_=ps[2][0:8, :]), reads=['ps2'], writes=['k32_1'])
        S.dma(lambda e: e.dma_start(out=vsnew, in_=vsn32[0:8, :, :]), reads=['k32_1'], writes=['vsnew'])
    S.barrier()

    if stop_after < 2.05:
        S.emit(st)
        return nc
    if True:
        V1 = at("V1", [128, 17, 8 * 65], BF16, 64 * KB)
        V4 = at("V4", [128, 20, 8 * 65], BF16, 146 * KB)
        V16 = at("V16", [128, 32, 8 * 65], BF16, 30 * KB)
        vflat = vnat.rearrange("t h d -> t (h d)")
        S.dma(lambda e: e.dma_start(out=V1[:], in_=vflat[1920:4096, :].rearrange("(a p) f -> p a f", p=128)),
              reads=['vnat'], writes=['V1'], q='pool')
        S.dma(lambda e: e.dma_start(out=V4[:].rearrange("p (g r) f -> p g r f", r=4),
                                    in_=vflat[1536:4096, :].rearrange("(g p r) f -> p g r f", p=128, r=4)),
              reads=['vnat'], writes=['V4'], q='pool')
        S.dma(lambda e: e.dma_start(out=V16[:].rearrange("p (g r) f -> p g r f", r=16),
                                    in_=vflat[0:4096, :].rearrange("(g p r) f -> p g r f", p=128, r=16)),
              reads=['vnat'], writes=['V16'], q='pool')
        if stop_after == 2.1:
            S.emit(st)
            return nc
        uacc = [at("uacc0", [65, 2048], F32, 22 * KB)] * 2
        pT = [at("pT%d" % i, [128, 256], BF16, 18 * KB + 512 * i) for i in range(8)]
        pE = [at("pE%d" % i, [128, 256], BF16, 167 * KB + 512 * i) for i in range(4)]
        pctr = [0]
        SB = [0, 1, 2, 3]
        OB = [4, 5]
        sctr = [0]
        octr = [0]
        LAG = 3
        obst = {'ob': OB[0]}
        for h in range(8):
            if h in (1, 2, 3, 4):
                s_ = h - 1
                S.dma(lambda e, s_=s_: e.dma_start(out=kcs[s_], in_=kcn[s_, 8:2048, :]), reads=['uarow'])
                S.dma(lambda e, s_=s_: e.dma_start(out=vcs[s_], in_=vcn[s_, 8:2048, :]), reads=['uarow'])
            p, q4 = h // 4, h % 4
            hs = slice(32 * q4, 32 * q4 + 32)
            tp = (32 * q4, 0)
            ua = uacc[0]
            allua = ['ua1_%d' % g for g in range(4)] + ['ua4_%d' % r_ for r_ in range(4)] + ['ua16_%d' % r_ for r_ in range(16)]
            pending = []

            def emit_pv(D, Vt, Vres, r, kb, vt, nb, pt, pi, prev_p, h=h, ua=ua):
                qb = kb
                if qb % 4 == 0 or D == 16:
                    obst['ob'] = OB[octr[0] % 2]
                    octr[0] += 1
                ob = obst['ob']
                osl = ps[ob][0:65, qb % 4 * 128: qb % 4 * 128 + 128] if D != 16 else ps[ob][0:65, 0:128]
                ppt, ppi, pkb = prev_p
                pvt = vt - (1 if D == 1 else D)
                poff = 0 if pkb == -1 else 128
                S.pe(lambda e: e.matmul(osl, lhsT=Vt[:, pvt, h * 65:(h + 1) * 65], rhs=ppt[:, poff:poff + 128], start=True, stop=False),
                     reads=[Vres, 'pT%d' % ppi], writes=['ps%d' % ob])
                S.pe(lambda e: e.matmul(osl, lhsT=Vt[:, vt, h * 65:(h + 1) * 65], rhs=pt[:, 0:128], start=False, stop=True),
                     reads=[Vres, 'pT%d' % pi], writes=['ps%d' % ob])
                if (qb % 4 == 3) or D == 16:
                    if D == 1:
                        g = qb // 4
                        dst = ua[:, g * 512:g * 512 + 512]
                        src = ps[ob][0:65, 0:512]
                        S.dve(lambda e: e.tensor_copy(out=dst, in_=src), reads=['ps%d' % ob], writes=['ua1_%d' % g])
                    elif D == 4:
                        dst = ua[:, r:2048:4]
                        src = ps[ob][0:65, 0:512]
                        S.dve(lambda e: e.tensor_tensor(out=dst, in0=src, in1=dst, op=ALU.add),
                              reads=['ps%d' % ob] + ['ua1_%d' % g for g in range(4)], writes=['ua4_%d' % r])
                    else:
                        dst = ua[:, r:2048:16]
                        src = ps[ob][0:65, 0:128]
                        S.dve(lambda e: e.tensor_tensor(out=dst, in0=src, in1=dst, op=ALU.add),
                              reads=['ps%d' % ob, 'ua4_%d' % (r % 4)], writes=['ua16_%d' % r])

            for (D, Vt, Vres) in [(1, V1, 'V1'), (4, V4, 'V4'), (16, V16, 'V16')]:
                nb = 16 // D
                for r in range(D):
                    prev_p = None
                    for kb in range(-1, nb):
                        kcol0 = 2048 + r + D * 128 * kb
                        kcols = slice(kcol0, kcol0 + 127 * D + 1, D)
                        if D == 1:
                            vt = kb + 1
                        elif D == 4:
                            vt = (kb + 1) * 4 + r
                        else:
                            vt = (kb + 1) * 16 + r
                        qb0 = max(kb, 0)
                        qb1 = min(kb + 1, nb - 1)
                        nq = (qb1 - qb0 + 1) * 128
                        qcol0 = r + D * 128 * qb0
                        qcols = slice(qcol0, qcol0 + (nq - 1) * D + 1, D)
                        if kb == -1:
                            mk = mprev[:, 0:128]
                        elif kb == nb - 1:
                            mk = mmain[:, 0:128]
                        else:
                            mk = mmain[:, 0:256]
                        sbk = SB[sctr[0] % 4]
                        sctr[0] += 1
                        sp_ = ps[sbk][:, 0:nq]
                        sres = 'ps%d' % sbk
                        S.pe(lambda e: e.matmul(sp_, lhsT=ak[hs, 2 * p, kcols], rhs=aq[hs, 2 * p, qcols], start=True, stop=False, tile_position=tp),
                             reads=[], writes=[sres])
                        S.pe(lambda e: e.matmul(sp_, lhsT=ak[hs, 2 * p + 1, kcols], rhs=aq[hs, 2 * p + 1, qcols], start=False, stop=True, tile_position=tp),
                             reads=[], writes=[sres])
                        pi = pctr[0] % 8
                        pctr[0] += 1
                        pt = pT[pi]
                        pe_ = pE[pi % 4]
                        S.act(lambda e: e.activation(out=pe_[:, 0:nq], in_=sp_, func=AF.Exp), reads=[sres], writes=['pE%d' % (pi % 4)])
                        S.pool(lambda e: e.tensor_tensor(out=pt[:, 0:nq], in0=pe_[:, 0:nq], in1=mk, op=ALU.mult),
                               reads=['pE%d' % (pi % 4), 'mmain', 'mprev'], writes=['pT%d' % pi])
                        if kb >= 0:
                            pending.append((D, Vt, Vres, r, kb, vt, nb, pt, pi, prev_p))
                        prev_p = (pt, pi, kb)
                        while len(pending) > LAG:
                            emit_pv(*pending.pop(0))
            while pending:
                emit_pv(*pending.pop(0))
            S.act(lambda e: e.activation(out=ua[64:65, :], in_=ua[64:65, :], func=AF.Ln), reads=allua, writes=['uarow'])
            S.act(lambda e: e.activation(out=ua[64:65, :], in_=ua[64:65, :], func=AF.Exp, scale=-1.0), reads=['uarow'], writes=['uarow'])
            for tt in range(4):
                S.pe(lambda e: e.matmul(ps[6][0:64, :], lhsT=self32[0:65, :], rhs=ua[0:65, tt * 512:(tt + 1) * 512], start=True, stop=True),
                     reads=allua + ['uarow', 'sel'], writes=['ps6'])
                S.dve(lambda e: e.tensor_tensor(out=hatt[:, h, tt * 512:(tt + 1) * 512], in0=ua[0:64, tt * 512:(tt + 1) * 512], in1=ps[6][0:64, :], op=ALU.mult),
                      reads=allua + ['uarow', 'ps6'], writes=[])
    S.barrier()
    if debug and os.environ.get("DUMP", "1") == "1":
        dstg = [at("dstg%d" % i, [128, 2048], F32, (30 + 8 * i) * KB) for i in range(2)]
        for i in range(12):
            d_ = dstg[i % 2]
            if i < 4:
                S.act(lambda e: e.copy(out=d_[:], in_=hret[:, i, :]), reads=['hret'], writes=['dstg%d' % (i % 2)])
                S.dma(lambda e: e.dma_start(out=dbg_hret[:, i, :], in_=d_[:]), reads=['dstg%d' % (i % 2)])
            else:
                S.act(lambda e: e.copy(out=d_[0:64, :], in_=hatt[:, i - 4, :]), reads=['hatt'], writes=['dstg%d' % (i % 2)])
                S.dma(lambda e: e.dma_start(out=dbg_hatt[:, i - 4, :], in_=d_[0:64, :]), reads=['dstg%d' % (i % 2)])
        S.barrier()
    if True:
        akc = [at("akc%d" % i, [128, 4, 2176], BF16, (98 + 17 * i) * KB) for i in range(2)]
        Vs = [at("Vs%d" % i, [128, 17, 512], BF16, (132 + 17 * i) * KB) for i in range(2)]
        mmult = at("mmult_s", [128, 136], F32, 18 * KB)
        pes = at("pes", [128, 136], F32, 19 * KB)
        pTs = [at("pTs%d" % i, [128, 136], BF16, 20 * KB + 512 * i) for i in range(2)]
        rden = at("rden", [64, 8], F32, 22 * KB)
        S.dma(lambda e: e.dma_start(out=mmult[:], in_=mmult_d), writes=['mmult'])
        for i in range(2):
            S.pool(lambda e, i=i: e.memset(akc[i][:, :, 2048:2176], 0.0), writes=['akc%d' % i])
            S.pool(lambda e, i=i: e.memset(Vs[i][:, 16, :], 0.0), writes=['Vs%d' % i])
        SBs = [0, 1, 2, 3]
        OBs = [4, 5]
        sc2 = 0
        for s_ in range(4):
            bi = s_ % 2
            s8 = slice(s_ * 8, s_ * 8 + 8)
            S.dma(lambda e, s_=s_, bi=bi: e.dma_start(out=akc[bi][:, :, 0:2048], in_=kcT[s_]), writes=['akc%d' % bi], q='pool')
            S.pool(lambda e, bi=bi, s8=s8: e.tensor_copy(out=akc[bi][:, :, 2048:2056], in_=aks[:, :, s8]), reads=['aks'], writes=['akc%d' % bi])
            S.dma(lambda e, s_=s_, bi=bi: e.dma_start(out=Vs[bi][:, 0:16, :], in_=vcn[s_].rearrange("(a p) f -> p a f", p=128)), writes=['Vs%d' % bi], q='pool')
            S.dma(lambda e, s_=s_, bi=bi: e.dma_start(out=Vs[bi][0:8, 16, :], in_=vsnew[:, s_, :]), reads=['vsnew'], writes=['Vs%d' % bi], q='pool')
            for h in range(8):
                p, q4 = h // 4, h % 4
                hs = slice(32 * q4, 32 * q4 + 32)
                tp = (32 * q4, 0)
                sbk = SBs[sc2 % 4]
                ob = OBs[sc2 % 2]
                pk = sc2 % 2
                sc2 += 1
                for tl in range(17):
                    for ab in range(2):
                        S.pe(lambda e, tl=tl, ab=ab: e.matmul(ps[sbk][:, tl * 8:tl * 8 + 8], lhsT=akc[bi][hs, 2 * p + ab, tl * 128:(tl + 1) * 128],
                                                              rhs=aqs[hs, 2 * p + ab, s8], start=(ab == 0), stop=(ab == 1), tile_position=tp),
                             reads=['akc%d' % bi, 'aqs'], writes=['ps%d' % sbk])
                S.act(lambda e: e.activation(out=pes[:], in_=ps[sbk][:, 0:136], func=AF.Exp), reads=['ps%d' % sbk], writes=['pes'])
                S.dve(lambda e: e.tensor_tensor(out=pTs[pk][:], in0=pes[:], in1=mmult[:], op=ALU.mult), reads=['pes', 'mmult'], writes=['pTs%d' % pk])
                for tl in range(17):
                    S.pe(lambda e, tl=tl: e.matmul(ps[ob][0:64, 0:8], lhsT=Vs[bi][:, tl, h * 64:(h + 1) * 64], rhs=pTs[pk][:, tl * 8:tl * 8 + 8],
                                                   start=(tl == 0), stop=(tl == 16)), reads=['Vs%d' % bi, 'pTs%d' % pk], writes=['ps%d' % ob])
                for tl in range(17):
                    S.pe(lambda e, tl=tl: e.matmul(ps[ob][0:64, 8:16], lhsT=ones[:, 0:64], rhs=pTs[pk][:, tl * 8:tl * 8 + 8],
                                                   start=(tl == 0), stop=(tl == 16)), reads=['ones', 'pTs%d' % pk], writes=['ps%d' % ob])
                S.dve(lambda e: e.reciprocal(out=rden[:], in_=ps[ob][0:64, 8:16]), reads=['ps%d' % ob], writes=['rden'])
                S.dve(lambda e: e.tensor_tensor(out=hatt_s[:, h, s8], in0=ps[ob][0:64, 0:8], in1=rden[:], op=ALU.mult),
                      reads=['ps%d' % ob, 'rden'], writes=['hatt_s'])
    S.barrier()
    if stop_after < 4:
        S.emit(st)
        return nc
    x1 = at("x1", [128, 8, 2048], F32, 98 * KB)
    if True:
        woRs = at("woRs", [128, 4, 1024], BF16, 162 * KB)
        woAs = at("woAs", [64, 8, 1024], BF16, 62 * KB)
        sq = at("sq2", [128, 8, 512], BF16, 22 * KB)
        S.dma(lambda e: e.dma_start(out=woRs[:], in_=woR), writes=['woRs'], q='pool')
        S.dma(lambda e: e.dma_start(out=woAs[:], in_=woA), writes=['woAs'], q='pool')
        load_x(6144, 0)
        for ti in range(4):
            bi = ti % 2
            tc0 = ti * 512
            if ti + 1 < 4:
                load_x(6144 + (ti + 1) * 512, (ti + 1) % 2)
            for dt_ in range(8):
                gb = gbank()
                dc = slice(dt_ * 128, (dt_ + 1) * 128)
                for hh in range(4):
                    S.pe(lambda e, gb=gb, hh=hh, dc=dc: e.matmul(ps[gb][:], lhsT=woRs[:, hh, dc], rhs=hret[:, hh, tc0:tc0 + 512], start=(hh == 0), stop=False),
                         reads=['woRs', 'hret'], writes=['ps%d' % gb])
                for hh in range(8):
                    S.pe(lambda e, gb=gb, hh=hh, dc=dc: e.matmul(ps[gb][:], lhsT=woAs[0:64, hh, dc], rhs=hatt[0:64, hh, tc0:tc0 + 512], start=False, stop=(hh == 7)),
                         reads=['woAs', 'hatt'], writes=['ps%d' % gb])
                S.act(lambda e, gb=gb, dt_=dt_: e.copy(out=x1[:, dt_, tc0:tc0 + 512], in_=ps[gb][:]), reads=['ps%d' % gb], writes=['x1'])
                S.act(lambda e, gb=gb, dt_=dt_: e.activation(out=sq[:, dt_, :], in_=ps[gb][:], func=AF.Square), reads=['ps%d' % gb], writes=['sq'])
            r = rstd_from_sq([sq[:, kt, :] for kt in range(8)], ['sq'], 0, 1.0 / 1024)
            for dt_ in range(8):
                t = tmpf[dt_ % 2]
                S.dve(lambda e, t=t, dt_=dt_: e.scalar_tensor_tensor(out=t[:], in0=x1[:, dt_, tc0:tc0 + 512], scalar=gg1[:, dt_, 0:1], in1=r[:], op0=ALU.mult, op1=ALU.mult),
                      reads=['x1', 'gg1', 'rstd0'], writes=['tmpf%d' % (dt_ % 2)])
                S.dve(lambda e, t=t, dt_=dt_: e.tensor_tensor(out=x1[:, dt_, tc0:tc0 + 512], in0=t[:], in1=xt[bi][:, dt_, :], op=ALU.add),
                       reads=['tmpf%d' % (dt_ % 2), 'xt%d' % bi], writes=['x1'])
    if True:
        for dt_ in range(8):
            gb = gbank()
            dc = slice(dt_ * 128, (dt_ + 1) * 128)
            for hh in range(4):
                S.pe(lambda e, gb=gb, hh=hh, dc=dc: e.matmul(ps[gb][:, 0:32], lhsT=woRs[:, hh, dc], rhs=hret_s[:, hh, :], start=(hh == 0), stop=False),
                     reads=['woRs', 'hret_s'], writes=['ps%d' % gb])
            for hh in range(8):
                S.pe(lambda e, gb=gb, hh=hh, dc=dc: e.matmul(ps[gb][:, 0:32], lhsT=woAs[0:64, hh, dc], rhs=hatt_s[0:64, hh, :], start=False, stop=(hh == 7)),
                     reads=['woAs', 'hatt_s'], writes=['ps%d' % gb])
            S.act(lambda e, gb=gb, dt_=dt_: e.copy(out=x1s[:, dt_, :], in_=ps[gb][:, 0:32]), reads=['ps%d' % gb], writes=['x1s'])
            S.act(lambda e, gb=gb, dt_=dt_: e.activation(out=sq[:, dt_, 0:32], in_=ps[gb][:, 0:32], func=AF.Square), reads=['ps%d' % gb], writes=['sq'])
        r = rstd_from_sq([sq[:, kt, 0:32] for kt in range(8)], ['sq'], 0, 1.0 / 1024, N=32)
        for dt_ in range(8):
            t = tmpf[dt_ % 2]
            for s_ in range(4):
                s8 = slice(s_ * 8, s_ * 8 + 8)
                S.dve(lambda e, t=t, dt_=dt_, s_=s_, s8=s8: e.scalar_tensor_tensor(out=t[:, s8], in0=x1s[:, dt_, s8], scalar=gg1[:, dt_, 1 + s_:2 + s_], in1=r[:, s8],
                                                                                 op0=ALU.mult, op1=ALU.mult), reads=['x1s', 'gg1', 'rstd0'], writes=['tmpf%d' % (dt_ % 2)])
            S.pool(lambda e, t=t, dt_=dt_: e.tensor_tensor(out=x1s[:, dt_, :], in0=t[:, 0:32], in1=xs[:, dt_, :], op=ALU.add),
                   reads=['tmpf%d' % (dt_ % 2), 'xs'], writes=['x1s'])
    if debug:
        S.dma(lambda e: e.dma_start(out=dbg_x1, in_=x1[:]), reads=['x1'])
    S.barrier()
    if stop_after < 5:
        S.emit(st)
        return nc
    if True:
        h2 = at("h2", [128, 8, 1056], BF16, 162 * KB)
        hff = at("hff", [128, 16, 1056], BF16, 30 * KB)
        fbuf = at("fbuf", [128, 8, 1056], F32, 63 * KB)
        wu = [at("wu%d" % i, [128, 8, 512], BF16, (179 + 8 * i) * KB) for i in range(2)]
        wd = [at("wd%d" % i, [128, 16, 128], BF16, (195 + 4 * i) * KB) for i in range(2)]
        rl = [at("rl%d" % i, [128, 512], BF16, (96 + i) * KB) for i in range(2)]
        sq = at("sq3", [128, 8, 512], BF16, 22 * KB)
        uctr = [0]
        dctr = [0]
        def c2_prenorm(T):
            t0 = T * 1024
            for hh in range(2):
                cs = slice(t0 + hh * 512, t0 + hh * 512 + 512)
                hs_ = slice(hh * 512, hh * 512 + 512)
                for kt in range(8):
                    S.act(lambda e, kt=kt, cs=cs: e.activation(out=sq[:, kt, :], in_=x1[:, kt, cs], func=AF.Square), reads=['x1'], writes=['sq'])
                r = rstd_from_sq([sq[:, kt, :] for kt in range(8)], ['sq'], 0, 1.0 / 1024)
                for kt in range(8):
                    t = tmpf[kt % 2]
                    S.dve(lambda e, t=t, kt=kt, cs=cs: e.tensor_tensor(out=t[:], in0=x1[:, kt, cs], in1=r[:], op=ALU.mult),
                          reads=['x1', 'rstd0'], writes=['tmpf%d' % (kt % 2)])
                    S.act(lambda e, t=t, kt=kt, hs_=hs_: e.activation(out=h2[:, kt, hs_], in_=t[:], func=AF.Identity,
                                                                      bias=mod[:, SH2 + kt, 0:1], scale=gm2[:, kt, 0:1]),
                          reads=['tmpf%d' % (kt % 2), 'mod', 'gm2'], writes=['h2'])
            if T == 0:
                for kt in range(8):
                    S.act(lambda e, kt=kt: e.activation(out=sq[:, kt, 0:32], in_=x1s[:, kt, :], func=AF.Square), reads=['x1s'], writes=['sq'])
                r = rstd_from_sq([sq[:, kt, 0:32] for kt in range(8)], ['sq'], 1, 1.0 / 1024, N=32)
                for kt in range(8):
                    t = tmpf[kt % 2]
                    S.dve(lambda e, t=t, kt=kt: e.tensor_tensor(out=t[:, 0:32], in0=x1s[:, kt, :], in1=r[:, 0:32], op=ALU.mult),
                          reads=['x1s', 'rstd1'], writes=['tmpf%d' % (kt % 2)])
                    for s_ in range(4):
                        S.act(lambda e, t=t, kt=kt, s_=s_: e.activation(out=h2[:, kt, 1024 + s_ * 8:1024 + s_ * 8 + 8], in_=t[:, s_ * 8:s_ * 8 + 8], func=AF.Identity,
                                                                        bias=mod[:, SH2 + kt, 1 + s_:2 + s_], scale=gm2[:, kt, 1 + s_:2 + s_]),
                              reads=['tmpf%d' % (kt % 2), 'mod', 'gm2'], writes=['h2'])
        def c2_up(T, fh):
            for fc in range(4):
                wi = uctr[0] % 2
                uctr[0] += 1
                f0 = fh * 2048 + fc * 512
                S.dma(lambda e, wi=wi, f0=f0: e.dma_start(out=wu[wi][:], in_=wup[:, :, f0:f0 + 512]), writes=['wu%d' % wi], q='pool')
                for ft in range(4):
                    fi = fc * 4 + ft
                    for hh in range(2):
                        gb = gbank()
                        for kt in range(8):
                            S.pe(lambda e, gb=gb, wi=wi, ft=ft, kt=kt, hh=hh: e.matmul(ps[gb][:], lhsT=wu[wi][:, kt, ft * 128:(ft + 1) * 128], rhs=h2[:, kt, hh * 512:(hh + 1) * 512],
                                                                             start=(kt == 0), stop=(kt == 7)), reads=['wu%d' % wi, 'h2'], writes=['ps%d' % gb])
                        ri = (fi * 2 + hh) % 2
                        S.act(lambda e, gb=gb, ri=ri: e.activation(out=rl[ri][:], in_=ps[gb][:], func=AF.Relu), reads=['ps%d' % gb], writes=['rl%d' % ri])
                        S.dve(lambda e, ri=ri, fi=fi, hh=hh: e.tensor_tensor(out=hff[:, fi, hh * 512:(hh + 1) * 512], in0=rl[ri][:], in1=rl[ri][:], op=ALU.mult),
                               reads=['rl%d' % ri], writes=['hff'])
                    if T == 0:
                        gb = gbank()
                        for kt in range(8):
                            S.pe(lambda e, gb=gb, wi=wi, ft=ft, kt=kt: e.matmul(ps[gb][:, 0:32], lhsT=wu[wi][:, kt, ft * 128:(ft + 1) * 128], rhs=h2[:, kt, 1024:1056],
                                                                             start=(kt == 0), stop=(kt == 7)), reads=['wu%d' % wi, 'h2'], writes=['ps%d' % gb])
                        ri = fi % 2
                        S.act(lambda e, gb=gb, ri=ri: e.activation(out=rl[ri][:, 0:32], in_=ps[gb][:, 0:32], func=AF.Relu), reads=['ps%d' % gb], writes=['rl%d' % ri])
                        S.dve(lambda e, ri=ri, fi=fi: e.tensor_tensor(out=hff[:, fi, 1024:1056], in0=rl[ri][:, 0:32], in1=rl[ri][:, 0:32], op=ALU.mult),
                               reads=['rl%d' % ri], writes=['hff'])
        def c2_down(T, fh):
            for dt_ in range(8):
                wi = dctr[0] % 2
                dctr[0] += 1
                S.dma(lambda e, wi=wi, dt_=dt_, fh=fh: e.dma_start(out=wd[wi][:], in_=wdn[dt_, :, fh * 16:(fh + 1) * 16, :]),
                      writes=['wd%d' % wi], q='pool')
                for hh in range(2):
                    gb = gbank()
                    for ft in range(16):
                        S.pe(lambda e, gb=gb, wi=wi, ft=ft, hh=hh: e.matmul(ps[gb][:], lhsT=wd[wi][:, ft, :], rhs=hff[:, ft, hh * 512:(hh + 1) * 512],
                                                                         start=(ft == 0), stop=(ft == 15)), reads=['wd%d' % wi, 'hff'], writes=['ps%d' % gb])
                    fs = fbuf[:, dt_, hh * 512:(hh + 1) * 512]
                    if fh == 0:
                        S.act(lambda e, gb=gb, fs=fs: e.copy(out=fs, in_=ps[gb][:]), reads=['ps%d' % gb], writes=['fbuf'])
                    else:
                        S.dve(lambda e, gb=gb, fs=fs: e.tensor_tensor(out=fs, in0=ps[gb][:], in1=fs, op=ALU.add), reads=['ps%d' % gb, 'fbuf'], writes=['fbuf'])
                if T == 0:
                    gb = gbank()
                    for ft in range(16):
                        S.pe(lambda e, gb=gb, wi=wi, ft=ft: e.matmul(ps[gb][:, 0:32], lhsT=wd[wi][:, ft, :], rhs=hff[:, ft, 1024:1056],
                                                                  start=(ft == 0), stop=(ft == 15)), reads=['wd%d' % wi, 'hff'], writes=['ps%d' % gb])
                    fs = fbuf[:, dt_, 1024:1056]
                    if fh == 0:
                        S.act(lambda e, gb=gb, fs=fs: e.copy(out=fs, in_=ps[gb][:, 0:32]), reads=['ps%d' % gb], writes=['fbuf'])
                    else:
                        S.dve(lambda e, gb=gb, fs=fs: e.tensor_tensor(out=fs, in0=ps[gb][:, 0:32], in1=fs, op=ALU.add), reads=['ps%d' % gb, 'fbuf'], writes=['fbuf'])
        def c2_post(T):
            t0 = T * 1024
            for hh in range(2):
                cs = slice(t0 + hh * 512, t0 + hh * 512 + 512)
                hs_ = slice(hh * 512, hh * 512 + 512)
                for kt in range(8):
                    S.act(lambda e, kt=kt, hs_=hs_: e.activation(out=sq[:, kt, :], in_=fbuf[:, kt, hs_], func=AF.Square), reads=['fbuf'], writes=['sq'])
                r = rstd_from_sq([sq[:, kt, :] for kt in range(8)], ['sq'], 0, 1.0 / 1024)
                for kt in range(8):
                    t = tmpf[kt % 2]
                    S.dve(lambda e, t=t, kt=kt, hs_=hs_: e.scalar_tensor_tensor(out=t[:], in0=fbuf[:, kt, hs_], scalar=gg2[:, kt, 0:1], in1=r[:], op0=ALU.mult, op1=ALU.mult),
                          reads=['fbuf', 'gg2', 'rstd0'], writes=['tmpf%d' % (kt % 2)])
                    S.dve(lambda e, t=t, kt=kt, cs=cs: e.tensor_tensor(out=x1[:, kt, cs], in0=t[:], in1=x1[:, kt, cs], op=ALU.add),
                          reads=['tmpf%d' % (kt % 2), 'x1'], writes=['x1'])
                S.dma(lambda e, cs=cs: e.dma_start(out=yT[:, :, cs], in_=x1[:, :, cs]), reads=['x1'])
            if T == 0:
                for kt in range(8):
                    S.act(lambda e, kt=kt: e.activation(out=sq[:, kt, 0:32], in_=fbuf[:, kt, 1024:1056], func=AF.Square), reads=['fbuf'], writes=['sq'])
                r = rstd_from_sq([sq[:, kt, 0:32] for kt in range(8)], ['sq'], 1, 1.0 / 1024, N=32)
                for kt in range(8):
                    t = tmpf[kt % 2]
                    for s_ in range(4):
                        s8 = slice(s_ * 8, s_ * 8 + 8)
                        S.dve(lambda e, t=t, kt=kt, s_=s_, s8=s8: e.scalar_tensor_tensor(out=t[:, s8], in0=fbuf[:, kt, 1024 + s_ * 8:1024 + s_ * 8 + 8], scalar=gg2[:, kt, 1 + s_:2 + s_],
                                                                                       in1=r[:, s8], op0=ALU.mult, op1=ALU.mult), reads=['fbuf', 'gg2', 'rstd1'], writes=['tmpf%d' % (kt % 2)])
                    S.dve(lambda e, t=t, kt=kt: e.tensor_tensor(out=x1s[:, kt, :], in0=t[:, 0:32], in1=x1s[:, kt, :], op=ALU.add),
                           reads=['tmpf%d' % (kt % 2), 'x1s'], writes=['x1s'])
                S.dma(lambda e: e.dma_start(out=ysT, in_=x1s[:]), reads=['x1s'])

        c2_prenorm(0)
        c2_up(0, 0)
        c2_down(0, 0)
        c2_up(0, 1)
        c2_prenorm(1)
        c2_down(0, 1)
        c2_up(1, 0)
        c2_post(0)
        c2_down(1, 0)
        c2_up(1, 1)
        c2_down(1, 1)
        c2_post(1)
    S.emit(st)
    return nc


def _pm(w, ncols):
    return np.ascontiguousarray(w.reshape(8, 128, ncols).transpose(1, 0, 2))


def _const_tables(j):
    f32 = np.float32
    inv_r = (1.0 / (f32(10000.0) ** np.linspace(0.0, 1.0, 64, dtype=f32))).astype(f32)
    inv_a = (1.0 / (f32(10000.0) ** (np.arange(0, 64, 2, dtype=f32) / f32(64)))).astype(f32)

    def tab(pos, inv_p, scale):
        ang = (pos.astype(f32)[None, :] * inv_p.astype(f32)[:, None]).astype(np.float64)
        t = np.stack([np.cos(ang), np.sin(ang)], axis=1) * scale
        return np.ascontiguousarray(t.astype(f32))

    t = np.arange(2048)
    pos_all = np.concatenate([(j - 3 + s) * 2048 + t for s in range(4)])
    pos_all = np.maximum(pos_all, 0)
    inv_r_p = inv_r[np.arange(128) % 64]
    inv_a_p = inv_a[np.arange(128) % 32]
    tabRK = tab(pos_all, inv_r_p, 128.0 ** -0.5)
    tabRQ = tab(pos_all[6144:], inv_r_p, 1.0)
    tabAK = tab(pos_all[4096:], inv_a_p, 1.0)
    tabAQ = tab(pos_all[6144:], inv_a_p, 0.125)
    log_g = np.log1p(-np.exp2(-5.0 - np.arange(4, dtype=np.float64)))
    jj = np.arange(128)
    kdectab = np.zeros((128, 4, 128))
    qdec = np.zeros((128, 2, 512))
    cdec = np.zeros((128, 4))
    for p in range(2):
        for f in range(128):
            hh = 2 * p + f // 64
            for ab in range(2):
                kdectab[:, 2 * p + ab, f] = np.exp(log_g[hh] * (127.0 - jj))
            qdec[f, p, :] = np.tile(np.exp(log_g[hh] * (jj + 1.0)), 4)
            cdec[f, p] = np.exp(log_g[hh] * 128.0)
            cdec[f, 2 + p] = np.exp(log_g[hh] * 512.0)
    dmask = np.zeros((128, 4, 128))
    for hh in range(4):
        d = jj[None, :] - jj[:, None]
        dmask[:, hh, :] = np.where(d >= 0, np.exp(log_g[hh] * np.maximum(d, 0)), 0.0)
    flags = np.zeros((128, 4))
    for s in range(3):
        flags[:, s] = 1.0 if (j - 3 + s) >= 0 else 0.0
    jp = jj[:, None]
    ii = jj[None, :]
    mmain = np.concatenate([np.where(jp <= ii, 1.0, 0.0), np.where(jp >= ii, 1.0, 0.0)], axis=1)
    mprev = np.where(jp >= ii, 1.0, 0.0) if j > 0 else np.zeros((128, 128))
    sel = np.zeros((128, 64))
    sel[64, :] = 1.0
    pos_s = np.tile(16384 + np.arange(8), 4)
    tabS = np.concatenate([tab(pos_s, inv_r_p, 1.0), tab(pos_s, inv_r_p, 128.0 ** -0.5),
                           tab(pos_s, inv_a_p, 0.125), tab(pos_s, inv_a_p, 1.0)], axis=1)
    dec8 = np.zeros((128, 640))
    l8 = np.arange(8)
    for p in range(2):
        for f in range(128):
            hh = 2 * p + f // 64
            for ab in range(2):
                dec8[0:8, (2 * p + ab) * 128 + f] = np.exp(log_g[hh] * (7.0 - l8))
            dec8[f, 544 + 32 * p:544 + 32 * p + 32] = np.tile(np.exp(log_g[hh] * (l8 + 1.0)), 4)
            dec8[f, 608 + p] = np.exp(log_g[hh] * 8.0)
    for hh in range(4):
        d8 = l8[None, :] - l8[:, None]
        dec8[0:8, 512 + hh * 8:512 + hh * 8 + 8] = np.where(d8 >= 0, np.exp(log_g[hh] * np.maximum(d8, 0)), 0.0)
    mm = np.zeros((2176, 8))
    for l in range(8):
        for Dd in (1, 4, 16):
            idx = 2048 + l - Dd * np.arange(129)
            np.add.at(mm[:, l], idx, 1.0)
    mmult = mm.reshape(17, 128, 8).transpose(1, 0, 2).reshape(128, 136)
    c = lambda a: np.ascontiguousarray(a, dtype=f32)
    return dict(tabS=c(tabS), dec8=c(dec8), mmult=c(mmult), tabRK=tabRK, tabRQ=tabRQ, tabAK=tabAK, tabAQ=tabAQ, kdectab=c(kdectab.reshape(128, 512)),
                dmask=c(dmask.reshape(128, 512)), qdec=c(qdec.reshape(128, 1024)), cdec=c(cdec), flags=c(flags),
                mmain=c(mmain), mprev=c(mprev), ident=c(np.eye(128)), sel=c(sel))


def _weight_layouts(w_ada, b_ada, g_pre_mix, g_post_mix, g_pre_ffn, g_post_ffn, w_in, ret_gain, w_o, w_up, w_down):
    w_in = w_in[0]
    cols_ret = []
    for base in (0, 512):
        for p in range(2):
            for ab in range(2):
                for hl in range(2):
                    hh = 2 * p + hl
                    cols_ret += [base + 128 * hh + 2 * i + ab for i in range(64)]
    cols_ret += list(range(1536, 2048))
    cols_ret += list(range(1024, 1536))
    cols_att = []
    for base in (2048, 2560):
        for p in range(2):
            for ab in range(2):
                for q4 in range(4):
                    hh = 4 * p + q4
                    cols_att += [base + 64 * hh + 32 * ab + i for i in range(32)]
    cols_att += list(range(3072, 3584))
    d = {}
    d['wret'] = _pm(w_in[:, cols_ret], 2048)
    d['watt'] = _pm(w_in[:, cols_att], 1536)
    d['wada'] = _pm(w_ada[0], 6144)
    d['bada'] = np.ascontiguousarray(b_ada[0].reshape(48, 128).T)
    d['gvec'] = np.ascontiguousarray(np.stack([g.reshape(8, 128).T for g in (g_pre_mix[0], g_post_mix[0], g_pre_ffn[0], g_post_ffn[0])], axis=1))
    d['retg'] = np.ascontiguousarray(ret_gain[0].reshape(4, 128).T)
    wo = w_o[0]
    d['woR'] = np.ascontiguousarray(wo[0:512].reshape(4, 128, 1024).transpose(1, 0, 2))
    d['woA'] = np.ascontiguousarray(wo[512:1024].reshape(8, 64, 1024).transpose(1, 0, 2))
    d['wup'] = _pm(w_up[0], 4096)
    d['wdn'] = np.ascontiguousarray(w_down[0].reshape(32, 128, 8, 128).transpose(2, 1, 0, 3))
    return d


_NC_CACHE = {}


def kernel(x_prompt, x_sample, c_prompt, c_sample, state_ret, cache_win_k, cache_win_v,
           w_ada, b_ada, g_pre_mix, g_post_mix, g_pre_ffn, g_post_ffn, w_in, ret_gain, w_o, w_up, w_down):
    f32 = np.float32
    args = [np.asarray(a, dtype=f32) for a in (w_ada, b_ada, g_pre_mix, g_post_mix, g_pre_ffn, g_post_ffn, w_in, ret_gain, w_o, w_up, w_down)]
    wl = _weight_layouts(*args)
    x_prompt = np.asarray(x_prompt, dtype=f32)
    x_sample = np.asarray(x_sample, dtype=f32)
    state_ret = np.asarray(state_ret, dtype=f32)
    cache_win_k = np.asarray(cache_win_k, dtype=f32)
    cache_win_v = np.asarray(cache_win_v, dtype=f32)
    c_prompt = np.asarray(c_prompt, dtype=f32)
    c_sample = np.asarray(c_sample, dtype=f32)
    if 'nc' not in _NC_CACHE:
        _NC_CACHE['nc'] = build_program()
    nc = _NC_CACHE['nc']
    in_maps = []
    for core in range(8):
        b, j = core // 4, core % 4
        xs = np.zeros((4, 2048, 1024), f32)
        for s in range(4):
            qq = j - 3 + s
            if qq >= 0:
                xs[s] = x_prompt[b, qq * 2048:(qq + 1) * 2048]
        xTc = np.ascontiguousarray(xs.reshape(8192, 8, 128).transpose(2, 1, 0))
        cc = np.concatenate([c_prompt[b:b + 1], c_sample[core * 4:(core + 1) * 4]], axis=0)
        cTc = np.ascontiguousarray(cc.reshape(5, 8, 128).transpose(2, 1, 0))
        m = dict(wl)
        m.update(_const_tables(j))
        m['xT'] = xTc
        m['cT'] = cTc
        sq_ = slice(core * 4, core * 4 + 4)
        m['xsT'] = np.ascontiguousarray(x_sample[sq_].reshape(32, 8, 128).transpose(2, 1, 0))
        si = np.zeros((4, 128, 2, 2, 2, 128), f32)
        st_ = state_ret[0, sq_]
        for p in range(2):
            for hl in range(2):
                for ab in range(2):
                    si[:, 64 * hl:64 * hl + 64, p, ab, hl, :] = st_[:, 2 * p + hl, ab::2, :]
        m['sinit'] = np.ascontiguousarray(si.reshape(4, 128, 1024))
        kc = cache_win_k[0, sq_]
        kct = kc.reshape(4, 2048, 2, 4, 2, 32).transpose(0, 3, 5, 2, 4, 1)
        m['kcT'] = np.ascontiguousarray(kct.reshape(4, 128, 4, 2048))
        m['kcn'] = np.ascontiguousarray(kc.reshape(4, 2048, 512))
        m['vcn'] = np.ascontiguousarray(cache_win_v[0, sq_].reshape(4, 2048, 512))
        in_maps.append(m)
    res = run_bass_kernel_spmd(nc, in_maps, core_ids=list(range(8)))
    R = res.results
    y_prompt = np.zeros((2, 8192, 1024), f32)
    state_p = np.zeros((1, 2, 4, 128, 128), f32)
    kp = np.zeros((1, 2, 2048, 8, 64), f32)
    vp = np.zeros((1, 2, 2048, 8, 64), f32)
    for core in range(8):
        b, j = core // 4, core % 4
        yTc = R[core]['yT']
        y_prompt[b, j * 2048:(j + 1) * 2048] = yTc.transpose(2, 1, 0).reshape(2048, 1024)
        if j == 3:
            kt_ = R[core]['kTo']
            for p in range(2):
                for ab in range(2):
                    for q4 in range(4):
                        kp[0, b, :, 4 * p + q4, 32 * ab:32 * ab + 32] = kt_[32 * q4:32 * q4 + 32, 2 * p + ab, :].T
            vp[0, b] = R[core]['vnat'][2048:4096, :, 0:64]
            sf = R[core]['sfin']
            for p in range(2):
                for ab in range(2):
                    for hl in range(2):
                        blk = sf[64 * hl:64 * hl + 64, p, ab * 256 + hl * 128: ab * 256 + hl * 128 + 128]
                        state_p[0, b, 2 * p + hl, ab::2, :] = blk
    y_sample = np.zeros((32, 8, 1024), f32)
    state_s = np.zeros((1, 32, 4, 128, 128), f32)
    ks = np.zeros((1, 32, 2048, 8, 64), f32)
    vs = np.zeros((1, 32, 2048, 8, 64), f32)
    for core in range(8):
        r_ = R[core]
        y_sample[core * 4:(core + 1) * 4] = r_['ysT'].transpose(2, 1, 0).reshape(4, 8, 1024)
        for s_ in range(4):
            q = core * 4 + s_
            sf = r_['sstate'][s_].reshape(128, 2, 512)
            for p in range(2):
                for ab in range(2):
                    for hl in range(2):
                        state_s[0, q, 2 * p + hl, ab::2, :] = sf[64 * hl:64 * hl + 64, p, ab * 256 + hl * 128: ab * 256 + hl * 128 + 128]
            ks[0, q, 0:2040] = r_['kcs'][s_].reshape(2040, 8, 64)
            vs[0, q, 0:2040] = r_['vcs'][s_].reshape(2040, 8, 64)
            kn = r_['ksnew'][:, :, s_ * 8:s_ * 8 + 8]
            for p in range(2):
                for ab in range(2):
                    for q4 in range(4):
                        ks[0, q, 2040:2048, 4 * p + q4, 32 * ab:32 * ab + 32] = kn[32 * q4:32 * q4 + 32, 2 * p + ab, :].T
            vs[0, q, 2040:2048] = r_['vsnew'][:, s_, :].reshape(8, 8, 64)
    return (y_prompt, y_sample, state_p, kp, vp, state_s, ks, vs)
```
